# Optimizing a Trainium2 kernel written in Bass

```python
import math
import jax, jax.numpy as jnp
from jax import lax
import numpy as np

D_MODEL = 1024
BATCH = 8
SEQ = 4096
DEPTH = 2

PLE_DIM = 256
ROPE_THETA = 10000.0
NORM_EPS = 1e-6
Q_BLOCK = 128

RET_WIDTH = D_MODEL // 2
RET_HEADS = 8
RET_HEAD_DIM = RET_WIDTH // RET_HEADS
RET_CHUNK = 128
S5_WIDTH = D_MODEL - RET_WIDTH
S5_GROUP = 16
S5_GROUPS = S5_WIDTH // S5_GROUP
S5_STATE = 64
EVEN_IN_WIDTH = 4 * RET_WIDTH + S5_WIDTH
MIX_WIDTH = RET_WIDTH + S5_WIDTH

DIFF_HEADS = 8
DIFF_QK_DIM = 64
DIFF_V_DIM = 2 * DIFF_QK_DIM
DIFF_QK_WIDTH = DIFF_HEADS * 2 * DIFF_QK_DIM
DIFF_WIDTH = DIFF_HEADS * DIFF_V_DIM
DIFF_IN_WIDTH = 2 * DIFF_QK_WIDTH + DIFF_WIDTH

FFN_HIDDEN = -(-8 * D_MODEL // (3 * 256)) * 256

N_EVEN = (DEPTH + 1) // 2
N_ODD = DEPTH // 2

kernel_name = "hybrid_retention_s5_diffattn_block"


def rms_norm(x, gain):
    xf = x.astype(jnp.float32)
    y = xf * lax.rsqrt(jnp.mean(xf * xf, axis=-1, keepdims=True) + NORM_EPS)
    return (y * gain.astype(jnp.float32)).astype(x.dtype)


def head_layer_norm(x):
    mu = jnp.mean(x, axis=-1, keepdims=True)
    xc = x - mu
    return xc * lax.rsqrt(jnp.mean(xc * xc, axis=-1, keepdims=True) + NORM_EPS)


def rotary(x, pos):
    d = x.shape[-1]
    inv = ROPE_THETA ** (-jnp.arange(0, d, 2, dtype=jnp.float32) / d)
    ang = pos.astype(jnp.float32)[:, None] * inv[None, :]
    cos = jnp.concatenate([jnp.cos(ang), jnp.cos(ang)], axis=-1)
    sin = jnp.concatenate([jnp.sin(ang), jnp.sin(ang)], axis=-1)
    xf = x.astype(jnp.float32)
    x1, x2 = xf[..., : d // 2], xf[..., d // 2:]
    rot = jnp.concatenate([-x2, x1], axis=-1)
    return (xf * cos + rot * sin).astype(x.dtype)


def retention(q, k, v):
    bsz, n_h, s_len, d = q.shape
    c = RET_CHUNK
    n_chunks = s_len // c
    gamma = 1.0 - 2.0 ** (-5.0 - jnp.arange(n_h, dtype=jnp.float32))
    log_g = jnp.log(gamma)
    idx = jnp.arange(c, dtype=jnp.float32)
    rel = idx[:, None] - idx[None, :]
    intra_decay = jnp.where(rel >= 0, jnp.exp(log_g[:, None, None] * jnp.maximum(rel, 0.0)), 0.0)
    k_decay = jnp.exp(log_g[:, None] * (c - 1 - idx))
    q_decay = jnp.exp(log_g[:, None] * (idx + 1.0))
    chunk_decay = jnp.exp(log_g * c)
    qf = q.astype(jnp.float32).reshape(bsz, n_h, n_chunks, c, d)
    kf = k.astype(jnp.float32).reshape(bsz, n_h, n_chunks, c, d) * (d ** -0.5)
    vf = v.astype(jnp.float32).reshape(bsz, n_h, n_chunks, c, d)
    scores = jnp.einsum('bhncd,bhnsd->bhncs', qf, kf) * intra_decay[:, None]
    intra = jnp.einsum('bhncs,bhnse->bhnce', scores, vf)
    kv = jnp.einsum('bhnsd,bhnse->nbhde', kf * k_decay[:, None, :, None], vf)

    def step(r_state, kv_n):
        return chunk_decay[:, None, None] * r_state + kv_n, r_state

    _, r_prev = lax.scan(step, jnp.zeros_like(kv[0]), kv)
    inter = jnp.einsum('bhncd,nbhde->bhnce', qf * q_decay[:, None, :, None], r_prev)
    return (intra + inter).reshape(bsz, n_h, s_len, d)


def s5_mixer(u, lam_re, lam_im, b_re, b_im, c_re, c_im, d_skip, log_step, w_glu):
    bsz, s_len, _ = u.shape
    f32 = jnp.float32
    uf = u.astype(f32).reshape(bsz, s_len, S5_GROUPS, S5_GROUP)
    lam = lax.complex(lam_re.astype(f32), lam_im.astype(f32))
    delta = jnp.exp(log_step.astype(f32))[:, None]
    lam_bar = jnp.exp(lam * delta)
    b_mat = lax.complex(b_re.astype(f32), b_im.astype(f32))
    b_bar = ((lam_bar - 1.0) / lam)[:, :, None] * b_mat
    bu = jnp.einsum('gpc,bsgc->bsgp', b_bar, uf.astype(jnp.complex64))
    a = jnp.broadcast_to(lam_bar, bu.shape)

    def combine(e1, e2):
        a1, x1 = e1
        a2, x2 = e2
        return a2 * a1, a2 * x1 + x2

    _, states = lax.associative_scan(combine, (a, bu), axis=1)
    c_mat = lax.complex(c_re.astype(f32), c_im.astype(f32))
    y = jnp.einsum('gcp,bsgp->bsgc', c_mat, states).real + d_skip.astype(f32) * uf
    y = jax.nn.gelu(y.reshape(bsz, s_len, S5_WIDTH))
    y = y * jax.nn.sigmoid(y @ w_glu.astype(f32))
    return y.astype(u.dtype)


def even_mixer(h, pos, w_in, w_out, lam_re, lam_im, b_re, b_im, c_re, c_im, d_skip, log_step, w_glu):
    bsz, s_len, _ = h.shape
    proj = h @ w_in
    q, k, v, g, u = jnp.split(proj, [RET_WIDTH, 2 * RET_WIDTH, 3 * RET_WIDTH, 4 * RET_WIDTH], axis=-1)

    def heads(t):
        return t.reshape(bsz, s_len, RET_HEADS, RET_HEAD_DIM).transpose(0, 2, 1, 3)

    ret = retention(rotary(heads(q), pos), rotary(heads(k), pos), heads(v))
    ret = head_layer_norm(ret).transpose(0, 2, 1, 3).reshape(bsz, s_len, RET_WIDTH)
    ret = (jax.nn.silu(g.astype(jnp.float32)) * ret).astype(h.dtype)
    ssm = s5_mixer(u, lam_re, lam_im, b_re, b_im, c_re, c_im, d_skip, log_step, w_glu)
    return jnp.concatenate([ret, ssm], axis=-1) @ w_out


def diff_attention(h, pos, w_qkv, w_o, lq1, lk1, lq2, lk2, subln, lambda_init):
    bsz, s_len, _ = h.shape
    f32 = jnp.float32
    proj = h @ w_qkv
    q, k, v = jnp.split(proj, [DIFF_QK_WIDTH, 2 * DIFF_QK_WIDTH], axis=-1)
    q = rotary(q.reshape(bsz, s_len, 2 * DIFF_HEADS, DIFF_QK_DIM).transpose(0, 2, 1, 3), pos)
    k = rotary(k.reshape(bsz, s_len, 2 * DIFF_HEADS, DIFF_QK_DIM).transpose(0, 2, 1, 3), pos)
    vf = v.reshape(bsz, s_len, DIFF_HEADS, DIFF_V_DIM).transpose(0, 2, 1, 3).astype(f32)
    kf = k.astype(f32)
    lam = (jnp.exp(jnp.sum(lq1.astype(f32) * lk1.astype(f32)))
           - jnp.exp(jnp.sum(lq2.astype(f32) * lk2.astype(f32))) + lambda_init)
    scale = DIFF_QK_DIM ** -0.5
    n_blocks = s_len // Q_BLOCK
    q_blocks = q.reshape(bsz, 2 * DIFF_HEADS, n_blocks, Q_BLOCK, DIFF_QK_DIM).transpose(2, 0, 1, 3, 4)
    qpos_blocks = pos.reshape(n_blocks, Q_BLOCK)

    def block(args):
        qblk, qpos = args
        s = jnp.einsum('bhqd,bhkd->bhqk', qblk.astype(f32), kf) * scale
        s = jnp.where(pos[None, :] <= qpos[:, None], s, -jnp.inf)
        att = jax.nn.softmax(s, axis=-1).reshape(bsz, DIFF_HEADS, 2, Q_BLOCK, s_len)
        w = att[:, :, 0] - lam * att[:, :, 1]
        return jnp.einsum('bhqk,bhkd->bhqd', w, vf)

    out = lax.map(block, (q_blocks, qpos_blocks))
    out = out.transpose(1, 2, 0, 3, 4).reshape(bsz, DIFF_HEADS, s_len, DIFF_V_DIM)
    out = rms_norm(out, subln) * (1.0 - lambda_init)
    out = out.transpose(0, 2, 1, 3).reshape(bsz, s_len, DIFF_WIDTH).astype(h.dtype)
    return out @ w_o


def swiglu(h, w_gate, w_up, w_down):
    return (jax.nn.silu(h @ w_gate) * (h @ w_up)) @ w_down


def setup_inputs(seed: int = 0) -> dict:
    key = jax.random.key(seed)
    ks = jax.random.split(key, 32)
    nrm = jax.random.normal
    f32 = jnp.float32
    inp = {}
    inp['x'] = nrm(ks[0], (BATCH, SEQ, D_MODEL), f32)
    inp['p'] = nrm(ks[1], (DEPTH, BATCH, SEQ, PLE_DIM), f32)
    inp['norm_mix'] = 1.0 + 0.02 * nrm(ks[2], (DEPTH, D_MODEL), f32)
    inp['norm_ffn'] = 1.0 + 0.02 * nrm(ks[3], (DEPTH, D_MODEL), f32)
    inp['norm_ple'] = 1.0 + 0.02 * nrm(ks[4], (DEPTH, D_MODEL), f32)
    inp['ret_s5_w_in'] = nrm(ks[5], (N_EVEN, D_MODEL, EVEN_IN_WIDTH), f32) * D_MODEL ** -0.5
    inp['ret_s5_w_out'] = nrm(ks[6], (N_EVEN, MIX_WIDTH, D_MODEL), f32) * MIX_WIDTH ** -0.5
    inp['s5_lambda_re'] = -0.5 + 0.01 * nrm(ks[7], (N_EVEN, S5_GROUPS, S5_STATE), f32)
    inp['s5_lambda_im'] = (math.pi * jnp.arange(S5_STATE, dtype=f32)
                           + 0.01 * nrm(ks[8], (N_EVEN, S5_GROUPS, S5_STATE), f32))
    inp['s5_b_re'] = nrm(ks[9], (N_EVEN, S5_GROUPS, S5_STATE, S5_GROUP), f32) * (2 * S5_GROUP) ** -0.5
    inp['s5_b_im'] = nrm(ks[10], (N_EVEN, S5_GROUPS, S5_STATE, S5_GROUP), f32) * (2 * S5_GROUP) ** -0.5
    inp['s5_c_re'] = nrm(ks[11], (N_EVEN, S5_GROUPS, S5_GROUP, S5_STATE), f32) * S5_STATE ** -0.5
    inp['s5_c_im'] = nrm(ks[12], (N_EVEN, S5_GROUPS, S5_GROUP, S5_STATE), f32) * S5_STATE ** -0.5
    inp['s5_d'] = nrm(ks[13], (N_EVEN, S5_GROUPS, S5_GROUP), f32)
    inp['s5_log_step'] = jax.random.uniform(ks[14], (N_EVEN, S5_GROUPS), f32,
                                            minval=math.log(1e-3), maxval=math.log(1e-1))
    inp['s5_w_glu'] = nrm(ks[15], (N_EVEN, S5_WIDTH, S5_WIDTH), f32) * S5_WIDTH ** -0.5
    inp['diff_w_qkv'] = nrm(ks[16], (N_ODD, D_MODEL, DIFF_IN_WIDTH), f32) * D_MODEL ** -0.5
    inp['diff_w_o'] = nrm(ks[17], (N_ODD, DIFF_WIDTH, D_MODEL), f32) * DIFF_WIDTH ** -0.5
    inp['diff_lambda_q1'] = 0.1 * nrm(ks[18], (N_ODD, DIFF_QK_DIM), f32)
    inp['diff_lambda_k1'] = 0.1 * nrm(ks[19], (N_ODD, DIFF_QK_DIM), f32)
    inp['diff_lambda_q2'] = 0.1 * nrm(ks[20], (N_ODD, DIFF_QK_DIM), f32)
    inp['diff_lambda_k2'] = 0.1 * nrm(ks[21], (N_ODD, DIFF_QK_DIM), f32)
    inp['diff_subln'] = 1.0 + 0.02 * nrm(ks[22], (N_ODD, DIFF_V_DIM), f32)
    inp['ffn_w_gate'] = nrm(ks[23], (DEPTH, D_MODEL, FFN_HIDDEN), f32) * D_MODEL ** -0.5
    inp['ffn_w_up'] = nrm(ks[24], (DEPTH, D_MODEL, FFN_HIDDEN), f32) * D_MODEL ** -0.5
    inp['ffn_w_down'] = nrm(ks[25], (DEPTH, FFN_HIDDEN, D_MODEL), f32) * FFN_HIDDEN ** -0.5
    inp['ple_w_proj'] = nrm(ks[26], (DEPTH, PLE_DIM, D_MODEL), f32) * PLE_DIM ** -0.5
    inp['ple_w_gate'] = nrm(ks[27], (DEPTH, D_MODEL, D_MODEL), f32) * D_MODEL ** -0.5
    inp['final_norm'] = 1.0 + 0.02 * nrm(ks[28], (D_MODEL,), f32)
    return inp


def reference(x, p, norm_mix, norm_ffn, norm_ple, ret_s5_w_in, ret_s5_w_out,
              s5_lambda_re, s5_lambda_im, s5_b_re, s5_b_im, s5_c_re, s5_c_im, s5_d,
              s5_log_step, s5_w_glu, diff_w_qkv, diff_w_o, diff_lambda_q1, diff_lambda_k1,
              diff_lambda_q2, diff_lambda_k2, diff_subln, ffn_w_gate, ffn_w_up, ffn_w_down,
              ple_w_proj, ple_w_gate, final_norm):
    pos = jnp.arange(x.shape[1], dtype=jnp.int32)
    h = x
    for i in range(DEPTH):
        hn = rms_norm(h, norm_mix[i])
        j = i // 2
        if i % 2 == 0:
            mix = even_mixer(hn, pos, ret_s5_w_in[j], ret_s5_w_out[j],
                             s5_lambda_re[j], s5_lambda_im[j], s5_b_re[j], s5_b_im[j],
                             s5_c_re[j], s5_c_im[j], s5_d[j], s5_log_step[j], s5_w_glu[j])
        else:
            lambda_init = 0.8 - 0.6 * math.exp(-0.3 * i)
            mix = diff_attention(hn, pos, diff_w_qkv[j], diff_w_o[j],
                                 diff_lambda_q1[j], diff_lambda_k1[j],
                                 diff_lambda_q2[j], diff_lambda_k2[j], diff_subln[j], lambda_init)
        h = h + mix
        h = h + swiglu(rms_norm(h, norm_ffn[i]), ffn_w_gate[i], ffn_w_up[i], ffn_w_down[i])
        gate = jax.nn.sigmoid(rms_norm(h, norm_ple[i]) @ ple_w_gate[i])
        h = h + (p[i] @ ple_w_proj[i]) * gate
    return rms_norm(h, final_norm)
```

```python
import contextlib
import os
import math
import numpy as np
import ml_dtypes
import concourse.bass as bass
import concourse.mybir as mybir
from concourse.bass_utils import run_bass_kernel_spmd

F32 = mybir.dt.float32
BF16 = mybir.dt.bfloat16
I32 = mybir.dt.int32
ALU = mybir.AluOpType
AF = mybir.ActivationFunctionType
AX = mybir.AxisListType

D = 1024
FF = 2816
NFC = FF // 128
PLE = 256
EPS = 1e-6
PI = math.pi


class Sched:
    ENGS = ("pe", "act", "dve", "pool", "sp")

    def __init__(self, nc):
        self.nc = nc
        self.ops = []
        self.last_w = {}
        self.readers = {}
        self.bar = set()
        self.last_on = {}
        self.dmas_since = []
        self.bg_dmas = []

    def op(self, eng, fn, reads=(), writes=(), dma=False, semkey=None, bg=False):
        i = len(self.ops)
        deps = set(self.bar)
        for k in reads:
            if k in self.last_w:
                deps.add(self.last_w[k])
        for k in writes:
            if k in self.last_w:
                deps.add(self.last_w[k])
            for r in self.readers.get(k, ()):
                deps.add(r)
        for k in reads:
            self.readers.setdefault(k, []).append(i)
        for k in writes:
            self.last_w[k] = i
            self.readers[k] = []
        deps.discard(i)
        if dma and semkey is None:
            semkey = (list(writes) + list(reads))[0]
        self.ops.append(dict(eng=eng, fn=fn, deps=deps, dma=dma, semkey=semkey))
        if dma and bg:
            self.bg_dmas.append(i)
        elif dma:
            self.dmas_since.append(i)
        else:
            self.last_on[eng] = i
        return i

    def barrier(self, include_bg=False):
        self.bar = set(self.last_on.values()) | set(self.dmas_since)
        if include_bg:
            self.bar |= set(self.bg_dmas)
            self.bg_dmas = []
        self.dmas_since = []
        self.last_w = {}
        self.readers = {}

    def emit(self, final_wait=()):
        nc, ops = self.nc, self.ops
        needed = set(final_wait)
        for o in ops:
            needed |= o["deps"]
        cnt = {e: 0 for e in self.ENGS}
        dcnt = {}
        for i, o in enumerate(ops):
            if o["dma"]:
                k = o["semkey"]
                dcnt[k] = dcnt.get(k, 0) + 16
                o["sig"] = ("d", k, dcnt[k])
            elif i in needed:
                cnt[o["eng"]] += 1
                o["sig"] = ("e", o["eng"], cnt[o["eng"]])
            else:
                o["sig"] = None
        with contextlib.ExitStack() as st:
            esem = {e: st.enter_context(nc.semaphore("s_" + e)) for e in self.ENGS}
            dsem = {}
            print("[sched] ops=%d dma_sems=%d" % (len(ops), len(dcnt)))
            for n, k in enumerate(dcnt):
                dsem[k] = st.enter_context(nc.semaphore("d%d" % n))
            block = st.enter_context(nc.Block())
            per = {e: [] for e in self.ENGS}
            for i, o in enumerate(ops):
                per[o["eng"]].append(i)

            def run(engname, eng):
                waited = {}

                def do_waits(deps):
                    want = {}
                    for d in deps:
                        s = ops[d]["sig"]
                        if s is None:
                            continue
                        if s[0] == "e" and s[1] == engname and engname == "pe":
                            continue
                        key = (s[0], s[1])
                        want[key] = max(want.get(key, 0), s[2])
                    for key, v in want.items():
                        if waited.get(key, 0) >= v:
                            continue
                        waited[key] = v
                        sem = esem[key[1]] if key[0] == "e" else dsem[key[1]]
                        eng.wait_ge(sem, v)

                for i in per[engname]:
                    o = ops[i]
                    do_waits(o["deps"])
                    ins = o["fn"](eng)
                    s = o["sig"]
                    if s is not None:
                        if s[0] == "d":
                            ins.then_inc(dsem[s[1]], 16)
                        else:
                            ins.then_inc(esem[s[1]], 1)
                if engname == "sp":
                    do_waits(final_wait)

            block.tensor(lambda e: run("pe", e))
            block.scalar(lambda e: run("act", e))
            block.vector(lambda e: run("dve", e))
            block.gpsimd(lambda e: run("pool", e))
            block.sync(lambda e: run("sp", e))


def host_consts(T):
    c = {}
    c["c_identb"] = np.eye(128, dtype=np.float32).astype(ml_dtypes.bfloat16)
    c["c_identf"] = np.eye(128, dtype=np.float32)
    d = 64
    inv = (10000.0 ** (-np.arange(0, d, 2, dtype=np.float32) / d)).astype(np.float32)
    pos = np.arange(T, dtype=np.float32)
    ang = (pos[None, :] * inv[:, None]).astype(np.float32)
    cos = np.cos(ang).astype(np.float32)
    sin = np.sin(ang).astype(np.float32)
    cos64 = np.concatenate([cos, cos], 0)
    sin64 = np.concatenate([-sin, sin], 0)
    c["c_rope"] = np.stack([np.concatenate([cos64, cos64], 0), np.concatenate([sin64, sin64], 0)], 0).astype(np.float32)
    gam = (1.0 - 2.0 ** (-5.0 - np.arange(8))).astype(np.float64)
    s = np.arange(128)
    m = np.zeros((128, 8, 128), np.float64)
    for h in range(8):
        mm_ = (gam[h] ** (-(s[:, None] + 1.0))) / 8.0 * (s[None, :] >= s[:, None])
        m[:, h, :] = mm_
    c["c_retmask"] = m.reshape(128, 1024).astype(np.float32)
    kd = np.zeros((128, 8, 64), np.float64)
    for h in range(8):
        kd[:, h, :] = ((gam[h] ** (127.0 - s)) / 8.0)[:, None]
    c["c_kdec"] = kd.reshape(128, 512).astype(np.float32)
    g128 = np.zeros((128, 4, 128), np.float64)
    for pr in range(4):
        for hl in range(2):
            g128[hl * 64:(hl + 1) * 64, pr, :] = gam[2 * pr + hl] ** 128
    c["c_g128"] = g128.reshape(128, 512).astype(np.float32)
    ep = np.zeros((128, 8), np.float64)
    for h in range(8):
        ep[:, h] = 64.0 * EPS / gam[h] ** (2.0 * (s + 1.0))
    c["c_reps"] = ep.astype(np.float32)
    mb = np.where(s[:, None] <= s[None, :], 0.0, -30000.0).astype(np.float32)
    c["c_maskb"] = mb.astype(ml_dtypes.bfloat16)
    c["c_iota"] = np.tile(np.arange(128, dtype=np.float32)[None, :], (128, 1))
    perm = np.zeros((128, 128), np.float32)
    for mcol in range(128):
        perm[(mcol + 64) % 128, mcol] = 1.0
    c["c_perm"] = perm
    rm = np.zeros((128, 8), np.float32)
    for p in range(128):
        rm[p, p // 16] = 1.0
    c["c_rowmask"] = rm
    sg = np.zeros((128, 8), np.float32)
    top = (np.arange(128) < 64)
    sg[:, 0] = np.where(top, -1.0, 1.0); sg[:, 1] = np.where(top, PI, -PI)
    sg[:, 2] = np.where(top, 1.0, -1.0); sg[:, 3] = np.where(top, -PI, PI)
    sg[:, 4] = np.where(top, 0.5, -0.5); sg[:, 5] = np.where(top, -0.5, 0.5); sg[:, 6] = PI
    c["c_sgn"] = sg
    return c


WEIGHT_NAMES = ["norm_mix", "norm_ffn", "norm_ple", "ret_s5_w_in", "ret_s5_w_out", "s5_lambda_re", "s5_lambda_im",
                "s5_b_re", "s5_b_im", "s5_c_re", "s5_c_im", "s5_d", "s5_log_step", "s5_w_glu", "diff_w_qkv",
                "diff_w_o", "diff_lambda_q1", "diff_lambda_k1", "diff_lambda_q2", "diff_lambda_k2", "diff_subln",
                "ffn_w_gate", "ffn_w_up", "ffn_w_down", "ple_w_proj", "ple_w_gate", "final_norm"]
WEIGHT_SHAPES = {
    "norm_mix": [2, 1024], "norm_ffn": [2, 1024], "norm_ple": [2, 1024], "ret_s5_w_in": [1, 1024, 2560],
    "ret_s5_w_out": [1, 1024, 1024], "s5_lambda_re": [1, 32, 64], "s5_lambda_im": [1, 32, 64],
    "s5_b_re": [1, 32, 64, 16], "s5_b_im": [1, 32, 64, 16], "s5_c_re": [1, 32, 16, 64], "s5_c_im": [1, 32, 16, 64],
    "s5_d": [1, 32, 16], "s5_log_step": [1, 32], "s5_w_glu": [1, 512, 512], "diff_w_qkv": [1, 1024, 3072],
    "diff_w_o": [1, 1024, 1024], "diff_lambda_q1": [1, 64], "diff_lambda_k1": [1, 64], "diff_lambda_q2": [1, 64],
    "diff_lambda_k2": [1, 64], "diff_subln": [1, 128], "ffn_w_gate": [2, 1024, 2816], "ffn_w_up": [2, 1024, 2816],
    "ffn_w_down": [2, 2816, 1024], "ple_w_proj": [2, 256, 1024], "ple_w_gate": [2, 1024, 1024], "final_norm": [1024],
}


class Builder:
    def __init__(self, T, debug=()):
        self.T = T
        self.NB = T // 512
        self.debug = debug
        nc = bass.Bass("TRN2", target_bir_lowering=False)
        self.nc = nc
        self.S = Sched(nc)
        self.din = {}
        self.din["x"] = nc.dram_tensor("x", [T, D], F32, kind="ExternalInput").ap()
        self.din["p"] = nc.dram_tensor("p", [2, T, PLE], F32, kind="ExternalInput").ap()
        for n in WEIGHT_NAMES:
            self.din[n] = nc.dram_tensor(n, WEIGHT_SHAPES[n], F32, kind="ExternalInput").ap()
        hc = host_consts(T)
        for n, a in hc.items():
            dt = BF16 if a.dtype == ml_dtypes.bfloat16 else F32
            self.din[n] = nc.dram_tensor(n, list(a.shape), dt, kind="ExternalInput").ap()
        self.y = nc.dram_tensor("y", [T, D], F32, kind="ExternalOutput").ap()
        sk = "ExternalOutput" if debug else "Internal"
        self.h1 = nc.dram_tensor("h1s", [T, D], F32, kind=sk).ap()
        self.mixT = nc.dram_tensor("mixTs", [D, T], BF16, kind=sk).ap()
        self.qkT = nc.dram_tensor("qkTs", [2048, T], BF16, kind=sk).ap()
        self.vS = nc.dram_tensor("vs", [T, D], BF16, kind=sk).ap()
        self.wbf = {}
        for l in range(2):
            for nm, shp in [("wo", [1024, 1024]), ("wg", [1024, FF]), ("wu", [1024, FF]), ("wd", [FF, 1024]),
                            ("wpg", [1024, 1024]), ("wpp", [PLE, 1024])]:
                self.wbf[(nm, l)] = nc.dram_tensor("%s_bf%d" % (nm, l), shp, BF16, kind="Internal").ap()
        self.dbg = {}
        self.AW = 52600
        self.arena = nc.alloc_sbuf_tensor("arena", [128, self.AW], F32)
        self.top = 0
        self.pb = [nc.alloc_psum_tensor("pb%d" % i, [128, 512], F32) for i in range(8)]
        self.pbi = 0
        self.conv_pending = {}
        self.conv_count = {}
        self.reserved = set()
        self.final = []

    def conv_list(self, l):
        din = self.din
        srcs = {"wo": (din["ret_s5_w_out"] if l == 0 else din["diff_w_o"])[0], "wg": din["ffn_w_gate"][l],
                "wu": din["ffn_w_up"][l], "wd": din["ffn_w_down"][l], "wpg": din["ple_w_gate"][l], "wpp": din["ple_w_proj"][l]}
        fns = []
        for nm, src in srcs.items():
            dst = self.wbf[(nm, l)]
            rows = dst.shape[0]
            step = 256
            for r0 in range(0, rows, step):
                def f(chain=None, dst=dst, src=src, r0=r0, step=step, nm=nm):
                    wk = [("wbf", l, "chain", chain)] if chain is not None else [("wbf", nm, l, r0)]
                    self.DMA("pool", dst[r0:r0 + step, :], src[r0:r0 + step, :], [], wk, semkey=("wbf", l, chain), bg=True)
                fns.append(f)
        return fns

    def convert_weights(self, l, n=None, chain_depth=None):
        if l not in self.conv_pending:
            self.conv_pending[l] = self.conv_list(l)
            self.conv_count[l] = 0
        lst = self.conv_pending[l]
        k = len(lst) if n is None else min(n, len(lst))
        for _ in range(k):
            f = lst.pop(0)
            if chain_depth:
                f(chain=self.conv_count[l] % chain_depth)
            else:
                f()
            self.conv_count[l] += 1

    def alloc(self, shape, dt):
        n = int(np.prod(shape))
        words = n if dt in (F32, I32) else (n + 1) // 2
        assert self.top + words <= self.AW, ("SBUF arena overflow", self.top, words)
        v = self.arena[:, self.top:self.top + words]
        self.top += words
        if dt != F32:
            v = v.bitcast(dt)
            if dt == BF16:
                v = v[:, 0:n]
        if len(shape) == 2:
            return v.rearrange("p (a b) -> p a b", a=shape[0])
        if len(shape) == 3:
            return v.rearrange("p (a b c) -> p a b c", a=shape[0], b=shape[1])
        return v

    def nb(self, reserve=False):
        while True:
            b = self.pbi
            self.pbi = (self.pbi + 1) % 8
            if b not in self.reserved:
                break
        if reserve:
            self.reserved.add(b)
        return b

    def MM(self, out, lhsT, rhs, start, stop, r, w):
        self.S.op("pe", lambda e: e.matmul(out, lhsT, rhs, start=start, stop=stop), reads=r, writes=w)

    def TR(self, out, in_, ident, r, w):
        self.S.op("pe", lambda e: e.transpose(out=out, in_=in_, identity=ident), reads=r, writes=w)

    def ACT(self, out, in_, func, r, w, scale=1.0, bias=None, accum=None):
        def f(e):
            kw = dict(out=out, in_=in_, func=func, scale=scale)
            if bias is not None:
                kw["bias"] = bias
            if accum is not None:
                kw["accum_out"] = accum
            return e.activation(**kw)
        self.S.op("act", f, reads=r, writes=w)

    def TT(self, eng, out, in0, in1, op, r, w):
        self.S.op(eng, lambda e: e.tensor_tensor(out=out, in0=in0, in1=in1, op=op), reads=r, writes=w)

    def TS(self, eng, out, in0, s1, s2, op0, op1, r, w):
        if op1 is None:
            self.S.op(eng, lambda e: e.tensor_scalar(out=out, in0=in0, scalar1=s1, scalar2=None, op0=op0), reads=r, writes=w)
        else:
            self.S.op(eng, lambda e: e.tensor_scalar(out=out, in0=in0, scalar1=s1, scalar2=s2, op0=op0, op1=op1), reads=r, writes=w)

    def STT(self, eng, out, in0, scalar, in1, op0, op1, r, w):
        self.S.op(eng, lambda e: e.scalar_tensor_tensor(out=out, in0=in0, scalar=scalar, in1=in1, op0=op0, op1=op1), reads=r, writes=w)

    def CP(self, eng, out, in_, r, w):
        self.S.op(eng, lambda e: e.tensor_copy(out=out, in_=in_), reads=r, writes=w)

    def MS(self, eng, out, val, w):
        self.S.op(eng, lambda e: e.memset(out, val), writes=w)

    def DMA(self, eng, out, in_, r, w, semkey=None, slow=False, bg=False):
        if bg:
            return self.S.op(eng, lambda e: e.dma_start(out=out, in_=in_), reads=r, writes=w, dma=True, semkey=semkey, bg=True)
        if slow:
            return self.S.op(eng, lambda e: e.dma_start(out=out, in_=in_, allow_slow_non_contiguous=True), reads=r, writes=w, dma=True, semkey=semkey)
        return self.S.op(eng, lambda e: e.dma_start(out=out, in_=in_), reads=r, writes=w, dma=True, semkey=semkey)

    def rsqrt(self, eng, out, a, k, key_a, key_out, tag):
        ti = self.tmp_i[:, 0:k]
        y = self.tmp_y[:, 0:k]
        ta = self.tmp_a[:, 0:k]
        tb = self.tmp_b[:, 0:k]
        yf = y.bitcast(F32)
        kk = ("rsq", tag)
        self.TS(eng, ti, a.bitcast(I32), 1, None, ALU.arith_shift_right, None, [key_a], [kk + ("ti",)])
        self.TS(eng, y, ti, -1, 1597463007, ALU.mult, ALU.add, [kk + ("ti",)], [kk + ("y",)])
        for _ in range(3):
            self.TT(eng, ta, yf, yf, ALU.mult, [kk + ("y",)], [kk + ("ta",)])
            self.TT(eng, tb, ta, a, ALU.mult, [kk + ("ta",), key_a], [kk + ("tb",)])
            self.TS(eng, ta, tb, -0.5, 1.5, ALU.mult, ALU.add, [kk + ("tb",)], [kk + ("ta",)])
            self.TT(eng, yf, yf, ta, ALU.mult, [kk + ("y",), kk + ("ta",)], [kk + ("y",)])
        self.CP(eng, out, yf, [kk + ("y",)], [key_out])

    def build(self):
        S = self.S
        din = self.din
        T, NB = self.T, self.NB
        self.identb = self.alloc([128], BF16)
        self.identf = self.alloc([128], F32)
        self.gT = self.alloc([7, 8], F32)
        self.sgn = self.alloc([8], F32)
        self.lam = self.alloc([4], F32)
        self.tmp_i = self.alloc([32], I32)
        self.tmp_y = self.alloc([32], I32)
        self.tmp_a = self.alloc([32], F32)
        self.tmp_b = self.alloc([32], F32)
        self.DMA("sp", self.identb, din["c_identb"], [], ["identb"])
        self.DMA("sp", self.identf, din["c_identf"], [], ["identf"])
        self.DMA("sp", self.sgn, din["c_sgn"], [], ["sgn"])
        for l in range(2):
            for j, nm in enumerate(["norm_mix", "norm_ffn", "norm_ple"]):
                self.DMA("sp", self.gT[:, 3 * l + j, :], din[nm][l].rearrange("(kc p) -> p kc", p=128), [], ["gT"], slow=True)
        self.TS("dve", self.gT[:, 0:6, :], self.gT[:, 0:6, :], 32.0, None, ALU.mult, None, ["gT"], ["gT"])
        lamt = self.alloc([4, 64], F32)
        for j, nm in enumerate(["diff_lambda_q1", "diff_lambda_k1", "diff_lambda_q2", "diff_lambda_k2"]):
            self.DMA("sp", lamt[:, j, :], din[nm][0:1, :].to_broadcast([128, 64]), [], ["lamt"], slow=True)
        lam2 = self.alloc([2, 64], F32)
        lsum = self.alloc([2], F32)
        self.TT("dve", lam2[:, 0, :], lamt[:, 0, :], lamt[:, 1, :], ALU.mult, ["lamt"], ["lam2"])
        self.TT("dve", lam2[:, 1, :], lamt[:, 2, :], lamt[:, 3, :], ALU.mult, ["lamt"], ["lam2"])
        S.op("dve", lambda e: e.tensor_reduce(out=lsum, in_=lam2, axis=AX.X, op=ALU.add), reads=["lam2"], writes=["lsum"])
        lexp = self.alloc([2], F32)
        self.ACT(lexp, lsum, AF.Exp, ["lsum"], ["lexp"])
        lambda_init = 0.8 - 0.6 * math.exp(-0.3 * 1)
        self.TT("dve", self.lam[:, 0:1], lexp[:, 1:2], lexp[:, 0:1], ALU.subtract, ["lexp"], ["lam"])
        self.TS("dve", self.lam[:, 0:1], self.lam[:, 0:1], -lambda_init, None, ALU.add, None, ["lam"], ["lam"])
        self.DMA("sp", self.lam[:, 1:2], din["diff_subln"][0].rearrange("(e o) -> e o", o=1), [], ["lam1"], slow=True)
        self.TS("dve", self.lam[:, 1:2], self.lam[:, 1:2], (1.0 - lambda_init) * math.sqrt(128.0), None, ALU.mult, None, ["lam1"], ["lam1"])
        self.lambda_init = lambda_init
        self.base_top = self.top
        stages = self.debug if self.debug else ("l0ab", "l0c", "l1a", "l1b", "l1c")
        if "l0ab" not in stages:
            self.convert_weights(0, chain_depth=4)
        if "l1b" not in stages:
            self.convert_weights(1, chain_depth=4)
        if "l0ab" in stages:
            self.layer0_ab()
        S.barrier(include_bg=True)
        if "l0c" in stages:
            self.top = self.base_top
            self.phase_c(0)
            S.barrier()
        if "l1a" in stages:
            self.top = self.base_top
            self.layer1_a()
            S.barrier()
        if "l1b" in stages:
            self.top = self.base_top
            self.layer1_b()
        S.barrier(include_bg=True)
        if "l1c" in stages:
            self.top = self.base_top
            self.phase_c(1)
        S.emit(final_wait=self.final)
        return self.nc

    def norm_block(self, src_tiles, src_keys, gidx, hnT, hnT_key, tagp, toff=0):
        n = len(src_tiles)
        ss = self.n_ss
        self.MS("dve", ss[:, 0:n], 0.0, [(tagp, "ss")])
        for tt in range(n):
            self.ACT(self.n_junk, src_tiles[tt], AF.Square, [src_keys[tt], (tagp, "ss")], [(tagp, "ss"), "n_junk"], accum=ss[:, tt:tt + 1])
        self.TS("dve", ss[:, 0:n], ss[:, 0:n], D * EPS, None, ALU.add, None, [(tagp, "ss")], [(tagp, "ss")])
        self.rsqrt("dve", self.n_rs[:, 0:n], ss[:, 0:n], n, (tagp, "ss"), (tagp, "rs"), tagp)
        for tt in range(n):
            sl = tt % 2
            xn = self.n_xn[:, sl, :]
            self.ACT(xn, src_tiles[tt], AF.Copy, [src_keys[tt], (tagp, "rs")], [("n_xn", sl)], scale=self.n_rs[:, tt:tt + 1])
            b = self.nb()
            pbb = self.pb[b][:].bitcast(BF16)
            for kc in range(8):
                self.TR(pbb[:, kc * 128:(kc + 1) * 128], xn[:, kc * 128:(kc + 1) * 128], self.identb, [("n_xn", sl), "identb"], [("pb", b)])
            t_ = toff + tt
            self.TT("dve", hnT[:, :, t_ * 128:(t_ + 1) * 128], pbb[:, 0:1024].rearrange("p (a b) -> p a b", a=8),
                    self.gT[:, gidx, :].unsqueeze(2).to_broadcast([128, 8, 128]), ALU.mult, [("pb", b), "gT"], [hnT_key])

    def alloc_norm(self):
        self.n_ss = self.alloc([4], F32)
        self.n_rs = self.alloc([4], F32)
        self.n_junk = self.alloc([1024], BF16)
        self.n_xn = self.alloc([2, 1024], BF16)

    def phase_c(self, l):
        din = self.din
        NB = self.NB
        wo_src = self.wbf[("wo", l)].rearrange("(kc p) f -> p kc f", p=128)
        wg_src = self.wbf[("wg", l)].rearrange("(kc p) f -> p kc f", p=128)
        wu_src = self.wbf[("wu", l)].rearrange("(kc p) f -> p kc f", p=128)
        wd_src = self.wbf[("wd", l)].rearrange("(fc p) f -> p fc f", p=128)
        wpg_src = self.wbf[("wpg", l)].rearrange("(kc p) f -> p kc f", p=128)
        wpp_src = self.wbf[("wpp", l)].rearrange("(kc p) f -> p kc f", p=128)
        hsrc = din["x"] if l == 0 else self.h1
        self.alloc_norm()
        if l == 1:
            self.fing = self.alloc([1024], F32)
            self.DMA("sp", self.fing, din["final_norm"].unsqueeze(0).to_broadcast([128, 1024]), [], ["fing"], slow=True)
            self.TS("dve", self.fing, self.fing, 32.0, None, ALU.mult, None, ["fing"], ["fing"])
        Wo = self.alloc([8, 1024], BF16)
        Wg = [self.alloc([8, 512], BF16) for _ in range(2)]
        Wu = [self.alloc([8, 512], BF16) for _ in range(2)]
        Wd = [self.alloc([NFC, 256], BF16) for _ in range(2)]
        Wpg = self.alloc([8, 1024], BF16)
        Wpp = self.alloc([2, 1024], BF16)
        mixb = [self.alloc([8, 512], BF16)]
        hm2 = [self.alloc([4, 1024], F32) for _ in range(2)]
        hnT = self.alloc([8, 512], BF16)
        actT = self.alloc([NFC, 512], BF16)
        sg = [self.alloc([512], BF16) for _ in range(2)]
        th = [self.alloc([512], F32) for _ in range(2)]
        pbf = [self.alloc([256], BF16) for _ in range(2)]
        pT = [self.alloc([2, 128], BF16) for _ in range(2)]
        tg = "c%d" % l
        groups = [(0, 512), (512, 512), (1024, 512), (1536, 512), (2048, 512), (2560, 256)]
        gi = 0
        for tb in range(NB):
            t0 = tb * 512
            ms = 0
            hsl = tb % 2
            hm = hm2[hsl]
            def load_block(tbx):
                tx = tbx * 512
                sx = tbx % 2
                self.DMA("sp", mixb[ms], self.mixT.rearrange("(kc p) t -> p kc t", p=128)[:, :, tx:tx + 512], [], [("mixb", ms)])
                for tt in range(4):
                    self.DMA("sp", hm2[sx][:, tt, :], hsrc[tx + tt * 128:tx + (tt + 1) * 128, :], [], [("hm", sx, tt)])
            if tb == 0:
                load_block(0)
            self.DMA("pool", Wo, wo_src, [], ["Wo"])
            for tt in range(4):
                for half in range(2):
                    b = self.nb()
                    for kc in range(8):
                        self.MM(self.pb[b][:], mixb[ms][:, kc, tt * 128:(tt + 1) * 128], Wo[:, kc, half * 512:(half + 1) * 512],
                                kc == 0, kc == 7, [("mixb", ms), "Wo"], [("pb", b)])
                    hv = hm[:, tt, half * 512:(half + 1) * 512]
                    self.TT("dve", hv, hv, self.pb[b][:], ALU.add, [("pb", b), ("hm", hsl, tt)], [("hm", hsl, tt)])
            self.norm_block([hm[:, tt, :] for tt in range(4)], [("hm", hsl, tt) for tt in range(4)], 3 * l + 1, hnT, "hnT", tg + "n1")
            for (f0, fw) in groups:
                s = gi % 2
                gi += 1
                self.DMA("pool", Wg[s][:, :, 0:fw], wg_src[:, :, f0:f0 + fw], [], [("Wg", s)])
                self.DMA("pool", Wu[s][:, :, 0:fw], wu_src[:, :, f0:f0 + fw], [], [("Wu", s)])
                for fl in range(fw // 128):
                    fc = f0 // 128 + fl
                    bg = self.nb()
                    for kc in range(8):
                        self.MM(self.pb[bg][:], Wg[s][:, kc, fl * 128:(fl + 1) * 128], hnT[:, kc, :], kc == 0, kc == 7,
                                [("Wg", s), "hnT"], [("pb", bg)])
                    bu = self.nb()
                    for kc in range(8):
                        self.MM(self.pb[bu][:], Wu[s][:, kc, fl * 128:(fl + 1) * 128], hnT[:, kc, :], kc == 0, kc == 7,
                                [("Wu", s), "hnT"], [("pb", bu)])
                    ss_ = fc % 2
                    self.ACT(sg[ss_], self.pb[bg][:], AF.Silu, [("pb", bg)], [("sg", ss_)])
                    self.TT("dve", actT[:, fc, :], sg[ss_], self.pb[bu][:], ALU.mult, [("sg", ss_), ("pb", bu)], [("actT", fc)])
            if tb + 1 < NB:
                load_block(tb + 1)
            for q in range(4):
                self.DMA("pool", Wd[q % 2], wd_src[:, :, q * 256:(q + 1) * 256], [], [("Wd", q % 2)])
                for tt in range(4):
                    b = self.nb()
                    for fc in range(NFC):
                        self.MM(self.pb[b][:, 0:256], actT[:, fc, tt * 128:(tt + 1) * 128], Wd[q % 2][:, fc, :], fc == 0, fc == NFC - 1,
                                [("actT", fc), ("Wd", q % 2)], [("pb", b)])
                    hv = hm[:, tt, q * 256:(q + 1) * 256]
                    self.TT("dve", hv, hv, self.pb[b][:, 0:256], ALU.add, [("pb", b), ("hm", hsl, tt)], [("hm", hsl, tt)])
            self.DMA("pool", Wpg, wpg_src, [], ["Wpg"])
            self.DMA("pool", Wpp, wpp_src, [], ["Wpp"])
            self.norm_block([hm[:, tt, :] for tt in range(4)], [("hm", hsl, tt) for tt in range(4)], 3 * l + 2, hnT, "hnT", tg + "n2")
            for tt in range(4):
                s = tt % 2
                self.DMA("pool", pbf[s], din["p"][l, t0 + tt * 128:t0 + (tt + 1) * 128, :], [], [("pbf", s)])
                b = self.nb()
                pbb = self.pb[b][:].bitcast(BF16)
                for pc in range(2):
                    self.TR(pbb[:, pc * 128:(pc + 1) * 128], pbf[s][:, pc * 128:(pc + 1) * 128], self.identb, [("pbf", s), "identb"], [("pb", b)])
                self.ACT(pT[s], pbb[:, 0:256].rearrange("p (a b) -> p a b", a=2), AF.Copy, [("pb", b)], [("pT", s)])
                for half in range(2):
                    ba = self.nb()
                    for pc in range(2):
                        self.MM(self.pb[ba][:], pT[s][:, pc, :], Wpp[:, pc, half * 512:(half + 1) * 512], pc == 0, pc == 1,
                                [("pT", s), "Wpp"], [("pb", ba)])
                    bz = self.nb()
                    for kc in range(8):
                        self.MM(self.pb[bz][:], hnT[:, kc, tt * 128:(tt + 1) * 128], Wpg[:, kc, half * 512:(half + 1) * 512], kc == 0, kc == 7,
                                ["hnT", "Wpg"], [("pb", bz)])
                    self.ACT(th[half], self.pb[bz][:], AF.Tanh, [("pb", bz)], [("th", half)], scale=0.5)
                    self.STT("dve", th[half], th[half], 1.0, self.pb[ba][:], ALU.add, ALU.mult, [("th", half), ("pb", ba)], [("th", half)])
                    hv = hm[:, tt, half * 512:(half + 1) * 512]
                    self.STT("dve", hv, th[half], 0.5, hv, ALU.mult, ALU.add, [("th", half), ("hm", hsl, tt)], [("hm", hsl, tt)])
            if l == 0:
                for tt in range(4):
                    self.DMA("sp", self.h1[t0 + tt * 128:t0 + (tt + 1) * 128, :], hm[:, tt, :], [("hm", hsl, tt)], [], semkey=("hm", hsl, tt))
            else:
                ss = self.n_ss
                self.MS("dve", ss[:, 0:4], 0.0, [(tg, "fss")])
                for tt in range(4):
                    self.ACT(self.n_junk, hm[:, tt, :], AF.Square, [("hm", hsl, tt), (tg, "fss")], [(tg, "fss"), "n_junk"], accum=ss[:, tt:tt + 1])
                self.TS("dve", ss[:, 0:4], ss[:, 0:4], D * EPS, None, ALU.add, None, [(tg, "fss")], [(tg, "fss")])
                self.rsqrt("dve", self.n_rs[:, 0:4], ss[:, 0:4], 4, (tg, "fss"), (tg, "frs"), tg + "f")
                for tt in range(4):
                    self.STT("dve", hm[:, tt, :], hm[:, tt, :], self.n_rs[:, tt:tt + 1], self.fing, ALU.mult, ALU.mult,
                             [("hm", hsl, tt), (tg, "frs"), "fing"], [("hm", hsl, tt)])
                    i = self.DMA("sp", self.y[t0 + tt * 128:t0 + (tt + 1) * 128, :], hm[:, tt, :], [("hm", hsl, tt)], [], semkey=("hm", hsl, tt))
                    self.final.append(i)

    def layer1_a(self):
        din = self.din
        NB = self.NB
        self.alloc_norm()
        wsrc = din["diff_w_qkv"][0].rearrange("(kc p) f -> p kc f", p=128)
        W = self.alloc([8, 3072], BF16)
        Wsw = self.alloc([8, 2048], BF16)
        for kc in range(8):
            self.DMA("pool", W[:, kc, :], wsrc[:, kc, :], [], [("W1", kc)])
        for kc in range(8):
            wv = W[:, kc, 0:2048].rearrange("p (m h j) -> p m h j", m=32, h=2)
            sv = Wsw[:, kc, :].rearrange("p (m h j) -> p m h j", m=32, h=2)
            self.CP("dve" if kc % 2 == 0 else "pool", sv[:, :, 0, :], wv[:, :, 1, :], [("W1", kc)], [("Wsw", kc)])
            self.CP("dve" if kc % 2 == 0 else "pool", sv[:, :, 1, :], wv[:, :, 0, :], [("W1", kc)], [("Wsw", kc)])
        Wk = [("W1", kc) for kc in range(8)]
        Wswk = [("Wsw", kc) for kc in range(8)]
        hx = self.alloc([4, 1024], F32)
        hnT = self.alloc([8, 512], BF16)
        rope = [self.alloc([2, 512], F32) for _ in range(2)]
        ta = [self.alloc([512], F32) for _ in range(2)]
        tb_ = [self.alloc([512], F32) for _ in range(2)]
        qk = [self.alloc([512], BF16) for _ in range(2)]
        vb = [self.alloc([512], BF16) for _ in range(2)]
        cnt = 0
        for tb in range(NB):
            t0 = tb * 512
            rs = tb % 2
            def load_blk(tbx):
                tx = tbx * 512
                for tt in range(4):
                    self.DMA("sp", hx[:, tt, :], self.h1[tx + tt * 128:tx + (tt + 1) * 128, :], [], [("hx", tt)])
                self.DMA("sp", rope[tbx % 2], din["c_rope"].rearrange("c p t -> p c t")[:, :, tx:tx + 512], [], [("rope", tbx % 2)])
            if tb == 0:
                load_blk(0)
            self.norm_block([hx[:, tt, :] for tt in range(4)], [("hx", tt) for tt in range(4)], 3, hnT, "hnT", "l1an")
            if tb + 1 < NB:
                load_blk(tb + 1)
            for ft in range(16):
                s = cnt % 2
                cnt += 1
                ba = self.nb()
                for kc in range(8):
                    self.MM(self.pb[ba][:], W[:, kc, ft * 128:(ft + 1) * 128], hnT[:, kc, :], kc == 0, kc == 7, ["hnT", Wk[kc]], [("pb", ba)])
                bs = self.nb()
                for kc in range(8):
                    self.MM(self.pb[bs][:], Wsw[:, kc, ft * 128:(ft + 1) * 128], hnT[:, kc, :], kc == 0, kc == 7, ["hnT", Wswk[kc]], [("pb", bs)])
                self.TT("dve", ta[s], self.pb[ba][:], rope[rs][:, 0, :], ALU.mult, [("pb", ba), ("rope", rs)], [("ta", s)])
                self.TT("dve", tb_[s], self.pb[bs][:], rope[rs][:, 1, :], ALU.mult, [("pb", bs), ("rope", rs)], [("tb", s)])
                self.TT("pool", qk[s], ta[s], tb_[s], ALU.add, [("ta", s), ("tb", s)], [("qk", s)])
                self.DMA("sp", self.qkT[ft * 128:(ft + 1) * 128, t0:t0 + 512], qk[s], [("qk", s)], [], semkey=("qk", s))
            for tt in range(4):
                for half in range(2):
                    s = cnt % 2
                    cnt += 1
                    b = self.nb()
                    for kc in range(8):
                        self.MM(self.pb[b][:], hnT[:, kc, tt * 128:(tt + 1) * 128], W[:, kc, 2048 + half * 512:2048 + (half + 1) * 512],
                                kc == 0, kc == 7, ["hnT", Wk[kc]], [("pb", b)])
                    self.ACT(vb[s], self.pb[b][:], AF.Copy, [("pb", b)], [("vb", s)])
                    self.DMA("sp", self.vS[t0 + tt * 128:t0 + (tt + 1) * 128, half * 512:(half + 1) * 512], vb[s], [("vb", s)], [], semkey=("vb", s))

    def layer1_b(self):
        din = self.din
        S = self.S
        T = self.T
        if os.environ.get('NOCONV') != '1':
            self.convert_weights(1, chain_depth=4)
        NQB = T // 512
        NT = T // 128
        maskb = self.alloc([128], BF16)
        self.DMA("sp", maskb, din["c_maskb"], [], ["maskb"])
        QT = [self.alloc([T], BF16) for _ in range(2)]
        KT = [self.alloc([T], BF16) for _ in range(2)]
        V = [self.alloc([NT, 130], BF16) for _ in range(2)]
        for s in range(2):
            self.MS("dve", V[s][:, :, 128:130], 1.0, [("Vone", s)])
        PT = [self.alloc([512], BF16) for _ in range(8)]
        O1 = self.alloc([4, 128], F32)
        att4 = self.alloc([4, 128], F32)
        attb = self.alloc([4, 128], BF16)
        junk = self.alloc([128], BF16)
        rsm = self.alloc([8], F32)
        ssq = self.alloc([4], F32)
        rsq = self.alloc([4], F32)
        aT = [self.alloc([512], BF16) for _ in range(2)]
        st = {"pti": 0, "blk": 0}

        def load_head(h):
            hs = h % 2
            self.DMA("sp", QT[hs], self.qkT[h * 128:(h + 1) * 128, :], [], [("QT", hs)])
            self.DMA("sp", KT[hs], self.qkT[1024 + h * 128:1024 + (h + 1) * 128, :], [], [("KT", hs)])
            self.DMA("sp", V[hs][:, :, 0:128], self.vS[:, h * 128:(h + 1) * 128].rearrange("(n p) e -> p n e", p=128),
                     [("Vone", hs)], [("V", hs)])

        pairs = [(h, qb, kt) for h in range(8) for qb in range(NQB) for kt in range(4 * qb + 4)]
        info = {}
        obs = {}
        ACC = [(0, 0), (0, 136), (0, 272), (1, 0), (1, 136), (1, 272), (2, 0), (2, 136)]

        def stage_a(pr):
            h, qb, kt = pr
            hs = h % 2
            if h == 0 and qb == 0 and kt == 0:
                load_head(0)
            q0 = qb * 512
            dk = kt - 4 * qb
            qlo = max(0, dk) * 128
            bss = [self.nb(), self.nb()]
            for rep in range(int(os.environ.get('DUP', '1'))):
              for m in range(2):
                rows = slice(m * 64, (m + 1) * 64)
                self.MM(self.pb[bss[m]][:, qlo:512], KT[hs][rows, kt * 128:(kt + 1) * 128], QT[hs][rows, q0 + qlo:q0 + 512], True, dk < 0,
                        [("KT", hs), ("QT", hs)], [("pb", bss[m])])
            pts = []
            for m in range(2):
                ps = self.pb[bss[m]]
                if dk >= 0:
                    self.MM(ps[:, qlo:qlo + 128], self.identb, maskb, False, True, ["identb", "maskb"], [("pb", bss[m])])
                pi = st["pti"] % len(PT)
                st["pti"] += 1
                pt = PT[pi]
                ptk = ("PT", pi)
                self.ACT(pt[:, qlo:512], ps[:, qlo:512], AF.Exp, [("pb", bss[m])], [ptk], scale=0.125)
                pts.append((pt, ptk))
            info[pr] = (pts, dk)

        def stage_c(pr):
            h, qb, kt = pr
            hs = h % 2
            q0 = qb * 512
            pts, dk = info.pop(pr)
            if qb == 0 and kt == 0 and h + 1 < 8:
                load_head(h + 1)
            if kt == 0:
                obs[(h, qb)] = [self.nb(reserve=True), self.nb(reserve=True), self.nb(reserve=True)]
            ob = obs[(h, qb)]

            def acc(m, qt):
                bi, col = ACC[m * 4 + qt]
                return self.pb[ob[bi]][:, col:col + 129], ("pb", ob[bi]), (kt == 0 and col == 0 and (m * 4 + qt) in (0, 3, 6))

            for m in range(2):
                pt, ptk = pts[m]
                for qt in range(max(0, dk), 4):
                    o, okey, first = acc(m, qt)
                    last = (kt == 4 * qb + qt)
                    self.MM(o, pt[:, qt * 128:(qt + 1) * 128], V[hs][:, kt, 0:129], first, last, [ptk, ("V", hs)], [okey])
            if kt != 4 * qb + 3:
                return
            for m in range(2):
                for qt in range(4):
                    o, okey, _ = acc(m, qt)
                    rc = rsm[:, m * 4 + qt:m * 4 + qt + 1]
                    rk = ("rsm", m * 4 + qt)
                    S.op("dve", lambda e, rc=rc, o=o: e.reciprocal(out=rc, in_=o[:, 128:129]), reads=[okey], writes=[rk])
                    if m == 0:
                        self.TS("dve", O1[:, qt, :], o[:, 0:128], rc, None, ALU.mult, None, [okey, rk], [("O1", qt)])
                    else:
                        self.TT("dve", rc, rc, self.lam[:, 0:1], ALU.mult, [rk, "lam"], [rk])
                        self.STT("dve", att4[:, qt, :], o[:, 0:128], rc, O1[:, qt, :], ALU.mult, ALU.add,
                                 [okey, rk, ("O1", qt)], [("att", qt)])
                        S.op("dve", lambda e, qt=qt: e.scalar_tensor_tensor(out=junk, in0=att4[:, qt, :], scalar=1.0, in1=att4[:, qt, :],
                                                                           op0=ALU.mult, op1=ALU.mult, accum_out=ssq[:, qt:qt + 1]),
                             reads=[("att", qt)], writes=["ssq", "junkb"])
            for b_ in ob:
                self.reserved.discard(b_)
            del obs[(h, qb)]
            self.TS("dve", ssq, ssq, 128.0 * EPS, None, ALU.add, None, ["ssq"], ["ssq"])
            self.rsqrt("dve", rsq, ssq, 4, "ssq", "rsq", "l1b")
            for qt in range(4):
                self.TS("dve", attb[:, qt, :], att4[:, qt, :], rsq[:, qt:qt + 1], None, ALU.mult, None, [("att", qt), "rsq"], [("attb", qt)])

            def part2(h=h, q0=q0):
                bt = self.nb()
                pbb = self.pb[bt][:].bitcast(BF16)
                for qt in range(4):
                    self.TR(pbb[:, qt * 128:(qt + 1) * 128], attb[:, qt, :], self.identb, [("attb", qt), "identb"], [("pb", bt)])
                a = aT[st["blk"] % 2]
                ak = ("aT", st["blk"] % 2)
                st["blk"] += 1
                self.ACT(a, pbb[:, 0:512], AF.Copy, [("pb", bt), "lam1"], [ak], scale=self.lam[:, 1:2])
                self.DMA("sp", self.mixT[h * 128:(h + 1) * 128, q0:q0 + 512], a, [ak], [], semkey=ak)
            deferred.append([DEFER, part2])

        deferred = []
        DEFER = int(os.environ.get('DEFER', '2'))
        LOOK = int(os.environ.get('LOOK', '2'))
        n = len(pairs)
        for i in range(n + LOOK):
            if i < n:
                stage_a(pairs[i])
            if i - LOOK >= 0:
                stage_c(pairs[i - LOOK])
            for dfr in list(deferred):
                dfr[0] -= 1
                if dfr[0] <= 0:
                    dfr[1]()
                    deferred.remove(dfr)
        for dfr in deferred:
            dfr[1]()

    def layer0_ab(self):
        din = self.din
        S = self.S
        NB = self.NB
        self.alloc_norm()
        sgn = self.sgn
        wsrc = din["ret_s5_w_in"][0].rearrange("(kc p) f -> p kc f", p=128)
        W = self.alloc([8, 2560], BF16)
        Wsw = self.alloc([8, 1024], BF16)
        for kc in range(8):
            self.DMA("pool", W[:, kc, :], wsrc[:, kc, :], [], [("W0", kc)])
        for kc in range(8):
            wv = W[:, kc, 0:1024].rearrange("p (m h j) -> p m h j", m=16, h=2)
            sv = Wsw[:, kc, :].rearrange("p (m h j) -> p m h j", m=16, h=2)
            self.CP("dve", sv[:, :, 0, :], wv[:, :, 1, :], [("W0", kc)], [("W0sw", kc)])
            self.CP("dve", sv[:, :, 1, :], wv[:, :, 0, :], [("W0", kc)], [("W0sw", kc)])
        Wk = [("W0", kc) for kc in range(8)]
        Wswk = [("W0sw", kc) for kc in range(8)]
        Wglu = self.alloc([4, 512], BF16)
        self.DMA("pool", Wglu, din["s5_w_glu"][0].rearrange("(j p) f -> p j f", p=128), [], ["Wglu"])
        retmask = self.alloc([1024], F32)
        kdec = self.alloc([512], F32)
        g128 = self.alloc([512], F32)
        reps = self.alloc([8], F32)
        self.DMA("sp", retmask, din["c_retmask"], [], ["retmask"])
        self.DMA("sp", kdec, din["c_kdec"], [], ["kdec"])
        self.DMA("sp", g128, din["c_g128"], [], ["g128"])
        self.DMA("sp", reps, din["c_reps"], [], ["reps"])
        import os
        STOP = int(os.environ.get('L0STOP', '99'))
        RSTOP = int(os.environ.get('RSTOP', '99'))
        if STOP <= 1:
            return
        mark = self.top
        Bexp = None
        Bexp = self.alloc([32, 128], BF16)
        Bsw = self.alloc([32, 128], BF16)
        C1 = self.alloc([32, 16], BF16)
        C2 = self.alloc([32, 16], BF16)
        COS = self.alloc([32, 128], BF16)
        SINS = self.alloc([32, 128], BF16)
        rr = self.alloc([32], F32)
        Arot = self.alloc([32], F32)
        Brot = self.alloc([32], F32)
        Ddiag = self.alloc([4, 128], BF16)
        perm = self.alloc([128], F32)
        self.DMA("sp", perm, din["c_perm"], [], ["perm"])
        tmark = self.top
        LR = self.alloc([32], F32)
        LI = self.alloc([32], F32)
        DL = self.alloc([32], F32)
        for half in range(2):
            self.DMA("sp", LR[half * 64:(half + 1) * 64, :], din["s5_lambda_re"][0].rearrange("g p -> p g"), [], ["LR"], slow=True)
            self.DMA("sp", LI[half * 64:(half + 1) * 64, :], din["s5_lambda_im"][0].rearrange("g p -> p g"), [], ["LI"], slow=True)
        self.DMA("sp", DL, din["s5_log_step"][0:1, :].to_broadcast([128, 32]), [], ["DL"], slow=True)
        self.ACT(DL, DL, AF.Exp, ["DL"], ["DL"])
        th_ = self.alloc([32], F32)
        aa = self.alloc([32], F32)
        self.TT("dve", aa, LR, DL, ALU.mult, ["LR", "DL"], ["aa"])
        self.TT("dve", th_, LI, DL, ALU.mult, ["LI", "DL"], ["th"])
        self.ACT(rr, aa, AF.Exp, ["aa"], ["rr"])
        iota = self.alloc([128], F32)
        self.DMA("sp", iota, din["c_iota"], [], ["iota"])
        ang = self.alloc([32, 128], F32)
        ang2 = self.alloc([32, 128], F32)
        self.TT("dve", ang, th_.unsqueeze(2).to_broadcast([128, 32, 128]), iota.unsqueeze(1).to_broadcast([128, 32, 128]), ALU.mult,
                ["th", "iota"], ["ang"])
        angi = self.alloc([32, 128], I32)

        def sin_of(out, angle, n3, shift, scale, rk, wk):
            a2 = ang2 if n3 else ang2[:, 0, 0:32]
            ai = angi if n3 else angi[:, 0, 0:32]
            self.TS("dve", a2, angle, shift, 1.0 / (2 * PI), ALU.add, ALU.mult, rk + [wk], ["ang2"])
            self.CP("dve", ai, a2, ["ang2"], ["angi"])
            self.CP("dve", a2, ai, ["angi"], ["ang2"])
            self.STT("dve", a2, a2, -2 * PI, angle, ALU.mult, ALU.add, ["ang2"] + rk, ["ang2"])
            self.TS("dve", a2, a2, shift, None, ALU.add, None, ["ang2"], ["ang2"])
            self.TS("dve", a2, a2, -PI, PI, ALU.max, ALU.min, ["ang2"], ["ang2"])
            self.ACT(out, a2, AF.Sin, ["ang2", "sgn"], [wk], scale=scale)

        sin_of(COS, ang, True, PI / 2, 1.0, ["ang"], "COS")
        sin_of(SINS, ang, True, 0.0, sgn[:, 2:3], ["ang"], "SINS")
        a128 = self.alloc([32], F32)
        self.TS("dve", a128, th_, 128.0, None, ALU.mult, None, ["th"], ["a128"])
        sin_of(Arot, a128, False, PI / 2, 1.0, ["a128"], "Arot")
        sin_of(Brot, a128, False, 0.0, sgn[:, 0:1], ["a128"], "Brot")
        c1 = self.alloc([32], F32)
        s1 = self.alloc([32], F32)
        sin_of(c1, th_, False, PI / 2, 1.0, ["th"], "c1")
        sin_of(s1, th_, False, 0.0, 1.0, ["th"], "s1")
        if STOP <= 2:
            return
        nre = self.alloc([32], F32)
        nim = self.alloc([32], F32)
        den = self.alloc([32], F32)
        fre = self.alloc([32], F32)
        fim = self.alloc([32], F32)
        u1 = self.alloc([32], F32)
        self.TT("dve", nre, rr, c1, ALU.mult, ["rr", "c1"], ["nre"])
        self.TS("dve", nre, nre, -1.0, None, ALU.add, None, ["nre"], ["nre"])
        self.TT("dve", nim, rr, s1, ALU.mult, ["rr", "s1"], ["nim"])
        self.TT("dve", den, LR, LR, ALU.mult, ["LR"], ["den"])
        self.TT("dve", u1, LI, LI, ALU.mult, ["LI"], ["u1"])
        self.TT("dve", den, den, u1, ALU.add, ["den", "u1"], ["den"])
        S.op("dve", lambda e: e.reciprocal(out=den, in_=den), reads=["den"], writes=["den"])
        self.TT("dve", fre, nre, LR, ALU.mult, ["nre", "LR"], ["fre"])
        self.TT("dve", u1, nim, LI, ALU.mult, ["nim", "LI", "den"], ["u1"])
        self.TT("dve", fre, fre, u1, ALU.add, ["fre", "u1"], ["fre"])
        self.TT("dve", fre, fre, den, ALU.mult, ["fre", "den"], ["fre"])
        self.TT("dve", fim, nim, LR, ALU.mult, ["nim", "LR"], ["fim"])
        self.TT("dve", u1, nre, LI, ALU.mult, ["nre", "LI", "fre"], ["u1"])
        self.TT("dve", fim, fim, u1, ALU.subtract, ["fim", "u1"], ["fim"])
        self.TT("dve", fim, fim, den, ALU.mult, ["fim", "den"], ["fim"])
        Bre = self.alloc([32, 16], F32)
        Bim = self.alloc([32, 16], F32)
        self.DMA("sp", Bre[0:64], din["s5_b_re"][0].rearrange("g p c -> p g c"), [], ["Bre"], slow=True)
        self.DMA("sp", Bim[0:64], din["s5_b_im"][0].rearrange("g p c -> p g c"), [], ["Bim"], slow=True)
        Bbr = self.alloc([32, 16], F32)
        Bbi = self.alloc([32, 16], F32)
        v1 = self.alloc([32, 16], F32)
        frb = fre[0:64].unsqueeze(2).to_broadcast([64, 32, 16])
        fib = fim[0:64].unsqueeze(2).to_broadcast([64, 32, 16])
        self.TT("dve", Bbr[0:64], Bre[0:64], frb, ALU.mult, ["Bre", "fre"], ["Bbr"])
        self.TT("dve", v1[0:64], Bim[0:64], fib, ALU.mult, ["Bim", "fim"], ["v1"])
        self.TT("dve", Bbr[0:64], Bbr[0:64], v1[0:64], ALU.subtract, ["Bbr", "v1"], ["Bbr"])
        self.TT("dve", Bbi[0:64], Bim[0:64], frb, ALU.mult, ["Bim", "fre"], ["Bbi"])
        self.TT("dve", v1[0:64], Bre[0:64], fib, ALU.mult, ["Bre", "fim", "Bbr"], ["v1"])
        self.TT("dve", Bbi[0:64], Bbi[0:64], v1[0:64], ALU.add, ["Bbi", "v1"], ["Bbi"])
        if STOP <= 3:
            return
        rowmask = self.alloc([8], F32)
        self.DMA("sp", rowmask, din["c_rowmask"], [], ["rowmask"])
        Tre = self.alloc([64], F32)
        Tim = self.alloc([64], F32)
        rmb = rowmask.unsqueeze(2).to_broadcast([128, 8, 64])
        for j in range(4):
            b = self.nb()
            self.TR(self.pb[b][:, 0:64], Bbr[0:64, j * 8:(j + 1) * 8, :].rearrange("p g c -> p (g c)"), self.identf[0:64, 0:64], ["Bbr", "identf"], [("pb", b)])
            self.TR(self.pb[b][:, 64:128], Bbi[0:64, j * 8:(j + 1) * 8, :].rearrange("p g c -> p (g c)"), self.identf[0:64, 0:64], ["Bbi", "identf"], [("pb", b)])
            self.CP("dve", Tre, self.pb[b][:, 0:64], [("pb", b)], ["Tre"])
            self.CP("dve", Tim, self.pb[b][:, 64:128], [("pb", b)], ["Tim"])
            treb = Tre.unsqueeze(1).to_broadcast([128, 8, 64])
            timb = Tim.unsqueeze(1).to_broadcast([128, 8, 64])
            self.TT("dve", Bexp[:, j * 8:(j + 1) * 8, 0:64], treb, rmb, ALU.mult, ["Tre", "rowmask"], ["Bexp"])
            self.TT("dve", Bexp[:, j * 8:(j + 1) * 8, 64:128], timb, rmb, ALU.mult, ["Tim", "rowmask"], ["Bexp"])
            self.TT("dve", Bsw[:, j * 8:(j + 1) * 8, 0:64], timb, rmb, ALU.mult, ["Tim", "rowmask"], ["Bsw"])
            self.TT("dve", Bsw[:, j * 8:(j + 1) * 8, 64:128], treb, rmb, ALU.mult, ["Tre", "rowmask"], ["Bsw"])
        if STOP <= 4:
            return
        CC = self.alloc([4, 128], F32)
        CC2 = self.alloc([4, 128], F32)
        cre = din["s5_c_re"][0].rearrange("(j g) c p -> (g c) j p", j=4)
        cim = din["s5_c_im"][0].rearrange("(j g) c p -> (g c) j p", j=4)
        self.DMA("sp", CC[:, :, 0:64], cre, [], ["CC"], slow=True)
        self.DMA("sp", CC[:, :, 64:128], cim, [], ["CC"], slow=True)
        self.DMA("sp", CC2[:, :, 0:64], cim, [], ["CC2"], slow=True)
        self.DMA("sp", CC2[:, :, 64:128], cre, [], ["CC2"], slow=True)
        for j in range(4):
            b = self.nb()
            self.TR(self.pb[b][:, 0:128], CC[:, j, :], self.identf, ["CC", "identf"], [("pb", b)])
            self.TR(self.pb[b][:, 128:256], CC2[:, j, :], self.identf, ["CC2", "identf"], [("pb", b)])
            self.TS("dve", C1[:, j * 8:(j + 1) * 8, :], self.pb[b][:, 0:128].rearrange("p (g c) -> p g c", g=8), sgn[:, 4:5], None, ALU.mult, None,
                    [("pb", b), "sgn"], ["C1"])
            self.TS("dve", C2[:, j * 8:(j + 1) * 8, :], self.pb[b][:, 128:256].rearrange("p (g c) -> p g c", g=8), sgn[:, 5:6], None, ALU.mult, None,
                    [("pb", b), "sgn"], ["C2"])
        Dcol = self.alloc([4], F32)
        self.DMA("sp", Dcol, din["s5_d"][0].rearrange("(j g) c -> (g c) j", j=4), [], ["Dcol"], slow=True)
        self.TS("dve", Dcol, Dcol, 0.5, None, ALU.mult, None, ["Dcol"], ["Dcol"])
        for j in range(4):
            self.TS("dve", Ddiag[:, j, :], self.identf, Dcol[:, j:j + 1], None, ALU.mult, None, ["identf", "Dcol"], ["Ddiag"])
        if STOP <= 5:
            return
        S5K = ["Bexp", "Bsw", "C1", "C2", "COS", "SINS", "rr", "Arot", "Brot", "Ddiag", "perm"]
        self.S.barrier()
        self.top = tmark

        hx = self.alloc([2, 1024], F32)
        hnT = self.alloc([8, 512], BF16)
        rope = [self.alloc([2, 512], F32)]
        ta = [self.alloc([512], F32)]
        tbb = [self.alloc([512], F32)]
        qT = self.alloc([4, 512], BF16)
        kT = self.alloc([4, 512], BF16)
        vtm = self.alloc([4, 512], BF16)
        sgt = self.alloc([4, 512], BF16)
        uT = self.alloc([4, 512], BF16)
        mixb = self.alloc([8, 512], BF16)
        PTm = self.alloc([1024], BF16)
        Kd = self.alloc([512], BF16)
        R = self.alloc([512], F32)
        Rbf = self.alloc([512], BF16)
        self.MS("dve", R, 0.0, ["R"])
        self.MS("dve", Rbf, 0.0, ["Rbf"])
        sum1 = self.alloc([8], F32)
        var1 = self.alloc([8], F32)
        rs8 = self.alloc([8], F32)
        xc = self.alloc([8, 64], F32)
        sq = self.alloc([8, 64], F32)
        rtm = self.alloc([512], BF16)
        RV = [self.alloc([8, 128], F32)]
        TMP = [self.alloc([8, 128], BF16)]
        Wst = [self.alloc([8, 128], F32)]
        P1 = [self.alloc([8, 128], BF16)]
        P2 = [self.alloc([8, 128], BF16)]
        w127 = self.alloc([32], F32)
        init = self.alloc([32], F32)
        ctmp = self.alloc([32], F32)
        self.MS("dve", init, 0.0, ["init"])
        gs = ta[0]
        gi1 = ta[0]
        gth = tbb[0]
        Gb = self.alloc([512], BF16)
        GT = self.alloc([4, 128], BF16)
        th2 = self.alloc([4, 128], F32)
        K2 = 2.0 * math.sqrt(2.0 / PI)
        for tb in range(NB):
            t0 = tb * 512
            rs_ = 0
            self.DMA("sp", rope[rs_], din["c_rope"].rearrange("c p t -> p c t")[:, :, t0:t0 + 512], [], [("rope", rs_)])
            for hf in range(2):
                for tt in range(2):
                    tg_ = hf * 2 + tt
                    self.DMA("sp", hx[:, tt, :], din["x"][t0 + tg_ * 128:t0 + (tg_ + 1) * 128, :], [], [("hx", tt)])
                self.norm_block([hx[:, tt, :] for tt in range(2)], [("hx", tt) for tt in range(2)], 0, hnT, "hnT", "l0n", toff=hf * 2)
            for ft in range(8):
                s = 0
                ba = self.nb()
                for kc in range(8):
                    self.MM(self.pb[ba][:], W[:, kc, ft * 128:(ft + 1) * 128], hnT[:, kc, :], kc == 0, kc == 7, ["hnT", Wk[kc]], [("pb", ba)])
                bs = self.nb()
                for kc in range(8):
                    self.MM(self.pb[bs][:], Wsw[:, kc, ft * 128:(ft + 1) * 128], hnT[:, kc, :], kc == 0, kc == 7, ["hnT", Wswk[kc]], [("pb", bs)])
                self.TT("dve", ta[s], self.pb[ba][:], rope[rs_][:, 0, :], ALU.mult, [("pb", ba), ("rope", rs_)], [("ta", s)])
                self.TT("dve", tbb[s], self.pb[bs][:], rope[rs_][:, 1, :], ALU.mult, [("pb", bs), ("rope", rs_)], [("tb", s)])
                dst = qT[:, ft, :] if ft < 4 else kT[:, ft - 4, :]
                dk_ = ("qT", ft) if ft < 4 else ("kT", ft - 4)
                self.TT("pool", dst, ta[s], tbb[s], ALU.add, [("ta", s), ("tb", s)], [dk_])
            for c in range(4):
                b = self.nb()
                for kc in range(8):
                    self.MM(self.pb[b][:], hnT[:, kc, c * 128:(c + 1) * 128], W[:, kc, 1024:1536], kc == 0, kc == 7, ["hnT", Wk[kc]], [("pb", b)])
                self.ACT(vtm[:, c, :], self.pb[b][:], AF.Copy, [("pb", b)], [("vtm", c)])
                b = self.nb()
                for kc in range(8):
                    self.MM(self.pb[b][:], hnT[:, kc, c * 128:(c + 1) * 128], W[:, kc, 1536:2048], kc == 0, kc == 7, ["hnT", Wk[kc]], [("pb", b)])
                self.ACT(sgt[:, c, :], self.pb[b][:], AF.Silu, [("pb", b)], [("sgt", c)])
            for j in range(4):
                b = self.nb()
                for kc in range(8):
                    self.MM(self.pb[b][:], W[:, kc, 2048 + j * 128:2048 + (j + 1) * 128], hnT[:, kc, :], kc == 0, kc == 7, ["hnT", Wk[kc]], [("pb", b)])
                self.ACT(uT[:, j, :], self.pb[b][:], AF.Copy, [("pb", b)], [("uT", j)])
            def ret_thread():
                for c in range(4):
                    cs = slice(c * 128, (c + 1) * 128)
                    self.convert_weights(0, n=2)
                    bs0, bs1 = self.nb(True), self.nb(True)
                    for h in range(8):
                        pr, hl = h // 2, h % 2
                        rows = slice(hl * 64, (hl + 1) * 64)
                        bb = bs0 if hl == 0 else bs1
                        self.MM(self.pb[bb][:, pr * 128:(pr + 1) * 128], kT[rows, pr, cs], qT[rows, pr, cs], True, True,
                                [("kT", pr), ("qT", pr)], [("pb", bb)])
                    yield
                    pv = PTm.rearrange("p (r l t) -> p r l t", r=4, l=2)
                    mv = retmask.rearrange("p (r l t) -> p r l t", r=4, l=2)
                    self.TT("dve", pv[:, :, 0, :], self.pb[bs0][:].rearrange("p (r t) -> p r t", r=4), mv[:, :, 0, :], ALU.mult,
                            [("pb", bs0), "retmask"], [("PTm", 0)])
                    self.TT("dve", pv[:, :, 1, :], self.pb[bs1][:].rearrange("p (r t) -> p r t", r=4), mv[:, :, 1, :], ALU.mult,
                            [("pb", bs1), "retmask"], [("PTm", 1)])
                    for b_ in (bs0, bs1):
                        self.reserved.discard(b_)
                    bk = self.nb(True)
                    pkb = self.pb[bk][:].bitcast(BF16)
                    for pr in range(4):
                        self.TR(pkb[:, pr * 128:(pr + 1) * 128], kT[:, pr, cs], self.identb, [("kT", pr), "identb"], [("pb", bk)])
                    yield
                    self.TT("dve", Kd, pkb[:, 0:512], kdec, ALU.mult, [("pb", bk), "kdec"], ["Kd"])
                    self.reserved.discard(bk)
                    bo = self.nb(True)
                    for h in range(8):
                        pr, hl = h // 2, h % 2
                        rows = slice(hl * 64, (hl + 1) * 64)
                        self.MM(self.pb[bo][:, h * 64:(h + 1) * 64], PTm[:, h * 128:(h + 1) * 128], vtm[:, c, h * 64:(h + 1) * 64], True, False,
                                [("PTm", 0), ("PTm", 1), ("vtm", c)], [("pb", bo)])
                        self.MM(self.pb[bo][:, h * 64:(h + 1) * 64], qT[rows, pr, cs], Rbf[rows, pr * 128 + hl * 64:pr * 128 + (hl + 1) * 64], False, True,
                                [("qT", pr), "Rbf"], [("pb", bo)])
                    bkv = self.nb(True)
                    for pr in range(4):
                        self.MM(self.pb[bkv][:, pr * 128:(pr + 1) * 128], Kd[:, pr * 128:(pr + 1) * 128], vtm[:, c, pr * 128:(pr + 1) * 128], True, True,
                                ["Kd", ("vtm", c)], [("pb", bkv)])
                    yield
                    self.TT("dve", R, R, g128, ALU.mult, ["R", "g128"], ["R"])
                    self.TT("dve", R, R, self.pb[bkv][:], ALU.add, ["R", ("pb", bkv)], ["R"])
                    self.reserved.discard(bkv)
                    self.ACT(Rbf, R, AF.Copy, ["R"], ["Rbf"])
                    po = self.pb[bo][:].rearrange("p (h e) -> p h e", h=8)
                    S.op("dve", lambda e, po=po: e.tensor_reduce(out=sum1, in_=po, axis=AX.X, op=ALU.add), reads=[("pb", bo)], writes=["sum1"])
                    self.TS("dve", sum1, sum1, 1.0 / 64.0, None, ALU.mult, None, ["sum1"], ["sum1"])
                    self.TT("dve", xc, po, sum1.unsqueeze(2).to_broadcast([128, 8, 64]), ALU.subtract, [("pb", bo), "sum1"], ["xc"])
                    self.reserved.discard(bo)
                    self.ACT(sq, xc, AF.Square, ["xc"], ["sq"])
                    yield
                    S.op("dve", lambda e: e.tensor_reduce(out=var1, in_=sq, axis=AX.X, op=ALU.add), reads=["sq"], writes=["var1"])
                    self.TT("dve", var1, var1, reps, ALU.add, ["var1", "reps"], ["var1"])
                    self.rsqrt("dve", rs8, var1, 8, "var1", "rs8", "l0ln")
                    self.STT("dve", xc, xc, 8.0, rs8.unsqueeze(2).to_broadcast([128, 8, 64]), ALU.mult, ALU.mult, ["xc", "rs8"], ["xc"])
                    self.TT("pool", rtm, xc.rearrange("p h e -> p (h e)"), sgt[:, c, :], ALU.mult, ["xc", ("sgt", c)], ["rtm"])
                    yield
                    bt = self.nb()
                    ptb = self.pb[bt][:].bitcast(BF16)
                    for j in range(4):
                        self.TR(ptb[:, j * 128:(j + 1) * 128], rtm[:, j * 128:(j + 1) * 128], self.identb, ["rtm", "identb"], [("pb", bt)])
                    self.ACT(mixb[:, 0:4, cs], ptb[:, 0:512].rearrange("p (j t) -> p j t", j=4), AF.Copy, [("pb", bt)], [("mixb", c)])
                    yield

            def s5_thread():
                for c in range(4):
                    cs = slice(c * 128, (c + 1) * 128)
                    by = None
                    pend = {}

                    def s1(ht):
                        j, hf = ht // 2, ht % 2
                        g0 = j * 8 + hf * 4
                        bu, bw = self.nb(True), self.nb(True)
                        for gl in range(4):
                            g = g0 + gl
                            self.MM(self.pb[bu][:, gl * 128:(gl + 1) * 128], Bexp[:, g, :], uT[:, j, cs], True, True, ["Bexp", ("uT", j)], [("pb", bu)])
                            self.MM(self.pb[bw][:, gl * 128:(gl + 1) * 128], Bsw[:, g, :], uT[:, j, cs], True, True, ["Bsw", ("uT", j)], [("pb", bw)])
                        pend[ht] = (bu, bw)

                    s1(0)
                    for ht in range(8):
                        j, hf = ht // 2, ht % 2
                        g0 = j * 8 + hf * 4
                        hs_ = slice(hf * 4, hf * 4 + 4)
                        if ht + 1 < 8:
                            s1(ht + 1)
                        bu, bw = pend.pop(ht)
                        yield
                        self.TT("dve", RV[0][:, hs_, :], self.pb[bu][:].rearrange("p (g t) -> p g t", g=4), COS[:, g0:g0 + 4, :], ALU.mult,
                                [("pb", bu), "COS"], [("RV", hf)])
                        self.TT("dve", TMP[0][:, hs_, :], self.pb[bw][:].rearrange("p (g t) -> p g t", g=4), SINS[:, g0:g0 + 4, :], ALU.mult,
                                [("pb", bw), "SINS"], [("TMP", hf)])
                        self.reserved.discard(bu)
                        self.reserved.discard(bw)
                        self.TT("dve", RV[0][:, hs_, :], RV[0][:, hs_, :], TMP[0][:, hs_, :], ALU.add, [("RV", hf), ("TMP", hf)], [("RV", hf)])
                        yield
                        for gl in range(4):
                            g = g0 + gl
                            S.op("dve", lambda e, gl=gl, g=g, hf=hf: e.tensor_tensor_scan(out=Wst[0][:, hf * 4 + gl, :], data0=rr[:, g:g + 1].to_broadcast([128, 128]),
                                                                                         data1=RV[0][:, hf * 4 + gl, :], initial=init[:, g:g + 1],
                                                                                         op0=ALU.mult, op1=ALU.add),
                                 reads=[("RV", hf), "rr", "init"], writes=[("Wst", hf)])
                        self.CP("dve", w127[:, g0:g0 + 4], Wst[0][:, hs_, 127], [("Wst", hf)], [("w127", ht)])
                        self.TT(os.environ.get('P1ENG', 'dve'), P1[0][:, hs_, :], Wst[0][:, hs_, :], COS[:, g0:g0 + 4, :], ALU.mult, [("Wst", hf), "COS"], [("P1", hf)])
                        self.TT("pool", P2[0][:, hs_, :], Wst[0][:, hs_, :], SINS[:, g0:g0 + 4, :], ALU.mult, [("Wst", hf), "SINS"], [("P2", hf)])
                        yield
                        if by is None:
                            by = self.nb(True)
                        if hf == 0:
                            self.MM(self.pb[by][:, j * 128:(j + 1) * 128], uT[:, j, cs], Ddiag[:, j, :], True, False, [("uT", j), "Ddiag"], [("pb", by)])
                        for gl in range(4):
                            g = g0 + gl
                            self.MM(self.pb[by][:, g * 16:(g + 1) * 16], P1[0][:, hf * 4 + gl, :], C1[:, g, :], False, False, [("P1", hf), "C1"], [("pb", by)])
                            self.MM(self.pb[by][:, g * 16:(g + 1) * 16], P2[0][:, hf * 4 + gl, :], C2[:, g, :], False, hf == 1 and gl == 3, [("P2", hf), "C2"], [("pb", by)])
                    bc = self.nb(True)
                    wk = [("w127", ht) for ht in range(8)]
                    self.MM(self.pb[bc][:, 0:32], perm, w127, True, True, ["perm"] + wk, [("pb", bc)])
                    yield
                    self.TT("dve", ctmp, self.pb[bc][:, 0:32], Brot, ALU.mult, [("pb", bc), "Brot"], ["ctmp"])
                    self.reserved.discard(bc)
                    self.TT("dve", init, w127, Arot, ALU.mult, wk + ["Arot"], ["init"])
                    self.TT("dve", init, init, ctmp, ALU.add, ["init", "ctmp"], ["init"])
                    yh = self.pb[by][:]
                    yk = ("pb", by)
                    self.ACT(gs, yh, AF.Square, [yk], [("ta", 0)])
                    self.TS("dve", gs, gs, 0.044715 * 4.0, 1.0, ALU.mult, ALU.add, [("ta", 0)], [("ta", 0)])
                    self.TT("dve", gi1, gs, yh, ALU.mult, [("ta", 0), yk], [("ta", 0)])
                    self.ACT(gth, gi1, AF.Tanh, [("ta", 0)], [("tb", 0)], scale=K2)
                    yield
                    self.STT("dve", Gb, gth, 1.0, yh, ALU.add, ALU.mult, [("tb", 0), yk], ["Gb"])
                    self.reserved.discard(by)
                    bt = self.nb(True)
                    ptb = self.pb[bt][:].bitcast(BF16)
                    for j in range(4):
                        self.TR(ptb[:, j * 128:(j + 1) * 128], Gb[:, j * 128:(j + 1) * 128], self.identb, ["Gb", "identb"], [("pb", bt)])
                    yield
                    self.ACT(GT, ptb[:, 0:512].rearrange("p (j t) -> p j t", j=4), AF.Copy, [("pb", bt)], ["GT"], scale=0.5)
                    self.reserved.discard(bt)
                    bz = self.nb(True)
                    for j2 in range(4):
                        for j in range(4):
                            self.MM(self.pb[bz][:, j2 * 128:(j2 + 1) * 128], Wglu[:, j, j2 * 128:(j2 + 1) * 128], GT[:, j, :], j == 0 and j2 == 0, j == 3,
                                    ["Wglu", "GT"], [("pb", bz)])
                    yield
                    self.ACT(th2, self.pb[bz][:].rearrange("p (j t) -> p j t", j=4), AF.Tanh, [("pb", bz)], ["th2"])
                    self.reserved.discard(bz)
                    self.STT("dve", mixb[:, 4:8, cs], th2, 1.0, GT, ALU.add, ALU.mult, ["th2", "GT"], [("mixb2", c)])
                    yield

            threads = [ret_thread(), s5_thread()]
            weights = [int(os.environ.get('WR', '1')), int(os.environ.get('WS', '4'))]
            if os.environ.get('ONLY') == 'ret':
                threads, weights = [ret_thread()], [1]
            if os.environ.get('ONLY') == 's5':
                threads, weights = [s5_thread()], [1]
            if os.environ.get('SEQ') == '1':
                threads, weights = [ret_thread(), s5_thread()], [1000, 1000]
            while threads:
                for ti in range(len(threads) - 1, -1, -1):
                    for _ in range(weights[ti]):
                        try:
                            next(threads[ti])
                        except StopIteration:
                            threads.pop(ti)
                            weights.pop(ti)
                            break
            st_i = self.DMA("sp", self.mixT.rearrange("(kc p) t -> p kc t", p=128)[:, :, t0:t0 + 512], mixb,
                     [("mixb", c) for c in range(4)] + [("mixb2", c) for c in range(4)], [], semkey="mixb_out")
            if tb == NB - 1:
                self.convert_weights(0, chain_depth=4)
            for c in range(4):
                self.S.readers.setdefault(("mixb", c), []).append(st_i)
                self.S.readers.setdefault(("mixb2", c), []).append(st_i)


_CACHE = {}


def _get_nc(T, debug=()):
    key = (T, tuple(debug))
    if key not in _CACHE:
        _CACHE[key] = Builder(T, debug).build()
    return _CACHE[key]


def kernel(**inputs):
    x = np.asarray(inputs["x"], dtype=np.float32)
    p = np.asarray(inputs["p"], dtype=np.float32)
    B, T, _ = x.shape
    nc = _get_nc(T)
    hc = host_consts(T)
    in_maps = []
    for b in range(B):
        m = {"x": np.ascontiguousarray(x[b]), "p": np.ascontiguousarray(p[:, b])}
        for n in WEIGHT_NAMES:
            m[n] = np.ascontiguousarray(np.asarray(inputs[n], dtype=np.float32))
        m.update(hc)
        in_maps.append(m)
    res = run_bass_kernel_spmd(nc, in_maps, core_ids=list(range(B)))
    return np.stack([np.asarray(r["y"], dtype=np.float32) for r in res.results], axis=0)
```

```python
import contextlib
import os
import math
import numpy as np
import ml_dtypes
import concourse.bass as bass
import concourse.mybir as mybir
from concourse.bass_utils import run_bass_kernel_spmd

F32 = mybir.dt.float32
BF16 = mybir.dt.bfloat16
I32 = mybir.dt.int32
ALU = mybir.AluOpType
AF = mybir.ActivationFunctionType
AX = mybir.AxisListType

D = 1024
FF = 2816
NFC = FF // 128
PLE = 256
EPS = 1e-6
PI = math.pi


class Sched:
    ENGS = ("pe", "act", "dve", "pool", "sp")

    def __init__(self, nc):
        self.nc = nc
        self.ops = []
        self.last_w = {}
        self.readers = {}
        self.bar = set()
        self.last_on = {}
        self.dmas_since = []
        self.bg_dmas = []

    def op(self, eng, fn, reads=(), writes=(), dma=False, semkey=None, bg=False):
        i = len(self.ops)
        deps = set(self.bar)
        for k in reads:
            if k in self.last_w:
                deps.add(self.last_w[k])
        for k in writes:
            if k in self.last_w:
                deps.add(self.last_w[k])
            for r in self.readers.get(k, ()):
                deps.add(r)
        for k in reads:
            self.readers.setdefault(k, []).append(i)
        for k in writes:
            self.last_w[k] = i
            self.readers[k] = []
        deps.discard(i)
        if dma and semkey is None:
            semkey = (list(writes) + list(reads))[0]
        self.ops.append(dict(eng=eng, fn=fn, deps=deps, dma=dma, semkey=semkey))
        if dma and bg:
            self.bg_dmas.append(i)
        elif dma:
            self.dmas_since.append(i)
        else:
            self.last_on[eng] = i
        return i

    def barrier(self, include_bg=False):
        self.bar = set(self.last_on.values()) | set(self.dmas_since)
        if include_bg:
            self.bar |= set(self.bg_dmas)
            self.bg_dmas = []
        self.dmas_since = []
        self.last_w = {}
        self.readers = {}

    def emit(self, final_wait=()):
        nc, ops = self.nc, self.ops
        needed = set(final_wait)
        for o in ops:
            needed |= o["deps"]
        cnt = {e: 0 for e in self.ENGS}
        dcnt = {}
        for i, o in enumerate(ops):
            if o["dma"]:
                k = o["semkey"]
                dcnt[k] = dcnt.get(k, 0) + 16
                o["sig"] = ("d", k, dcnt[k])
            elif i in needed:
                cnt[o["eng"]] += 1
                o["sig"] = ("e", o["eng"], cnt[o["eng"]])
            else:
                o["sig"] = None
        with contextlib.ExitStack() as st:
            esem = {e: st.enter_context(nc.semaphore("s_" + e)) for e in self.ENGS}
            dsem = {}
            print("[sched] ops=%d dma_sems=%d" % (len(ops), len(dcnt)))
            for n, k in enumerate(dcnt):
                dsem[k] = st.enter_context(nc.semaphore("d%d" % n))
            block = st.enter_context(nc.Block())
            per = {e: [] for e in self.ENGS}
            for i, o in enumerate(ops):
                per[o["eng"]].append(i)

            def run(engname, eng):
                waited = {}

                def do_waits(deps):
                    want = {}
                    for d in deps:
                        s = ops[d]["sig"]
                        if s is None:
                            continue
                        if s[0] == "e" and s[1] == engname and engname == "pe":
                            continue
                        key = (s[0], s[1])
                        want[key] = max(want.get(key, 0), s[2])
                    for key, v in want.items():
                        if waited.get(key, 0) >= v:
                            continue
                        waited[key] = v
                        sem = esem[key[1]] if key[0] == "e" else dsem[key[1]]
                        eng.wait_ge(sem, v)

                for i in per[engname]:
                    o = ops[i]
                    do_waits(o["deps"])
                    ins = o["fn"](eng)
                    s = o["sig"]
                    if s is not None:
                        if s[0] == "d":
                            ins.then_inc(dsem[s[1]], 16)
                        else:
                            ins.then_inc(esem[s[1]], 1)
                if engname == "sp":
                    do_waits(final_wait)

            block.tensor(lambda e: run("pe", e))
            block.scalar(lambda e: run("act", e))
            block.vector(lambda e: run("dve", e))
            block.gpsimd(lambda e: run("pool", e))
            block.sync(lambda e: run("sp", e))


def host_consts(T):
    c = {}
    c["c_identb"] = np.eye(128, dtype=np.float32).astype(ml_dtypes.bfloat16)
    c["c_identf"] = np.eye(128, dtype=np.float32)
    d = 64
    inv = (10000.0 ** (-np.arange(0, d, 2, dtype=np.float32) / d)).astype(np.float32)
    pos = np.arange(T, dtype=np.float32)
    ang = (pos[None, :] * inv[:, None]).astype(np.float32)
    cos = np.cos(ang).astype(np.float32)
    sin = np.sin(ang).astype(np.float32)
    cos64 = np.concatenate([cos, cos], 0)
    sin64 = np.concatenate([-sin, sin], 0)
    c["c_rope"] = np.stack([np.concatenate([cos64, cos64], 0), np.concatenate([sin64, sin64], 0)], 0).astype(np.float32)
    gam = (1.0 - 2.0 ** (-5.0 - np.arange(8))).astype(np.float64)
    s = np.arange(128)
    m = np.zeros((128, 8, 128), np.float64)
    for h in range(8):
        mm_ = (gam[h] ** (-(s[:, None] + 1.0))) / 8.0 * (s[None, :] >= s[:, None])
        m[:, h, :] = mm_
    c["c_retmask"] = m.reshape(128, 1024).astype(np.float32)
    kd = np.zeros((128, 8, 64), np.float64)
    for h in range(8):
        kd[:, h, :] = ((gam[h] ** (127.0 - s)) / 8.0)[:, None]
    c["c_kdec"] = kd.reshape(128, 512).astype(np.float32)
    g128 = np.zeros((128, 4, 128), np.float64)
    for pr in range(4):
        for hl in range(2):
            g128[hl * 64:(hl + 1) * 64, pr, :] = gam[2 * pr + hl] ** 128
    c["c_g128"] = g128.reshape(128, 512).astype(np.float32)
    ep = np.zeros((128, 8), np.float64)
    for h in range(8):
        ep[:, h] = 64.0 * EPS / gam[h] ** (2.0 * (s + 1.0))
    c["c_reps"] = ep.astype(np.float32)
    mb = np.where(s[:, None] <= s[None, :], 0.0, -30000.0).astype(np.float32)
    c["c_maskb"] = mb.astype(ml_dtypes.bfloat16)
    c["c_iota"] = np.tile(np.arange(128, dtype=np.float32)[None, :], (128, 1))
    perm = np.zeros((128, 128), np.float32)
    for mcol in range(128):
        perm[(mcol + 64) % 128, mcol] = 1.0
    c["c_perm"] = perm
    rm = np.zeros((128, 8), np.float32)
    for p in range(128):
        rm[p, p // 16] = 1.0
    c["c_rowmask"] = rm
    sg = np.zeros((128, 8), np.float32)
    top = (np.arange(128) < 64)
    sg[:, 0] = np.where(top, -1.0, 1.0); sg[:, 1] = np.where(top, PI, -PI)
    sg[:, 2] = np.where(top, 1.0, -1.0); sg[:, 3] = np.where(top, -PI, PI)
    sg[:, 4] = np.where(top, 0.5, -0.5); sg[:, 5] = np.where(top, -0.5, 0.5); sg[:, 6] = PI
    c["c_sgn"] = sg
    return c


WEIGHT_NAMES = ["norm_mix", "norm_ffn", "norm_ple", "ret_s5_w_in", "ret_s5_w_out", "s5_lambda_re", "s5_lambda_im",
                "s5_b_re", "s5_b_im", "s5_c_re", "s5_c_im", "s5_d", "s5_log_step", "s5_w_glu", "diff_w_qkv",
                "diff_w_o", "diff_lambda_q1", "diff_lambda_k1", "diff_lambda_q2", "diff_lambda_k2", "diff_subln",
                "ffn_w_gate", "ffn_w_up", "ffn_w_down", "ple_w_proj", "ple_w_gate", "final_norm"]
WEIGHT_SHAPES = {
    "norm_mix": [2, 1024], "norm_ffn": [2, 1024], "norm_ple": [2, 1024], "ret_s5_w_in": [1, 1024, 2560],
    "ret_s5_w_out": [1, 1024, 1024], "s5_lambda_re": [1, 32, 64], "s5_lambda_im": [1, 32, 64],
    "s5_b_re": [1, 32, 64, 16], "s5_b_im": [1, 32, 64, 16], "s5_c_re": [1, 32, 16, 64], "s5_c_im": [1, 32, 16, 64],
    "s5_d": [1, 32, 16], "s5_log_step": [1, 32], "s5_w_glu": [1, 512, 512], "diff_w_qkv": [1, 1024, 3072],
    "diff_w_o": [1, 1024, 1024], "diff_lambda_q1": [1, 64], "diff_lambda_k1": [1, 64], "diff_lambda_q2": [1, 64],
    "diff_lambda_k2": [1, 64], "diff_subln": [1, 128], "ffn_w_gate": [2, 1024, 2816], "ffn_w_up": [2, 1024, 2816],
    "ffn_w_down": [2, 2816, 1024], "ple_w_proj": [2, 256, 1024], "ple_w_gate": [2, 1024, 1024], "final_norm": [1024],
}


class Builder:
    def __init__(self, T, debug=()):
        self.T = T
        self.NB = T // 512
        self.debug = debug
        nc = bass.Bass("TRN2", target_bir_lowering=False)
        self.nc = nc
        self.S = Sched(nc)
        self.din = {}
        self.din["x"] = nc.dram_tensor("x", [T, D], F32, kind="ExternalInput").ap()
        self.din["p"] = nc.dram_tensor("p", [2, T, PLE], F32, kind="ExternalInput").ap()
        for n in WEIGHT_NAMES:
            self.din[n] = nc.dram_tensor(n, WEIGHT_SHAPES[n], F32, kind="ExternalInput").ap()
        hc = host_consts(T)
        for n, a in hc.items():
            dt = BF16 if a.dtype == ml_dtypes.bfloat16 else F32
            self.din[n] = nc.dram_tensor(n, list(a.shape), dt, kind="ExternalInput").ap()
        self.y = nc.dram_tensor("y", [T, D], F32, kind="ExternalOutput").ap()
        sk = "ExternalOutput" if debug else "Internal"
        self.h1 = nc.dram_tensor("h1s", [T, D], F32, kind=sk).ap()
        self.mixT = nc.dram_tensor("mixTs", [D, T], BF16, kind=sk).ap()
        self.qkT = nc.dram_tensor("qkTs", [2048, T], BF16, kind=sk).ap()
        self.vS = nc.dram_tensor("vs", [T, D], BF16, kind=sk).ap()
        self.wbf = {}
        for l in range(2):
            for nm, shp in [("wo", [1024, 1024]), ("wg", [1024, FF]), ("wu", [1024, FF]), ("wd", [FF, 1024]),
                            ("wpg", [1024, 1024]), ("wpp", [PLE, 1024])]:
                self.wbf[(nm, l)] = nc.dram_tensor("%s_bf%d" % (nm, l), shp, BF16, kind="Internal").ap()
        self.dbg = {}
        self.AW = 52600
        self.arena = nc.alloc_sbuf_tensor("arena", [128, self.AW], F32)
        self.top = 0
        self.pb = [nc.alloc_psum_tensor("pb%d" % i, [128, 512], F32) for i in range(8)]
        self.pbi = 0
        self.conv_pending = {}
        self.conv_count = {}
        self.reserved = set()
        self.final = []

    def conv_list(self, l):
        din = self.din
        srcs = {"wo": (din["ret_s5_w_out"] if l == 0 else din["diff_w_o"])[0], "wg": din["ffn_w_gate"][l],
                "wu": din["ffn_w_up"][l], "wd": din["ffn_w_down"][l], "wpg": din["ple_w_gate"][l], "wpp": din["ple_w_proj"][l]}
        fns = []
        for nm, src in srcs.items():
            dst = self.wbf[(nm, l)]
            rows = dst.shape[0]
            step = 256
            for r0 in range(0, rows, step):
                def f(chain=None, dst=dst, src=src, r0=r0, step=step, nm=nm):
                    wk = [("wbf", l, "chain", chain)] if chain is not None else [("wbf", nm, l, r0)]
                    self.DMA("pool", dst[r0:r0 + step, :], src[r0:r0 + step, :], [], wk, semkey=("wbf", l, chain), bg=True)
                fns.append(f)
        return fns

    def convert_weights(self, l, n=None, chain_depth=None):
        if l not in self.conv_pending:
            self.conv_pending[l] = self.conv_list(l)
            self.conv_count[l] = 0
        lst = self.conv_pending[l]
        k = len(lst) if n is None else min(n, len(lst))
        for _ in range(k):
            f = lst.pop(0)
            if chain_depth:
                f(chain=self.conv_count[l] % chain_depth)
            else:
                f()
            self.conv_count[l] += 1

    def alloc(self, shape, dt):
        n = int(np.prod(shape))
        words = n if dt in (F32, I32) else (n + 1) // 2
        assert self.top + words <= self.AW, ("SBUF arena overflow", self.top, words)
        v = self.arena[:, self.top:self.top + words]
        self.top += words
        if dt != F32:
            v = v.bitcast(dt)
            if dt == BF16:
                v = v[:, 0:n]
        if len(shape) == 2:
            return v.rearrange("p (a b) -> p a b", a=shape[0])
        if len(shape) == 3:
            return v.rearrange("p (a b c) -> p a b c", a=shape[0], b=shape[1])
        return v

    def nb(self, reserve=False):
        while True:
            b = self.pbi
            self.pbi = (self.pbi + 1) % 8
            if b not in self.reserved:
                break
        if reserve:
            self.reserved.add(b)
        return b

    def MM(self, out, lhsT, rhs, start, stop, r, w):
        self.S.op("pe", lambda e: e.matmul(out, lhsT, rhs, start=start, stop=stop), reads=r, writes=w)

    def TR(self, out, in_, ident, r, w):
        self.S.op("pe", lambda e: e.transpose(out=out, in_=in_, identity=ident), reads=r, writes=w)

    def ACT(self, out, in_, func, r, w, scale=1.0, bias=None, accum=None):
        def f(e):
            kw = dict(out=out, in_=in_, func=func, scale=scale)
            if bias is not None:
                kw["bias"] = bias
            if accum is not None:
                kw["accum_out"] = accum
            return e.activation(**kw)
        self.S.op("act", f, reads=r, writes=w)

    def TT(self, eng, out, in0, in1, op, r, w):
        self.S.op(eng, lambda e: e.tensor_tensor(out=out, in0=in0, in1=in1, op=op), reads=r, writes=w)

    def TS(self, eng, out, in0, s1, s2, op0, op1, r, w):
        if op1 is None:
            self.S.op(eng, lambda e: e.tensor_scalar(out=out, in0=in0, scalar1=s1, scalar2=None, op0=op0), reads=r, writes=w)
        else:
            self.S.op(eng, lambda e: e.tensor_scalar(out=out, in0=in0, scalar1=s1, scalar2=s2, op0=op0, op1=op1), reads=r, writes=w)

    def STT(self, eng, out, in0, scalar, in1, op0, op1, r, w):
        self.S.op(eng, lambda e: e.scalar_tensor_tensor(out=out, in0=in0, scalar=scalar, in1=in1, op0=op0, op1=op1), reads=r, writes=w)

    def CP(self, eng, out, in_, r, w):
        self.S.op(eng, lambda e: e.tensor_copy(out=out, in_=in_), reads=r, writes=w)

    def MS(self, eng, out, val, w):
        self.S.op(eng, lambda e: e.memset(out, val), writes=w)

    def DMA(self, eng, out, in_, r, w, semkey=None, slow=False, bg=False):
        if bg:
            return self.S.op(eng, lambda e: e.dma_start(out=out, in_=in_), reads=r, writes=w, dma=True, semkey=semkey, bg=True)
        if slow:
            return self.S.op(eng, lambda e: e.dma_start(out=out, in_=in_, allow_slow_non_contiguous=True), reads=r, writes=w, dma=True, semkey=semkey)
        return self.S.op(eng, lambda e: e.dma_start(out=out, in_=in_), reads=r, writes=w, dma=True, semkey=semkey)

    def rsqrt(self, eng, out, a, k, key_a, key_out, tag):
        ti = self.tmp_i[:, 0:k]
        y = self.tmp_y[:, 0:k]
        ta = self.tmp_a[:, 0:k]
        tb = self.tmp_b[:, 0:k]
        yf = y.bitcast(F32)
        kk = ("rsq", tag)
        self.TS(eng, ti, a.bitcast(I32), 1, None, ALU.arith_shift_right, None, [key_a], [kk + ("ti",)])
        self.TS(eng, y, ti, -1, 1597463007, ALU.mult, ALU.add, [kk + ("ti",)], [kk + ("y",)])
        for _ in range(3):
            self.TT(eng, ta, yf, yf, ALU.mult, [kk + ("y",)], [kk + ("ta",)])
            self.TT(eng, tb, ta, a, ALU.mult, [kk + ("ta",), key_a], [kk + ("tb",)])
            self.TS(eng, ta, tb, -0.5, 1.5, ALU.mult, ALU.add, [kk + ("tb",)], [kk + ("ta",)])
            self.TT(eng, yf, yf, ta, ALU.mult, [kk + ("y",), kk + ("ta",)], [kk + ("y",)])
        self.CP(eng, out, yf, [kk + ("y",)], [key_out])

    def build(self):
        S = self.S
        din = self.din
        T, NB = self.T, self.NB
        self.identb = self.alloc([128], BF16)
        self.identf = self.alloc([128], F32)
        self.gT = self.alloc([7, 8], F32)
        self.sgn = self.alloc([8], F32)
        self.lam = self.alloc([4], F32)
        self.tmp_i = self.alloc([32], I32)
        self.tmp_y = self.alloc([32], I32)
        self.tmp_a = self.alloc([32], F32)
        self.tmp_b = self.alloc([32], F32)
        self.DMA("sp", self.identb, din["c_identb"], [], ["identb"])
        self.DMA("sp", self.identf, din["c_identf"], [], ["identf"])
        self.DMA("sp", self.sgn, din["c_sgn"], [], ["sgn"])
        for l in range(2):
            for j, nm in enumerate(["norm_mix", "norm_ffn", "norm_ple"]):
                self.DMA("sp", self.gT[:, 3 * l + j, :], din[nm][l].rearrange("(kc p) -> p kc", p=128), [], ["gT"], slow=True)
        self.TS("dve", self.gT[:, 0:6, :], self.gT[:, 0:6, :], 32.0, None, ALU.mult, None, ["gT"], ["gT"])
        lamt = self.alloc([4, 64], F32)
        for j, nm in enumerate(["diff_lambda_q1", "diff_lambda_k1", "diff_lambda_q2", "diff_lambda_k2"]):
            self.DMA("sp", lamt[:, j, :], din[nm][0:1, :].to_broadcast([128, 64]), [], ["lamt"], slow=True)
        lam2 = self.alloc([2, 64], F32)
        lsum = self.alloc([2], F32)
        self.TT("dve", lam2[:, 0, :], lamt[:, 0, :], lamt[:, 1, :], ALU.mult, ["lamt"], ["lam2"])
        self.TT("dve", lam2[:, 1, :], lamt[:, 2, :], lamt[:, 3, :], ALU.mult, ["lamt"], ["lam2"])
        S.op("dve", lambda e: e.tensor_reduce(out=lsum, in_=lam2, axis=AX.X, op=ALU.add), reads=["lam2"], writes=["lsum"])
        lexp = self.alloc([2], F32)
        self.ACT(lexp, lsum, AF.Exp, ["lsum"], ["lexp"])
        lambda_init = 0.8 - 0.6 * math.exp(-0.3 * 1)
        self.TT("dve", self.lam[:, 0:1], lexp[:, 1:2], lexp[:, 0:1], ALU.subtract, ["lexp"], ["lam"])
        self.TS("dve", self.lam[:, 0:1], self.lam[:, 0:1], -lambda_init, None, ALU.add, None, ["lam"], ["lam"])
        self.DMA("sp", self.lam[:, 1:2], din["diff_subln"][0].rearrange("(e o) -> e o", o=1), [], ["lam1"], slow=True)
        self.TS("dve", self.lam[:, 1:2], self.lam[:, 1:2], (1.0 - lambda_init) * math.sqrt(128.0), None, ALU.mult, None, ["lam1"], ["lam1"])
        self.lambda_init = lambda_init
        self.base_top = self.top
        stages = self.debug if self.debug else ("l0ab", "l0c", "l1a", "l1b", "l1c")
        if "l0ab" not in stages:
            self.convert_weights(0, chain_depth=4)
        if "l1b" not in stages:
            self.convert_weights(1, chain_depth=4)
        if "l0ab" in stages:
            self.layer0_ab()
        S.barrier(include_bg=True)
        if "l0c" in stages:
            self.top = self.base_top
            self.phase_c(0)
            S.barrier()
        if "l1a" in stages:
            self.top = self.base_top
            self.layer1_a()
            S.barrier()
        if "l1b" in stages:
            self.top = self.base_top
            self.layer1_b()
        S.barrier(include_bg=True)
        if "l1c" in stages:
            self.top = self.base_top
            self.phase_c(1)
        S.emit(final_wait=self.final)
        return self.nc

    def norm_block(self, src_tiles, src_keys, gidx, hnT, hnT_key, tagp, toff=0):
        n = len(src_tiles)
        ss = self.n_ss
        self.MS("dve", ss[:, 0:n], 0.0, [(tagp, "ss")])
        for tt in range(n):
            self.ACT(self.n_junk, src_tiles[tt], AF.Square, [src_keys[tt], (tagp, "ss")], [(tagp, "ss"), "n_junk"], accum=ss[:, tt:tt + 1])
        self.TS("dve", ss[:, 0:n], ss[:, 0:n], D * EPS, None, ALU.add, None, [(tagp, "ss")], [(tagp, "ss")])
        self.rsqrt("dve", self.n_rs[:, 0:n], ss[:, 0:n], n, (tagp, "ss"), (tagp, "rs"), tagp)
        for tt in range(n):
            sl = tt % 2
            xn = self.n_xn[:, sl, :]
            self.ACT(xn, src_tiles[tt], AF.Copy, [src_keys[tt], (tagp, "rs")], [("n_xn", sl)], scale=self.n_rs[:, tt:tt + 1])
            b = self.nb()
            pbb = self.pb[b][:].bitcast(BF16)
            for kc in range(8):
                self.TR(pbb[:, kc * 128:(kc + 1) * 128], xn[:, kc * 128:(kc + 1) * 128], self.identb, [("n_xn", sl), "identb"], [("pb", b)])
            t_ = toff + tt
            self.TT("dve", hnT[:, :, t_ * 128:(t_ + 1) * 128], pbb[:, 0:1024].rearrange("p (a b) -> p a b", a=8),
                    self.gT[:, gidx, :].unsqueeze(2).to_broadcast([128, 8, 128]), ALU.mult, [("pb", b), "gT"], [hnT_key])

    def alloc_norm(self):
        self.n_ss = self.alloc([4], F32)
        self.n_rs = self.alloc([4], F32)
        self.n_junk = self.alloc([1024], BF16)
        self.n_xn = self.alloc([2, 1024], BF16)

    def phase_c(self, l):
        din = self.din
        NB = self.NB
        wo_src = self.wbf[("wo", l)].rearrange("(kc p) f -> p kc f", p=128)
        wg_src = self.wbf[("wg", l)].rearrange("(kc p) f -> p kc f", p=128)
        wu_src = self.wbf[("wu", l)].rearrange("(kc p) f -> p kc f", p=128)
        wd_src = self.wbf[("wd", l)].rearrange("(fc p) f -> p fc f", p=128)
        wpg_src = self.wbf[("wpg", l)].rearrange("(kc p) f -> p kc f", p=128)
        wpp_src = self.wbf[("wpp", l)].rearrange("(kc p) f -> p kc f", p=128)
        hsrc = din["x"] if l == 0 else self.h1
        self.alloc_norm()
        if l == 1:
            self.fing = self.alloc([1024], F32)
            self.DMA("sp", self.fing, din["final_norm"].unsqueeze(0).to_broadcast([128, 1024]), [], ["fing"], slow=True)
            self.TS("dve", self.fing, self.fing, 32.0, None, ALU.mult, None, ["fing"], ["fing"])
        Wo = self.alloc([8, 1024], BF16)
        Wg = [self.alloc([8, 512], BF16) for _ in range(2)]
        Wu = [self.alloc([8, 512], BF16) for _ in range(2)]
        Wd = [self.alloc([NFC, 256], BF16) for _ in range(2)]
        Wpg = self.alloc([8, 1024], BF16)
        Wpp = self.alloc([2, 1024], BF16)
        mixb = [self.alloc([8, 512], BF16)]
        hm2 = [self.alloc([4, 1024], F32) for _ in range(2)]
        hnT = self.alloc([8, 512], BF16)
        actT = self.alloc([NFC, 512], BF16)
        sg = [self.alloc([512], BF16) for _ in range(2)]
        th = [self.alloc([512], F32) for _ in range(2)]
        pbf = [self.alloc([256], BF16) for _ in range(2)]
        pT = [self.alloc([2, 128], BF16) for _ in range(2)]
        tg = "c%d" % l
        groups = [(0, 512), (512, 512), (1024, 512), (1536, 512), (2048, 512), (2560, 256)]
        gi = 0
        for tb in range(NB):
            t0 = tb * 512
            ms = 0
            hsl = tb % 2
            hm = hm2[hsl]
            def load_block(tbx):
                tx = tbx * 512
                sx = tbx % 2
                self.DMA("sp", mixb[ms], self.mixT.rearrange("(kc p) t -> p kc t", p=128)[:, :, tx:tx + 512], [], [("mixb", ms)])
                for tt in range(4):
                    self.DMA("sp", hm2[sx][:, tt, :], hsrc[tx + tt * 128:tx + (tt + 1) * 128, :], [], [("hm", sx, tt)])
            if tb == 0:
                load_block(0)
            self.DMA("pool", Wo, wo_src, [], ["Wo"])
            for tt in range(4):
                for half in range(2):
                    b = self.nb()
                    for kc in range(8):
                        self.MM(self.pb[b][:], mixb[ms][:, kc, tt * 128:(tt + 1) * 128], Wo[:, kc, half * 512:(half + 1) * 512],
                                kc == 0, kc == 7, [("mixb", ms), "Wo"], [("pb", b)])
                    hv = hm[:, tt, half * 512:(half + 1) * 512]
                    self.TT("dve", hv, hv, self.pb[b][:], ALU.add, [("pb", b), ("hm", hsl, tt)], [("hm", hsl, tt)])
            self.norm_block([hm[:, tt, :] for tt in range(4)], [("hm", hsl, tt) for tt in range(4)], 3 * l + 1, hnT, "hnT", tg + "n1")
            for (f0, fw) in groups:
                s = gi % 2
                gi += 1
                self.DMA("pool", Wg[s][:, :, 0:fw], wg_src[:, :, f0:f0 + fw], [], [("Wg", s)])
                self.DMA("pool", Wu[s][:, :, 0:fw], wu_src[:, :, f0:f0 + fw], [], [("Wu", s)])
                for fl in range(fw // 128):
                    fc = f0 // 128 + fl
                    bg = self.nb()
                    for kc in range(8):
                        self.MM(self.pb[bg][:], Wg[s][:, kc, fl * 128:(fl + 1) * 128], hnT[:, kc, :], kc == 0, kc == 7,
                                [("Wg", s), "hnT"], [("pb", bg)])
                    bu = self.nb()
                    for kc in range(8):
                        self.MM(self.pb[bu][:], Wu[s][:, kc, fl * 128:(fl + 1) * 128], hnT[:, kc, :], kc == 0, kc == 7,
                                [("Wu", s), "hnT"], [("pb", bu)])
                    ss_ = fc % 2
                    self.ACT(sg[ss_], self.pb[bg][:], AF.Silu, [("pb", bg)], [("sg", ss_)])
                    self.TT("dve", actT[:, fc, :], sg[ss_], self.pb[bu][:], ALU.mult, [("sg", ss_), ("pb", bu)], [("actT", fc)])
            if tb + 1 < NB:
                load_block(tb + 1)
            for q in range(4):
                self.DMA("pool", Wd[q % 2], wd_src[:, :, q * 256:(q + 1) * 256], [], [("Wd", q % 2)])
                for tt in range(4):
                    b = self.nb()
                    for fc in range(NFC):
                        self.MM(self.pb[b][:, 0:256], actT[:, fc, tt * 128:(tt + 1) * 128], Wd[q % 2][:, fc, :], fc == 0, fc == NFC - 1,
                                [("actT", fc), ("Wd", q % 2)], [("pb", b)])
                    hv = hm[:, tt, q * 256:(q + 1) * 256]
                    self.TT("dve", hv, hv, self.pb[b][:, 0:256], ALU.add, [("pb", b), ("hm", hsl, tt)], [("hm", hsl, tt)])
            self.DMA("pool", Wpg, wpg_src, [], ["Wpg"])
            self.DMA("pool", Wpp, wpp_src, [], ["Wpp"])
            self.norm_block([hm[:, tt, :] for tt in range(4)], [("hm", hsl, tt) for tt in range(4)], 3 * l + 2, hnT, "hnT", tg + "n2")
            for tt in range(4):
                s = tt % 2
                self.DMA("pool", pbf[s], din["p"][l, t0 + tt * 128:t0 + (tt + 1) * 128, :], [], [("pbf", s)])
                b = self.nb()
                pbb = self.pb[b][:].bitcast(BF16)
                for pc in range(2):
                    self.TR(pbb[:, pc * 128:(pc + 1) * 128], pbf[s][:, pc * 128:(pc + 1) * 128], self.identb, [("pbf", s), "identb"], [("pb", b)])
                self.ACT(pT[s], pbb[:, 0:256].rearrange("p (a b) -> p a b", a=2), AF.Copy, [("pb", b)], [("pT", s)])
                for half in range(2):
                    ba = self.nb()
                    for pc in range(2):
                        self.MM(self.pb[ba][:], pT[s][:, pc, :], Wpp[:, pc, half * 512:(half + 1) * 512], pc == 0, pc == 1,
                                [("pT", s), "Wpp"], [("pb", ba)])
                    bz = self.nb()
                    for kc in range(8):
                        self.MM(self.pb[bz][:], hnT[:, kc, tt * 128:(tt + 1) * 128], Wpg[:, kc, half * 512:(half + 1) * 512], kc == 0, kc == 7,
                                ["hnT", "Wpg"], [("pb", bz)])
                    self.ACT(th[half], self.pb[bz][:], AF.Tanh, [("pb", bz)], [("th", half)], scale=0.5)
                    self.STT("dve", th[half], th[half], 1.0, self.pb[ba][:], ALU.add, ALU.mult, [("th", half), ("pb", ba)], [("th", half)])
                    hv = hm[:, tt, half * 512:(half + 1) * 512]
                    self.STT("dve", hv, th[half], 0.5, hv, ALU.mult, ALU.add, [("th", half), ("hm", hsl, tt)], [("hm", hsl, tt)])
            if l == 0:
                for tt in range(4):
                    self.DMA("sp", self.h1[t0 + tt * 128:t0 + (tt + 1) * 128, :], hm[:, tt, :], [("hm", hsl, tt)], [], semkey=("hm", hsl, tt))
            else:
                ss = self.n_ss
                self.MS("dve", ss[:, 0:4], 0.0, [(tg, "fss")])
                for tt in range(4):
                    self.ACT(self.n_junk, hm[:, tt, :], AF.Square, [("hm", hsl, tt), (tg, "fss")], [(tg, "fss"), "n_junk"], accum=ss[:, tt:tt + 1])
                self.TS("dve", ss[:, 0:4], ss[:, 0:4], D * EPS, None, ALU.add, None, [(tg, "fss")], [(tg, "fss")])
                self.rsqrt("dve", self.n_rs[:, 0:4], ss[:, 0:4], 4, (tg, "fss"), (tg, "frs"), tg + "f")
                for tt in range(4):
                    self.STT("dve", hm[:, tt, :], hm[:, tt, :], self.n_rs[:, tt:tt + 1], self.fing, ALU.mult, ALU.mult,
                             [("hm", hsl, tt), (tg, "frs"), "fing"], [("hm", hsl, tt)])
                    i = self.DMA("sp", self.y[t0 + tt * 128:t0 + (tt + 1) * 128, :], hm[:, tt, :], [("hm", hsl, tt)], [], semkey=("hm", hsl, tt))
                    self.final.append(i)

    def layer1_a(self):
        din = self.din
        NB = self.NB
        self.alloc_norm()
        wsrc = din["diff_w_qkv"][0].rearrange("(kc p) f -> p kc f", p=128)
        W = self.alloc([8, 3072], BF16)
        Wsw = self.alloc([8, 2048], BF16)
        for kc in range(8):
            self.DMA("pool", W[:, kc, :], wsrc[:, kc, :], [], [("W1", kc)])
        for kc in range(8):
            wv = W[:, kc, 0:2048].rearrange("p (m h j) -> p m h j", m=32, h=2)
            sv = Wsw[:, kc, :].rearrange("p (m h j) -> p m h j", m=32, h=2)
            self.CP("dve" if kc % 2 == 0 else "pool", sv[:, :, 0, :], wv[:, :, 1, :], [("W1", kc)], [("Wsw", kc)])
            self.CP("dve" if kc % 2 == 0 else "pool", sv[:, :, 1, :], wv[:, :, 0, :], [("W1", kc)], [("Wsw", kc)])
        Wk = [("W1", kc) for kc in range(8)]
        Wswk = [("Wsw", kc) for kc in range(8)]
        hx = self.alloc([4, 1024], F32)
        hnT = self.alloc([8, 512], BF16)
        rope = [self.alloc([2, 512], F32) for _ in range(2)]
        ta = [self.alloc([512], F32) for _ in range(2)]
        tb_ = [self.alloc([512], F32) for _ in range(2)]
        qk = [self.alloc([512], BF16) for _ in range(2)]
        vb = [self.alloc([512], BF16) for _ in range(2)]
        cnt = 0
        for tb in range(NB):
            t0 = tb * 512
            rs = tb % 2
            for tt in range(4):
                self.DMA("sp", hx[:, tt, :], self.h1[t0 + tt * 128:t0 + (tt + 1) * 128, :], [], [("hx", tt)])
            self.DMA("sp", rope[rs], din["c_rope"].rearrange("c p t -> p c t")[:, :, t0:t0 + 512], [], [("rope", rs)])
            self.norm_block([hx[:, tt, :] for tt in range(4)], [("hx", tt) for tt in range(4)], 3, hnT, "hnT", "l1an")
            for ft in range(16):
                s = cnt % 2
                cnt += 1
                ba = self.nb()
                for kc in range(8):
                    self.MM(self.pb[ba][:], W[:, kc, ft * 128:(ft + 1) * 128], hnT[:, kc, :], kc == 0, kc == 7, ["hnT", Wk[kc]], [("pb", ba)])
                bs = self.nb()
                for kc in range(8):
                    self.MM(self.pb[bs][:], Wsw[:, kc, ft * 128:(ft + 1) * 128], hnT[:, kc, :], kc == 0, kc == 7, ["hnT", Wswk[kc]], [("pb", bs)])
                self.TT("dve", ta[s], self.pb[ba][:], rope[rs][:, 0, :], ALU.mult, [("pb", ba), ("rope", rs)], [("ta", s)])
                self.TT("dve", tb_[s], self.pb[bs][:], rope[rs][:, 1, :], ALU.mult, [("pb", bs), ("rope", rs)], [("tb", s)])
                self.TT("pool", qk[s], ta[s], tb_[s], ALU.add, [("ta", s), ("tb", s)], [("qk", s)])
                self.DMA("sp", self.qkT[ft * 128:(ft + 1) * 128, t0:t0 + 512], qk[s], [("qk", s)], [], semkey=("qk", s))
            for tt in range(4):
                for half in range(2):
                    s = cnt % 2
                    cnt += 1
                    b = self.nb()
                    for kc in range(8):
                        self.MM(self.pb[b][:], hnT[:, kc, tt * 128:(tt + 1) * 128], W[:, kc, 2048 + half * 512:2048 + (half + 1) * 512],
                                kc == 0, kc == 7, ["hnT", Wk[kc]], [("pb", b)])
                    self.ACT(vb[s], self.pb[b][:], AF.Copy, [("pb", b)], [("vb", s)])
                    self.DMA("sp", self.vS[t0 + tt * 128:t0 + (tt + 1) * 128, half * 512:(half + 1) * 512], vb[s], [("vb", s)], [], semkey=("vb", s))

    def layer1_b(self):
        din = self.din
        S = self.S
        T = self.T
        if os.environ.get('NOCONV') != '1':
            self.convert_weights(1, chain_depth=4)
        NQB = T // 512
        NT = T // 128
        maskb = self.alloc([128], BF16)
        self.DMA("sp", maskb, din["c_maskb"], [], ["maskb"])
        QT = [self.alloc([T], BF16) for _ in range(2)]
        KT = [self.alloc([T], BF16) for _ in range(2)]
        V = [self.alloc([NT, 130], BF16) for _ in range(2)]
        for s in range(2):
            self.MS("dve", V[s][:, :, 128:130], 1.0, [("Vone", s)])
        PT = [self.alloc([512], BF16) for _ in range(8)]
        O1 = self.alloc([4, 128], F32)
        att4 = self.alloc([4, 128], F32)
        attb = self.alloc([4, 128], BF16)
        junk = self.alloc([128], BF16)
        rsm = self.alloc([8], F32)
        ssq = self.alloc([4], F32)
        rsq = self.alloc([4], F32)
        aT = [self.alloc([512], BF16) for _ in range(2)]
        st = {"pti": 0, "blk": 0}

        def load_head(h):
            hs = h % 2
            self.DMA("sp", QT[hs], self.qkT[h * 128:(h + 1) * 128, :], [], [("QT", hs)])
            self.DMA("sp", KT[hs], self.qkT[1024 + h * 128:1024 + (h + 1) * 128, :], [], [("KT", hs)])
            self.DMA("sp", V[hs][:, :, 0:128], self.vS[:, h * 128:(h + 1) * 128].rearrange("(n p) e -> p n e", p=128),
                     [("Vone", hs)], [("V", hs)])

        pairs = [(h, qb, kt) for h in range(8) for qb in range(NQB) for kt in range(4 * qb + 4)]
        info = {}
        obs = {}
        ACC = [(0, 0), (0, 136), (0, 272), (1, 0), (1, 136), (1, 272), (2, 0), (2, 136)]

        def stage_a(pr):
            h, qb, kt = pr
            hs = h % 2
            if h == 0 and qb == 0 and kt == 0:
                load_head(0)
            q0 = qb * 512
            dk = kt - 4 * qb
            qlo = max(0, dk) * 128
            bss = [self.nb(), self.nb()]
            for rep in range(int(os.environ.get('DUP', '1'))):
              for m in range(2):
                rows = slice(m * 64, (m + 1) * 64)
                self.MM(self.pb[bss[m]][:, qlo:512], KT[hs][rows, kt * 128:(kt + 1) * 128], QT[hs][rows, q0 + qlo:q0 + 512], True, dk < 0,
                        [("KT", hs), ("QT", hs)], [("pb", bss[m])])
            pts = []
            for m in range(2):
                ps = self.pb[bss[m]]
                if dk >= 0:
                    self.MM(ps[:, qlo:qlo + 128], self.identb, maskb, False, True, ["identb", "maskb"], [("pb", bss[m])])
                pi = st["pti"] % len(PT)
                st["pti"] += 1
                pt = PT[pi]
                ptk = ("PT", pi)
                self.ACT(pt[:, qlo:512], ps[:, qlo:512], AF.Exp, [("pb", bss[m])], [ptk], scale=0.125)
                pts.append((pt, ptk))
            info[pr] = (pts, dk)

        def stage_c(pr):
            h, qb, kt = pr
            hs = h % 2
            q0 = qb * 512
            pts, dk = info.pop(pr)
            if qb == 0 and kt == 0 and h + 1 < 8:
                load_head(h + 1)
            if kt == 0:
                obs[(h, qb)] = [self.nb(reserve=True), self.nb(reserve=True), self.nb(reserve=True)]
            ob = obs[(h, qb)]

            def acc(m, qt):
                bi, col = ACC[m * 4 + qt]
                return self.pb[ob[bi]][:, col:col + 129], ("pb", ob[bi]), (kt == 0 and col == 0 and (m * 4 + qt) in (0, 3, 6))

            for m in range(2):
                pt, ptk = pts[m]
                for qt in range(max(0, dk), 4):
                    o, okey, first = acc(m, qt)
                    last = (kt == 4 * qb + qt) and (m * 4 + qt) in (2, 3, 7)
                    self.MM(o, pt[:, qt * 128:(qt + 1) * 128], V[hs][:, kt, 0:129], first, last, [ptk, ("V", hs)], [okey])
            if kt != 4 * qb + 3:
                return
            for m in range(2):
                for qt in range(4):
                    o, okey, _ = acc(m, qt)
                    rc = rsm[:, m * 4 + qt:m * 4 + qt + 1]
                    rk = ("rsm", m * 4 + qt)
                    S.op("dve", lambda e, rc=rc, o=o: e.reciprocal(out=rc, in_=o[:, 128:129]), reads=[okey], writes=[rk])
                    if m == 0:
                        self.TS("dve", O1[:, qt, :], o[:, 0:128], rc, None, ALU.mult, None, [okey, rk], [("O1", qt)])
                    else:
                        self.TT("dve", rc, rc, self.lam[:, 0:1], ALU.mult, [rk, "lam"], [rk])
                        self.STT("dve", att4[:, qt, :], o[:, 0:128], rc, O1[:, qt, :], ALU.mult, ALU.add,
                                 [okey, rk, ("O1", qt)], [("att", qt)])
                        S.op("dve", lambda e, qt=qt: e.scalar_tensor_tensor(out=junk, in0=att4[:, qt, :], scalar=1.0, in1=att4[:, qt, :],
                                                                           op0=ALU.mult, op1=ALU.mult, accum_out=ssq[:, qt:qt + 1]),
                             reads=[("att", qt)], writes=["ssq", "junkb"])
            for b_ in ob:
                self.reserved.discard(b_)
            del obs[(h, qb)]
            self.TS("dve", ssq, ssq, 128.0 * EPS, None, ALU.add, None, ["ssq"], ["ssq"])
            self.rsqrt("dve", rsq, ssq, 4, "ssq", "rsq", "l1b")
            for qt in range(4):
                self.TS("dve", attb[:, qt, :], att4[:, qt, :], rsq[:, qt:qt + 1], None, ALU.mult, None, [("att", qt), "rsq"], [("attb", qt)])

            def part2(h=h, q0=q0):
                bt = self.nb()
                pbb = self.pb[bt][:].bitcast(BF16)
                for qt in range(4):
                    self.TR(pbb[:, qt * 128:(qt + 1) * 128], attb[:, qt, :], self.identb, [("attb", qt), "identb"], [("pb", bt)])
                a = aT[st["blk"] % 2]
                ak = ("aT", st["blk"] % 2)
                st["blk"] += 1
                self.ACT(a, pbb[:, 0:512], AF.Copy, [("pb", bt), "lam1"], [ak], scale=self.lam[:, 1:2])
                self.DMA("sp", self.mixT[h * 128:(h + 1) * 128, q0:q0 + 512], a, [ak], [], semkey=ak)
            deferred.append([DEFER, part2])

        deferred = []
        DEFER = int(os.environ.get('DEFER', '2'))
        LOOK = int(os.environ.get('LOOK', '2'))
        n = len(pairs)
        for i in range(n + LOOK):
            if i < n:
                stage_a(pairs[i])
            if i - LOOK >= 0:
                stage_c(pairs[i - LOOK])
            for dfr in list(deferred):
                dfr[0] -= 1
                if dfr[0] <= 0:
                    dfr[1]()
                    deferred.remove(dfr)
        for dfr in deferred:
            dfr[1]()

    def layer0_ab(self):
        din = self.din
        S = self.S
        NB = self.NB
        self.alloc_norm()
        sgn = self.sgn
        wsrc = din["ret_s5_w_in"][0].rearrange("(kc p) f -> p kc f", p=128)
        W = self.alloc([8, 2560], BF16)
        Wsw = self.alloc([8, 1024], BF16)
        for kc in range(8):
            self.DMA("pool", W[:, kc, :], wsrc[:, kc, :], [], [("W0", kc)])
        for kc in range(8):
            wv = W[:, kc, 0:1024].rearrange("p (m h j) -> p m h j", m=16, h=2)
            sv = Wsw[:, kc, :].rearrange("p (m h j) -> p m h j", m=16, h=2)
            self.CP("dve", sv[:, :, 0, :], wv[:, :, 1, :], [("W0", kc)], [("W0sw", kc)])
            self.CP("dve", sv[:, :, 1, :], wv[:, :, 0, :], [("W0", kc)], [("W0sw", kc)])
        Wk = [("W0", kc) for kc in range(8)]
        Wswk = [("W0sw", kc) for kc in range(8)]
        Wglu = self.alloc([4, 512], BF16)
        self.DMA("pool", Wglu, din["s5_w_glu"][0].rearrange("(j p) f -> p j f", p=128), [], ["Wglu"])
        retmask = self.alloc([1024], F32)
        kdec = self.alloc([512], F32)
        g128 = self.alloc([512], F32)
        reps = self.alloc([8], F32)
        self.DMA("sp", retmask, din["c_retmask"], [], ["retmask"])
        self.DMA("sp", kdec, din["c_kdec"], [], ["kdec"])
        self.DMA("sp", g128, din["c_g128"], [], ["g128"])
        self.DMA("sp", reps, din["c_reps"], [], ["reps"])
        import os
        STOP = int(os.environ.get('L0STOP', '99'))
        RSTOP = int(os.environ.get('RSTOP', '99'))
        if STOP <= 1:
            return
        mark = self.top
        Bexp = None
        Bexp = self.alloc([32, 128], BF16)
        Bsw = self.alloc([32, 128], BF16)
        C1 = self.alloc([32, 16], BF16)
        C2 = self.alloc([32, 16], BF16)
        COS = self.alloc([32, 128], BF16)
        SINS = self.alloc([32, 128], BF16)
        rr = self.alloc([32], F32)
        Arot = self.alloc([32], F32)
        Brot = self.alloc([32], F32)
        Ddiag = self.alloc([4, 128], BF16)
        perm = self.alloc([128], F32)
        self.DMA("sp", perm, din["c_perm"], [], ["perm"])
        tmark = self.top
        LR = self.alloc([32], F32)
        LI = self.alloc([32], F32)
        DL = self.alloc([32], F32)
        for half in range(2):
            self.DMA("sp", LR[half * 64:(half + 1) * 64, :], din["s5_lambda_re"][0].rearrange("g p -> p g"), [], ["LR"], slow=True)
            self.DMA("sp", LI[half * 64:(half + 1) * 64, :], din["s5_lambda_im"][0].rearrange("g p -> p g"), [], ["LI"], slow=True)
        self.DMA("sp", DL, din["s5_log_step"][0:1, :].to_broadcast([128, 32]), [], ["DL"], slow=True)
        self.ACT(DL, DL, AF.Exp, ["DL"], ["DL"])
        th_ = self.alloc([32], F32)
        aa = self.alloc([32], F32)
        self.TT("dve", aa, LR, DL, ALU.mult, ["LR", "DL"], ["aa"])
        self.TT("dve", th_, LI, DL, ALU.mult, ["LI", "DL"], ["th"])
        self.ACT(rr, aa, AF.Exp, ["aa"], ["rr"])
        iota = self.alloc([128], F32)
        self.DMA("sp", iota, din["c_iota"], [], ["iota"])
        ang = self.alloc([32, 128], F32)
        ang2 = self.alloc([32, 128], F32)
        self.TT("dve", ang, th_.unsqueeze(2).to_broadcast([128, 32, 128]), iota.unsqueeze(1).to_broadcast([128, 32, 128]), ALU.mult,
                ["th", "iota"], ["ang"])
        angi = self.alloc([32, 128], I32)

        def sin_of(out, angle, n3, shift, scale, rk, wk):
            a2 = ang2 if n3 else ang2[:, 0, 0:32]
            ai = angi if n3 else angi[:, 0, 0:32]
            self.TS("dve", a2, angle, shift, 1.0 / (2 * PI), ALU.add, ALU.mult, rk + [wk], ["ang2"])
            self.CP("dve", ai, a2, ["ang2"], ["angi"])
            self.CP("dve", a2, ai, ["angi"], ["ang2"])
            self.STT("dve", a2, a2, -2 * PI, angle, ALU.mult, ALU.add, ["ang2"] + rk, ["ang2"])
            self.TS("dve", a2, a2, shift, None, ALU.add, None, ["ang2"], ["ang2"])
            self.TS("dve", a2, a2, -PI, PI, ALU.max, ALU.min, ["ang2"], ["ang2"])
            self.ACT(out, a2, AF.Sin, ["ang2", "sgn"], [wk], scale=scale)

        sin_of(COS, ang, True, PI / 2, 1.0, ["ang"], "COS")
        sin_of(SINS, ang, True, 0.0, sgn[:, 2:3], ["ang"], "SINS")
        a128 = self.alloc([32], F32)
        self.TS("dve", a128, th_, 128.0, None, ALU.mult, None, ["th"], ["a128"])
        sin_of(Arot, a128, False, PI / 2, 1.0, ["a128"], "Arot")
        sin_of(Brot, a128, False, 0.0, sgn[:, 0:1], ["a128"], "Brot")
        c1 = self.alloc([32], F32)
        s1 = self.alloc([32], F32)
        sin_of(c1, th_, False, PI / 2, 1.0, ["th"], "c1")
        sin_of(s1, th_, False, 0.0, 1.0, ["th"], "s1")
        if STOP <= 2:
            return
        nre = self.alloc([32], F32)
        nim = self.alloc([32], F32)
        den = self.alloc([32], F32)
        fre = self.alloc([32], F32)
        fim = self.alloc([32], F32)
        u1 = self.alloc([32], F32)
        self.TT("dve", nre, rr, c1, ALU.mult, ["rr", "c1"], ["nre"])
        self.TS("dve", nre, nre, -1.0, None, ALU.add, None, ["nre"], ["nre"])
        self.TT("dve", nim, rr, s1, ALU.mult, ["rr", "s1"], ["nim"])
        self.TT("dve", den, LR, LR, ALU.mult, ["LR"], ["den"])
        self.TT("dve", u1, LI, LI, ALU.mult, ["LI"], ["u1"])
        self.TT("dve", den, den, u1, ALU.add, ["den", "u1"], ["den"])
        S.op("dve", lambda e: e.reciprocal(out=den, in_=den), reads=["den"], writes=["den"])
        self.TT("dve", fre, nre, LR, ALU.mult, ["nre", "LR"], ["fre"])
        self.TT("dve", u1, nim, LI, ALU.mult, ["nim", "LI", "den"], ["u1"])
        self.TT("dve", fre, fre, u1, ALU.add, ["fre", "u1"], ["fre"])
        self.TT("dve", fre, fre, den, ALU.mult, ["fre", "den"], ["fre"])
        self.TT("dve", fim, nim, LR, ALU.mult, ["nim", "LR"], ["fim"])
        self.TT("dve", u1, nre, LI, ALU.mult, ["nre", "LI", "fre"], ["u1"])
        self.TT("dve", fim, fim, u1, ALU.subtract, ["fim", "u1"], ["fim"])
        self.TT("dve", fim, fim, den, ALU.mult, ["fim", "den"], ["fim"])
        Bre = self.alloc([32, 16], F32)
        Bim = self.alloc([32, 16], F32)
        self.DMA("sp", Bre[0:64], din["s5_b_re"][0].rearrange("g p c -> p g c"), [], ["Bre"], slow=True)
        self.DMA("sp", Bim[0:64], din["s5_b_im"][0].rearrange("g p c -> p g c"), [], ["Bim"], slow=True)
        Bbr = self.alloc([32, 16], F32)
        Bbi = self.alloc([32, 16], F32)
        v1 = self.alloc([32, 16], F32)
        frb = fre[0:64].unsqueeze(2).to_broadcast([64, 32, 16])
        fib = fim[0:64].unsqueeze(2).to_broadcast([64, 32, 16])
        self.TT("dve", Bbr[0:64], Bre[0:64], frb, ALU.mult, ["Bre", "fre"], ["Bbr"])
        self.TT("dve", v1[0:64], Bim[0:64], fib, ALU.mult, ["Bim", "fim"], ["v1"])
        self.TT("dve", Bbr[0:64], Bbr[0:64], v1[0:64], ALU.subtract, ["Bbr", "v1"], ["Bbr"])
        self.TT("dve", Bbi[0:64], Bim[0:64], frb, ALU.mult, ["Bim", "fre"], ["Bbi"])
        self.TT("dve", v1[0:64], Bre[0:64], fib, ALU.mult, ["Bre", "fim", "Bbr"], ["v1"])
        self.TT("dve", Bbi[0:64], Bbi[0:64], v1[0:64], ALU.add, ["Bbi", "v1"], ["Bbi"])
        if STOP <= 3:
            return
        rowmask = self.alloc([8], F32)
        self.DMA("sp", rowmask, din["c_rowmask"], [], ["rowmask"])
        Tre = self.alloc([64], F32)
        Tim = self.alloc([64], F32)
        rmb = rowmask.unsqueeze(2).to_broadcast([128, 8, 64])
        for j in range(4):
            b = self.nb()
            self.TR(self.pb[b][:, 0:64], Bbr[0:64, j * 8:(j + 1) * 8, :].rearrange("p g c -> p (g c)"), self.identf[0:64, 0:64], ["Bbr", "identf"], [("pb", b)])
            self.TR(self.pb[b][:, 64:128], Bbi[0:64, j * 8:(j + 1) * 8, :].rearrange("p g c -> p (g c)"), self.identf[0:64, 0:64], ["Bbi", "identf"], [("pb", b)])
            self.CP("dve", Tre, self.pb[b][:, 0:64], [("pb", b)], ["Tre"])
            self.CP("dve", Tim, self.pb[b][:, 64:128], [("pb", b)], ["Tim"])
            treb = Tre.unsqueeze(1).to_broadcast([128, 8, 64])
            timb = Tim.unsqueeze(1).to_broadcast([128, 8, 64])
            self.TT("dve", Bexp[:, j * 8:(j + 1) * 8, 0:64], treb, rmb, ALU.mult, ["Tre", "rowmask"], ["Bexp"])
            self.TT("dve", Bexp[:, j * 8:(j + 1) * 8, 64:128], timb, rmb, ALU.mult, ["Tim", "rowmask"], ["Bexp"])
            self.TT("dve", Bsw[:, j * 8:(j + 1) * 8, 0:64], timb, rmb, ALU.mult, ["Tim", "rowmask"], ["Bsw"])
            self.TT("dve", Bsw[:, j * 8:(j + 1) * 8, 64:128], treb, rmb, ALU.mult, ["Tre", "rowmask"], ["Bsw"])
        if STOP <= 4:
            return
        CC = self.alloc([4, 128], F32)
        CC2 = self.alloc([4, 128], F32)
        cre = din["s5_c_re"][0].rearrange("(j g) c p -> (g c) j p", j=4)
        cim = din["s5_c_im"][0].rearrange("(j g) c p -> (g c) j p", j=4)
        self.DMA("sp", CC[:, :, 0:64], cre, [], ["CC"], slow=True)
        self.DMA("sp", CC[:, :, 64:128], cim, [], ["CC"], slow=True)
        self.DMA("sp", CC2[:, :, 0:64], cim, [], ["CC2"], slow=True)
        self.DMA("sp", CC2[:, :, 64:128], cre, [], ["CC2"], slow=True)
        for j in range(4):
            b = self.nb()
            self.TR(self.pb[b][:, 0:128], CC[:, j, :], self.identf, ["CC", "identf"], [("pb", b)])
            self.TR(self.pb[b][:, 128:256], CC2[:, j, :], self.identf, ["CC2", "identf"], [("pb", b)])
            self.TS("dve", C1[:, j * 8:(j + 1) * 8, :], self.pb[b][:, 0:128].rearrange("p (g c) -> p g c", g=8), sgn[:, 4:5], None, ALU.mult, None,
                    [("pb", b), "sgn"], ["C1"])
            self.TS("dve", C2[:, j * 8:(j + 1) * 8, :], self.pb[b][:, 128:256].rearrange("p (g c) -> p g c", g=8), sgn[:, 5:6], None, ALU.mult, None,
                    [("pb", b), "sgn"], ["C2"])
        Dcol = self.alloc([4], F32)
        self.DMA("sp", Dcol, din["s5_d"][0].rearrange("(j g) c -> (g c) j", j=4), [], ["Dcol"], slow=True)
        self.TS("dve", Dcol, Dcol, 0.5, None, ALU.mult, None, ["Dcol"], ["Dcol"])
        for j in range(4):
            self.TS("dve", Ddiag[:, j, :], self.identf, Dcol[:, j:j + 1], None, ALU.mult, None, ["identf", "Dcol"], ["Ddiag"])
        if STOP <= 5:
            return
        S5K = ["Bexp", "Bsw", "C1", "C2", "COS", "SINS", "rr", "Arot", "Brot", "Ddiag", "perm"]
        self.S.barrier()
        self.top = tmark

        hx = self.alloc([2, 1024], F32)
        hnT = self.alloc([8, 512], BF16)
        rope = [self.alloc([2, 512], F32)]
        ta = [self.alloc([512], F32)]
        tbb = [self.alloc([512], F32)]
        qT = self.alloc([4, 512], BF16)
        kT = self.alloc([4, 512], BF16)
        vtm = self.alloc([4, 512], BF16)
        sgt = self.alloc([4, 512], BF16)
        uT = self.alloc([4, 512], BF16)
        mixb = self.alloc([8, 512], BF16)
        PTm = self.alloc([1024], BF16)
        Kd = self.alloc([512], BF16)
        R = self.alloc([512], F32)
        Rbf = self.alloc([512], BF16)
        self.MS("dve", R, 0.0, ["R"])
        self.MS("dve", Rbf, 0.0, ["Rbf"])
        sum1 = self.alloc([8], F32)
        var1 = self.alloc([8], F32)
        rs8 = self.alloc([8], F32)
        xc = self.alloc([8, 64], F32)
        sq = self.alloc([8, 64], F32)
        rtm = self.alloc([512], BF16)
        RV = [self.alloc([8, 128], F32)]
        TMP = [self.alloc([8, 128], BF16)]
        Wst = [self.alloc([8, 128], F32)]
        P1 = [self.alloc([8, 128], BF16)]
        P2 = [self.alloc([8, 128], BF16)]
        w127 = self.alloc([32], F32)
        init = self.alloc([32], F32)
        ctmp = self.alloc([32], F32)
        self.MS("dve", init, 0.0, ["init"])
        gs = ta[0]
        gi1 = ta[0]
        gth = tbb[0]
        Gb = self.alloc([512], BF16)
        GT = self.alloc([4, 128], BF16)
        th2 = self.alloc([4, 128], F32)
        K2 = 2.0 * math.sqrt(2.0 / PI)
        for tb in range(NB):
            t0 = tb * 512
            rs_ = 0
            self.DMA("sp", rope[rs_], din["c_rope"].rearrange("c p t -> p c t")[:, :, t0:t0 + 512], [], [("rope", rs_)])
            for hf in range(2):
                for tt in range(2):
                    tg_ = hf * 2 + tt
                    self.DMA("sp", hx[:, tt, :], din["x"][t0 + tg_ * 128:t0 + (tg_ + 1) * 128, :], [], [("hx", tt)])
                self.norm_block([hx[:, tt, :] for tt in range(2)], [("hx", tt) for tt in range(2)], 0, hnT, "hnT", "l0n", toff=hf * 2)
            for ft in range(8):
                s = 0
                ba = self.nb()
                for kc in range(8):
                    self.MM(self.pb[ba][:], W[:, kc, ft * 128:(ft + 1) * 128], hnT[:, kc, :], kc == 0, kc == 7, ["hnT", Wk[kc]], [("pb", ba)])
                bs = self.nb()
                for kc in range(8):
                    self.MM(self.pb[bs][:], Wsw[:, kc, ft * 128:(ft + 1) * 128], hnT[:, kc, :], kc == 0, kc == 7, ["hnT", Wswk[kc]], [("pb", bs)])
                self.TT("dve", ta[s], self.pb[ba][:], rope[rs_][:, 0, :], ALU.mult, [("pb", ba), ("rope", rs_)], [("ta", s)])
                self.TT("dve", tbb[s], self.pb[bs][:], rope[rs_][:, 1, :], ALU.mult, [("pb", bs), ("rope", rs_)], [("tb", s)])
                dst = qT[:, ft, :] if ft < 4 else kT[:, ft - 4, :]
                dk_ = ("qT", ft) if ft < 4 else ("kT", ft - 4)
                self.TT("pool", dst, ta[s], tbb[s], ALU.add, [("ta", s), ("tb", s)], [dk_])
            for c in range(4):
                b = self.nb()
                for kc in range(8):
                    self.MM(self.pb[b][:], hnT[:, kc, c * 128:(c + 1) * 128], W[:, kc, 1024:1536], kc == 0, kc == 7, ["hnT", Wk[kc]], [("pb", b)])
                self.ACT(vtm[:, c, :], self.pb[b][:], AF.Copy, [("pb", b)], [("vtm", c)])
                b = self.nb()
                for kc in range(8):
                    self.MM(self.pb[b][:], hnT[:, kc, c * 128:(c + 1) * 128], W[:, kc, 1536:2048], kc == 0, kc == 7, ["hnT", Wk[kc]], [("pb", b)])
                self.ACT(sgt[:, c, :], self.pb[b][:], AF.Silu, [("pb", b)], [("sgt", c)])
            for j in range(4):
                b = self.nb()
                for kc in range(8):
                    self.MM(self.pb[b][:], W[:, kc, 2048 + j * 128:2048 + (j + 1) * 128], hnT[:, kc, :], kc == 0, kc == 7, ["hnT", Wk[kc]], [("pb", b)])
                self.ACT(uT[:, j, :], self.pb[b][:], AF.Copy, [("pb", b)], [("uT", j)])
            def ret_thread():
                for c in range(4):
                    cs = slice(c * 128, (c + 1) * 128)
                    self.convert_weights(0, n=2)
                    bs0, bs1 = self.nb(True), self.nb(True)
                    for h in range(8):
                        pr, hl = h // 2, h % 2
                        rows = slice(hl * 64, (hl + 1) * 64)
                        bb = bs0 if hl == 0 else bs1
                        self.MM(self.pb[bb][:, pr * 128:(pr + 1) * 128], kT[rows, pr, cs], qT[rows, pr, cs], True, True,
                                [("kT", pr), ("qT", pr)], [("pb", bb)])
                    yield
                    pv = PTm.rearrange("p (r l t) -> p r l t", r=4, l=2)
                    mv = retmask.rearrange("p (r l t) -> p r l t", r=4, l=2)
                    self.TT("dve", pv[:, :, 0, :], self.pb[bs0][:].rearrange("p (r t) -> p r t", r=4), mv[:, :, 0, :], ALU.mult,
                            [("pb", bs0), "retmask"], [("PTm", 0)])
                    self.TT("dve", pv[:, :, 1, :], self.pb[bs1][:].rearrange("p (r t) -> p r t", r=4), mv[:, :, 1, :], ALU.mult,
                            [("pb", bs1), "retmask"], [("PTm", 1)])
                    for b_ in (bs0, bs1):
                        self.reserved.discard(b_)
                    bk = self.nb(True)
                    pkb = self.pb[bk][:].bitcast(BF16)
                    for pr in range(4):
                        self.TR(pkb[:, pr * 128:(pr + 1) * 128], kT[:, pr, cs], self.identb, [("kT", pr), "identb"], [("pb", bk)])
                    yield
                    self.TT("dve", Kd, pkb[:, 0:512], kdec, ALU.mult, [("pb", bk), "kdec"], ["Kd"])
                    self.reserved.discard(bk)
                    bo = self.nb(True)
                    for h in range(8):
                        pr, hl = h // 2, h % 2
                        rows = slice(hl * 64, (hl + 1) * 64)
                        self.MM(self.pb[bo][:, h * 64:(h + 1) * 64], PTm[:, h * 128:(h + 1) * 128], vtm[:, c, h * 64:(h + 1) * 64], True, False,
                                [("PTm", 0), ("PTm", 1), ("vtm", c)], [("pb", bo)])
                        self.MM(self.pb[bo][:, h * 64:(h + 1) * 64], qT[rows, pr, cs], Rbf[rows, pr * 128 + hl * 64:pr * 128 + (hl + 1) * 64], False, True,
                                [("qT", pr), "Rbf"], [("pb", bo)])
                    bkv = self.nb(True)
                    for pr in range(4):
                        self.MM(self.pb[bkv][:, pr * 128:(pr + 1) * 128], Kd[:, pr * 128:(pr + 1) * 128], vtm[:, c, pr * 128:(pr + 1) * 128], True, True,
                                ["Kd", ("vtm", c)], [("pb", bkv)])
                    yield
                    self.TT("dve", R, R, g128, ALU.mult, ["R", "g128"], ["R"])
                    self.TT("dve", R, R, self.pb[bkv][:], ALU.add, ["R", ("pb", bkv)], ["R"])
                    self.reserved.discard(bkv)
                    self.ACT(Rbf, R, AF.Copy, ["R"], ["Rbf"])
                    po = self.pb[bo][:].rearrange("p (h e) -> p h e", h=8)
                    S.op("dve", lambda e, po=po: e.tensor_reduce(out=sum1, in_=po, axis=AX.X, op=ALU.add), reads=[("pb", bo)], writes=["sum1"])
                    self.TS("dve", sum1, sum1, 1.0 / 64.0, None, ALU.mult, None, ["sum1"], ["sum1"])
                    self.TT("dve", xc, po, sum1.unsqueeze(2).to_broadcast([128, 8, 64]), ALU.subtract, [("pb", bo), "sum1"], ["xc"])
                    self.reserved.discard(bo)
                    self.ACT(sq, xc, AF.Square, ["xc"], ["sq"])
                    yield
                    S.op("dve", lambda e: e.tensor_reduce(out=var1, in_=sq, axis=AX.X, op=ALU.add), reads=["sq"], writes=["var1"])
                    self.TT("dve", var1, var1, reps, ALU.add, ["var1", "reps"], ["var1"])
                    self.rsqrt("dve", rs8, var1, 8, "var1", "rs8", "l0ln")
                    self.STT("dve", xc, xc, 8.0, rs8.unsqueeze(2).to_broadcast([128, 8, 64]), ALU.mult, ALU.mult, ["xc", "rs8"], ["xc"])
                    self.TT("pool", rtm, xc.rearrange("p h e -> p (h e)"), sgt[:, c, :], ALU.mult, ["xc", ("sgt", c)], ["rtm"])
                    yield
                    bt = self.nb()
                    ptb = self.pb[bt][:].bitcast(BF16)
                    for j in range(4):
                        self.TR(ptb[:, j * 128:(j + 1) * 128], rtm[:, j * 128:(j + 1) * 128], self.identb, ["rtm", "identb"], [("pb", bt)])
                    self.ACT(mixb[:, 0:4, cs], ptb[:, 0:512].rearrange("p (j t) -> p j t", j=4), AF.Copy, [("pb", bt)], [("mixb", c)])
                    yield

            def s5_thread():
                for c in range(4):
                    cs = slice(c * 128, (c + 1) * 128)
                    by = None
                    pend = {}

                    def s1(ht):
                        j, hf = ht // 2, ht % 2
                        g0 = j * 8 + hf * 4
                        bu, bw = self.nb(True), self.nb(True)
                        for gl in range(4):
                            g = g0 + gl
                            self.MM(self.pb[bu][:, gl * 128:(gl + 1) * 128], Bexp[:, g, :], uT[:, j, cs], True, True, ["Bexp", ("uT", j)], [("pb", bu)])
                            self.MM(self.pb[bw][:, gl * 128:(gl + 1) * 128], Bsw[:, g, :], uT[:, j, cs], True, True, ["Bsw", ("uT", j)], [("pb", bw)])
                        pend[ht] = (bu, bw)

                    s1(0)
                    for ht in range(8):
                        j, hf = ht // 2, ht % 2
                        g0 = j * 8 + hf * 4
                        hs_ = slice(hf * 4, hf * 4 + 4)
                        if ht + 1 < 8:
                            s1(ht + 1)
                        bu, bw = pend.pop(ht)
                        yield
                        self.TT("dve", RV[0][:, hs_, :], self.pb[bu][:].rearrange("p (g t) -> p g t", g=4), COS[:, g0:g0 + 4, :], ALU.mult,
                                [("pb", bu), "COS"], [("RV", hf)])
                        self.TT("dve", TMP[0][:, hs_, :], self.pb[bw][:].rearrange("p (g t) -> p g t", g=4), SINS[:, g0:g0 + 4, :], ALU.mult,
                                [("pb", bw), "SINS"], [("TMP", hf)])
                        self.reserved.discard(bu)
                        self.reserved.discard(bw)
                        self.TT("dve", RV[0][:, hs_, :], RV[0][:, hs_, :], TMP[0][:, hs_, :], ALU.add, [("RV", hf), ("TMP", hf)], [("RV", hf)])
                        yield
                        for gl in range(4):
                            g = g0 + gl
                            S.op("dve", lambda e, gl=gl, g=g, hf=hf: e.tensor_tensor_scan(out=Wst[0][:, hf * 4 + gl, :], data0=rr[:, g:g + 1].to_broadcast([128, 128]),
                                                                                         data1=RV[0][:, hf * 4 + gl, :], initial=init[:, g:g + 1],
                                                                                         op0=ALU.mult, op1=ALU.add),
                                 reads=[("RV", hf), "rr", "init"], writes=[("Wst", hf)])
                        self.CP("dve", w127[:, g0:g0 + 4], Wst[0][:, hs_, 127], [("Wst", hf)], [("w127", ht)])
                        self.TT(os.environ.get('P1ENG', 'dve'), P1[0][:, hs_, :], Wst[0][:, hs_, :], COS[:, g0:g0 + 4, :], ALU.mult, [("Wst", hf), "COS"], [("P1", hf)])
                        self.TT("pool", P2[0][:, hs_, :], Wst[0][:, hs_, :], SINS[:, g0:g0 + 4, :], ALU.mult, [("Wst", hf), "SINS"], [("P2", hf)])
                        yield
                        if by is None:
                            by = self.nb(True)
                        if hf == 0:
                            self.MM(self.pb[by][:, j * 128:(j + 1) * 128], uT[:, j, cs], Ddiag[:, j, :], True, False, [("uT", j), "Ddiag"], [("pb", by)])
                        for gl in range(4):
                            g = g0 + gl
                            self.MM(self.pb[by][:, g * 16:(g + 1) * 16], P1[0][:, hf * 4 + gl, :], C1[:, g, :], False, False, [("P1", hf), "C1"], [("pb", by)])
                            self.MM(self.pb[by][:, g * 16:(g + 1) * 16], P2[0][:, hf * 4 + gl, :], C2[:, g, :], False, hf == 1 and gl == 3, [("P2", hf), "C2"], [("pb", by)])
                    bc = self.nb(True)
                    wk = [("w127", ht) for ht in range(8)]
                    self.MM(self.pb[bc][:, 0:32], perm, w127, True, True, ["perm"] + wk, [("pb", bc)])
                    yield
                    self.TT("dve", ctmp, self.pb[bc][:, 0:32], Brot, ALU.mult, [("pb", bc), "Brot"], ["ctmp"])
                    self.reserved.discard(bc)
                    self.TT("dve", init, w127, Arot, ALU.mult, wk + ["Arot"], ["init"])
                    self.TT("dve", init, init, ctmp, ALU.add, ["init", "ctmp"], ["init"])
                    yh = self.pb[by][:]
                    yk = ("pb", by)
                    self.ACT(gs, yh, AF.Square, [yk], [("ta", 0)])
                    self.TS("dve", gs, gs, 0.044715 * 4.0, 1.0, ALU.mult, ALU.add, [("ta", 0)], [("ta", 0)])
                    self.TT("dve", gi1, gs, yh, ALU.mult, [("ta", 0), yk], [("ta", 0)])
                    self.ACT(gth, gi1, AF.Tanh, [("ta", 0)], [("tb", 0)], scale=K2)
                    yield
                    self.STT("dve", Gb, gth, 1.0, yh, ALU.add, ALU.mult, [("tb", 0), yk], ["Gb"])
                    self.reserved.discard(by)
                    bt = self.nb(True)
                    ptb = self.pb[bt][:].bitcast(BF16)
                    for j in range(4):
                        self.TR(ptb[:, j * 128:(j + 1) * 128], Gb[:, j * 128:(j + 1) * 128], self.identb, ["Gb", "identb"], [("pb", bt)])
                    yield
                    self.ACT(GT, ptb[:, 0:512].rearrange("p (j t) -> p j t", j=4), AF.Copy, [("pb", bt)], ["GT"], scale=0.5)
                    self.reserved.discard(bt)
                    bz = self.nb(True)
                    for j2 in range(4):
                        for j in range(4):
                            self.MM(self.pb[bz][:, j2 * 128:(j2 + 1) * 128], Wglu[:, j, j2 * 128:(j2 + 1) * 128], GT[:, j, :], j == 0, j == 3,
                                    ["Wglu", "GT"], [("pb", bz)])
                    yield
                    self.ACT(th2, self.pb[bz][:].rearrange("p (j t) -> p j t", j=4), AF.Tanh, [("pb", bz)], ["th2"])
                    self.reserved.discard(bz)
                    self.STT("dve", mixb[:, 4:8, cs], th2, 1.0, GT, ALU.add, ALU.mult, ["th2", "GT"], [("mixb2", c)])
                    yield

            threads = [ret_thread(), s5_thread()]
            weights = [int(os.environ.get('WR', '1')), int(os.environ.get('WS', '4'))]
            if os.environ.get('ONLY') == 'ret':
                threads, weights = [ret_thread()], [1]
            if os.environ.get('ONLY') == 's5':
                threads, weights = [s5_thread()], [1]
            if os.environ.get('SEQ') == '1':
                threads, weights = [ret_thread(), s5_thread()], [1000, 1000]
            while threads:
                for ti in range(len(threads) - 1, -1, -1):
                    for _ in range(weights[ti]):
                        try:
                            next(threads[ti])
                        except StopIteration:
                            threads.pop(ti)
                            weights.pop(ti)
                            break
            st_i = self.DMA("sp", self.mixT.rearrange("(kc p) t -> p kc t", p=128)[:, :, t0:t0 + 512], mixb,
                     [("mixb", c) for c in range(4)] + [("mixb2", c) for c in range(4)], [], semkey="mixb_out")
            if tb == NB - 1:
                self.convert_weights(0, chain_depth=4)
            for c in range(4):
                self.S.readers.setdefault(("mixb", c), []).append(st_i)
                self.S.readers.setdefault(("mixb2", c), []).append(st_i)


_CACHE = {}


def _get_nc(T, debug=()):
    key = (T, tuple(debug))
    if key not in _CACHE:
        _CACHE[key] = Builder(T, debug).build()
    return _CACHE[key]


def kernel(**inputs):
    x = np.asarray(inputs["x"], dtype=np.float32)
    p = np.asarray(inputs["p"], dtype=np.float32)
    B, T, _ = x.shape
    nc = _get_nc(T)
    hc = host_consts(T)
    in_maps = []
    for b in range(B):
        m = {"x": np.ascontiguousarray(x[b]), "p": np.ascontiguousarray(p[:, b])}
        for n in WEIGHT_NAMES:
            m[n] = np.ascontiguousarray(np.asarray(inputs[n], dtype=np.float32))
        m.update(hc)
        in_maps.append(m)
    res = run_bass_kernel_spmd(nc, in_maps, core_ids=list(range(B)))
    return np.stack([np.asarray(r["y"], dtype=np.float32) for r in res.results], axis=0)
```

```python
import contextlib
import os
import math
import numpy as np
import ml_dtypes
import concourse.bass as bass
import concourse.mybir as mybir
from concourse.bass_utils import run_bass_kernel_spmd

F32 = mybir.dt.float32
BF16 = mybir.dt.bfloat16
I32 = mybir.dt.int32
ALU = mybir.AluOpType
AF = mybir.ActivationFunctionType
AX = mybir.AxisListType

D = 1024
FF = 2816
NFC = FF // 128
PLE = 256
EPS = 1e-6
PI = math.pi


class Sched:
    ENGS = ("pe", "act", "dve", "pool", "sp")

    def __init__(self, nc):
        self.nc = nc
        self.ops = []
        self.last_w = {}
        self.readers = {}
        self.bar = set()
        self.last_on = {}
        self.dmas_since = []
        self.bg_dmas = []

    def op(self, eng, fn, reads=(), writes=(), dma=False, semkey=None, bg=False):
        i = len(self.ops)
        deps = set(self.bar)
        for k in reads:
            if k in self.last_w:
                deps.add(self.last_w[k])
        for k in writes:
            if k in self.last_w:
                deps.add(self.last_w[k])
            for r in self.readers.get(k, ()):
                deps.add(r)
        for k in reads:
            self.readers.setdefault(k, []).append(i)
        for k in writes:
            self.last_w[k] = i
            self.readers[k] = []
        deps.discard(i)
        if dma and semkey is None:
            semkey = (list(writes) + list(reads))[0]
        self.ops.append(dict(eng=eng, fn=fn, deps=deps, dma=dma, semkey=semkey))
        if dma and bg:
            self.bg_dmas.append(i)
        elif dma:
            self.dmas_since.append(i)
        else:
            self.last_on[eng] = i
        return i

    def barrier(self, include_bg=False):
        self.bar = set(self.last_on.values()) | set(self.dmas_since)
        if include_bg:
            self.bar |= set(self.bg_dmas)
            self.bg_dmas = []
        self.dmas_since = []
        self.last_w = {}
        self.readers = {}

    def emit(self, final_wait=()):
        nc, ops = self.nc, self.ops
        needed = set(final_wait)
        for o in ops:
            needed |= o["deps"]
        cnt = {e: 0 for e in self.ENGS}
        dcnt = {}
        for i, o in enumerate(ops):
            if o["dma"]:
                k = o["semkey"]
                dcnt[k] = dcnt.get(k, 0) + 16
                o["sig"] = ("d", k, dcnt[k])
            elif i in needed:
                cnt[o["eng"]] += 1
                o["sig"] = ("e", o["eng"], cnt[o["eng"]])
            else:
                o["sig"] = None
        with contextlib.ExitStack() as st:
            esem = {e: st.enter_context(nc.semaphore("s_" + e)) for e in self.ENGS}
            dsem = {}
            print("[sched] ops=%d dma_sems=%d" % (len(ops), len(dcnt)))
            for n, k in enumerate(dcnt):
                dsem[k] = st.enter_context(nc.semaphore("d%d" % n))
            block = st.enter_context(nc.Block())
            per = {e: [] for e in self.ENGS}
            for i, o in enumerate(ops):
                per[o["eng"]].append(i)

            def run(engname, eng):
                waited = {}

                def do_waits(deps):
                    want = {}
                    for d in deps:
                        s = ops[d]["sig"]
                        if s is None:
                            continue
                        if s[0] == "e" and s[1] == engname and engname == "pe":
                            continue
                        key = (s[0], s[1])
                        want[key] = max(want.get(key, 0), s[2])
                    for key, v in want.items():
                        if waited.get(key, 0) >= v:
                            continue
                        waited[key] = v
                        sem = esem[key[1]] if key[0] == "e" else dsem[key[1]]
                        eng.wait_ge(sem, v)

                for i in per[engname]:
                    o = ops[i]
                    do_waits(o["deps"])
                    ins = o["fn"](eng)
                    s = o["sig"]
                    if s is not None:
                        if s[0] == "d":
                            ins.then_inc(dsem[s[1]], 16)
                        else:
                            ins.then_inc(esem[s[1]], 1)
                if engname == "sp":
                    do_waits(final_wait)

            block.tensor(lambda e: run("pe", e))
            block.scalar(lambda e: run("act", e))
            block.vector(lambda e: run("dve", e))
            block.gpsimd(lambda e: run("pool", e))
            block.sync(lambda e: run("sp", e))


def host_consts(T):
    c = {}
    c["c_identb"] = np.eye(128, dtype=np.float32).astype(ml_dtypes.bfloat16)
    c["c_identf"] = np.eye(128, dtype=np.float32)
    d = 64
    inv = (10000.0 ** (-np.arange(0, d, 2, dtype=np.float32) / d)).astype(np.float32)
    pos = np.arange(T, dtype=np.float32)
    ang = (pos[None, :] * inv[:, None]).astype(np.float32)
    cos = np.cos(ang).astype(np.float32)
    sin = np.sin(ang).astype(np.float32)
    cos64 = np.concatenate([cos, cos], 0)
    sin64 = np.concatenate([-sin, sin], 0)
    c["c_rope"] = np.stack([np.concatenate([cos64, cos64], 0), np.concatenate([sin64, sin64], 0)], 0).astype(np.float32)
    gam = (1.0 - 2.0 ** (-5.0 - np.arange(8))).astype(np.float64)
    s = np.arange(128)
    m = np.zeros((128, 8, 128), np.float64)
    for h in range(8):
        mm_ = (gam[h] ** (-(s[:, None] + 1.0))) / 8.0 * (s[None, :] >= s[:, None])
        m[:, h, :] = mm_
    c["c_retmask"] = m.reshape(128, 1024).astype(np.float32)
    kd = np.zeros((128, 8, 64), np.float64)
    for h in range(8):
        kd[:, h, :] = ((gam[h] ** (127.0 - s)) / 8.0)[:, None]
    c["c_kdec"] = kd.reshape(128, 512).astype(np.float32)
    g128 = np.zeros((128, 4, 128), np.float64)
    for pr in range(4):
        for hl in range(2):
            g128[hl * 64:(hl + 1) * 64, pr, :] = gam[2 * pr + hl] ** 128
    c["c_g128"] = g128.reshape(128, 512).astype(np.float32)
    ep = np.zeros((128, 8), np.float64)
    for h in range(8):
        ep[:, h] = 64.0 * EPS / gam[h] ** (2.0 * (s + 1.0))
    c["c_reps"] = ep.astype(np.float32)
    mb = np.where(s[:, None] <= s[None, :], 0.0, -30000.0).astype(np.float32)
    c["c_maskb"] = mb.astype(ml_dtypes.bfloat16)
    c["c_iota"] = np.tile(np.arange(128, dtype=np.float32)[None, :], (128, 1))
    perm = np.zeros((128, 128), np.float32)
    for mcol in range(128):
        perm[(mcol + 64) % 128, mcol] = 1.0
    c["c_perm"] = perm
    rm = np.zeros((128, 8), np.float32)
    for p in range(128):
        rm[p, p // 16] = 1.0
    c["c_rowmask"] = rm
    sg = np.zeros((128, 8), np.float32)
    top = (np.arange(128) < 64)
    sg[:, 0] = np.where(top, -1.0, 1.0); sg[:, 1] = np.where(top, PI, -PI)
    sg[:, 2] = np.where(top, 1.0, -1.0); sg[:, 3] = np.where(top, -PI, PI)
    sg[:, 4] = np.where(top, 0.5, -0.5); sg[:, 5] = np.where(top, -0.5, 0.5); sg[:, 6] = PI
    c["c_sgn"] = sg
    return c


WEIGHT_NAMES = ["norm_mix", "norm_ffn", "norm_ple", "ret_s5_w_in", "ret_s5_w_out", "s5_lambda_re", "s5_lambda_im",
                "s5_b_re", "s5_b_im", "s5_c_re", "s5_c_im", "s5_d", "s5_log_step", "s5_w_glu", "diff_w_qkv",
                "diff_w_o", "diff_lambda_q1", "diff_lambda_k1", "diff_lambda_q2", "diff_lambda_k2", "diff_subln",
                "ffn_w_gate", "ffn_w_up", "ffn_w_down", "ple_w_proj", "ple_w_gate", "final_norm"]
WEIGHT_SHAPES = {
    "norm_mix": [2, 1024], "norm_ffn": [2, 1024], "norm_ple": [2, 1024], "ret_s5_w_in": [1, 1024, 2560],
    "ret_s5_w_out": [1, 1024, 1024], "s5_lambda_re": [1, 32, 64], "s5_lambda_im": [1, 32, 64],
    "s5_b_re": [1, 32, 64, 16], "s5_b_im": [1, 32, 64, 16], "s5_c_re": [1, 32, 16, 64], "s5_c_im": [1, 32, 16, 64],
    "s5_d": [1, 32, 16], "s5_log_step": [1, 32], "s5_w_glu": [1, 512, 512], "diff_w_qkv": [1, 1024, 3072],
    "diff_w_o": [1, 1024, 1024], "diff_lambda_q1": [1, 64], "diff_lambda_k1": [1, 64], "diff_lambda_q2": [1, 64],
    "diff_lambda_k2": [1, 64], "diff_subln": [1, 128], "ffn_w_gate": [2, 1024, 2816], "ffn_w_up": [2, 1024, 2816],
    "ffn_w_down": [2, 2816, 1024], "ple_w_proj": [2, 256, 1024], "ple_w_gate": [2, 1024, 1024], "final_norm": [1024],
}


class Builder:
    def __init__(self, T, debug=()):
        self.T = T
        self.NB = T // 512
        self.debug = debug
        nc = bass.Bass("TRN2", target_bir_lowering=False)
        self.nc = nc
        self.S = Sched(nc)
        self.din = {}
        self.din["x"] = nc.dram_tensor("x", [T, D], F32, kind="ExternalInput").ap()
        self.din["p"] = nc.dram_tensor("p", [2, T, PLE], F32, kind="ExternalInput").ap()
        for n in WEIGHT_NAMES:
            self.din[n] = nc.dram_tensor(n, WEIGHT_SHAPES[n], F32, kind="ExternalInput").ap()
        hc = host_consts(T)
        for n, a in hc.items():
            dt = BF16 if a.dtype == ml_dtypes.bfloat16 else F32
            self.din[n] = nc.dram_tensor(n, list(a.shape), dt, kind="ExternalInput").ap()
        self.y = nc.dram_tensor("y", [T, D], F32, kind="ExternalOutput").ap()
        sk = "ExternalOutput" if debug else "Internal"
        self.h1 = nc.dram_tensor("h1s", [T, D], F32, kind=sk).ap()
        self.mixT = nc.dram_tensor("mixTs", [D, T], BF16, kind=sk).ap()
        self.qkT = nc.dram_tensor("qkTs", [2048, T], BF16, kind=sk).ap()
        self.vS = nc.dram_tensor("vs", [T, D], BF16, kind=sk).ap()
        self.wbf = {}
        for l in range(2):
            for nm, shp in [("wo", [1024, 1024]), ("wg", [1024, FF]), ("wu", [1024, FF]), ("wd", [FF, 1024]),
                            ("wpg", [1024, 1024]), ("wpp", [PLE, 1024])]:
                self.wbf[(nm, l)] = nc.dram_tensor("%s_bf%d" % (nm, l), shp, BF16, kind="Internal").ap()
        self.dbg = {}
        self.AW = 52600
        self.arena = nc.alloc_sbuf_tensor("arena", [128, self.AW], F32)
        self.top = 0
        self.pb = [nc.alloc_psum_tensor("pb%d" % i, [128, 512], F32) for i in range(8)]
        self.pbi = 0
        self.conv_pending = {}
        self.conv_count = {}
        self.reserved = set()
        self.final = []

    def conv_list(self, l):
        din = self.din
        srcs = {"wo": (din["ret_s5_w_out"] if l == 0 else din["diff_w_o"])[0], "wg": din["ffn_w_gate"][l],
                "wu": din["ffn_w_up"][l], "wd": din["ffn_w_down"][l], "wpg": din["ple_w_gate"][l], "wpp": din["ple_w_proj"][l]}
        fns = []
        for nm, src in srcs.items():
            dst = self.wbf[(nm, l)]
            rows = dst.shape[0]
            step = 256
            for r0 in range(0, rows, step):
                def f(chain=None, dst=dst, src=src, r0=r0, step=step, nm=nm):
                    wk = [("wbf", l, "chain", chain)] if chain is not None else [("wbf", nm, l, r0)]
                    self.DMA("pool", dst[r0:r0 + step, :], src[r0:r0 + step, :], [], wk, semkey=("wbf", l, chain), bg=True)
                fns.append(f)
        return fns

    def convert_weights(self, l, n=None, chain_depth=None):
        if l not in self.conv_pending:
            self.conv_pending[l] = self.conv_list(l)
            self.conv_count[l] = 0
        lst = self.conv_pending[l]
        k = len(lst) if n is None else min(n, len(lst))
        for _ in range(k):
            f = lst.pop(0)
            if chain_depth:
                f(chain=self.conv_count[l] % chain_depth)
            else:
                f()
            self.conv_count[l] += 1

    def alloc(self, shape, dt):
        n = int(np.prod(shape))
        words = n if dt in (F32, I32) else (n + 1) // 2
        assert self.top + words <= self.AW, ("SBUF arena overflow", self.top, words)
        v = self.arena[:, self.top:self.top + words]
        self.top += words
        if dt != F32:
            v = v.bitcast(dt)
            if dt == BF16:
                v = v[:, 0:n]
        if len(shape) == 2:
            return v.rearrange("p (a b) -> p a b", a=shape[0])
        if len(shape) == 3:
            return v.rearrange("p (a b c) -> p a b c", a=shape[0], b=shape[1])
        return v

    def nb(self, reserve=False):
        while True:
            b = self.pbi
            self.pbi = (self.pbi + 1) % 8
            if b not in self.reserved:
                break
        if reserve:
            self.reserved.add(b)
        return b

    def MM(self, out, lhsT, rhs, start, stop, r, w):
        self.S.op("pe", lambda e: e.matmul(out, lhsT, rhs, start=start, stop=stop), reads=r, writes=w)

    def TR(self, out, in_, ident, r, w):
        self.S.op("pe", lambda e: e.transpose(out=out, in_=in_, identity=ident), reads=r, writes=w)

    def ACT(self, out, in_, func, r, w, scale=1.0, bias=None, accum=None):
        def f(e):
            kw = dict(out=out, in_=in_, func=func, scale=scale)
            if bias is not None:
                kw["bias"] = bias
            if accum is not None:
                kw["accum_out"] = accum
            return e.activation(**kw)
        self.S.op("act", f, reads=r, writes=w)

    def TT(self, eng, out, in0, in1, op, r, w):
        self.S.op(eng, lambda e: e.tensor_tensor(out=out, in0=in0, in1=in1, op=op), reads=r, writes=w)

    def TS(self, eng, out, in0, s1, s2, op0, op1, r, w):
        if op1 is None:
            self.S.op(eng, lambda e: e.tensor_scalar(out=out, in0=in0, scalar1=s1, scalar2=None, op0=op0), reads=r, writes=w)
        else:
            self.S.op(eng, lambda e: e.tensor_scalar(out=out, in0=in0, scalar1=s1, scalar2=s2, op0=op0, op1=op1), reads=r, writes=w)

    def STT(self, eng, out, in0, scalar, in1, op0, op1, r, w):
        self.S.op(eng, lambda e: e.scalar_tensor_tensor(out=out, in0=in0, scalar=scalar, in1=in1, op0=op0, op1=op1), reads=r, writes=w)

    def CP(self, eng, out, in_, r, w):
        self.S.op(eng, lambda e: e.tensor_copy(out=out, in_=in_), reads=r, writes=w)

    def MS(self, eng, out, val, w):
        self.S.op(eng, lambda e: e.memset(out, val), writes=w)

    def DMA(self, eng, out, in_, r, w, semkey=None, slow=False, bg=False):
        if bg:
            return self.S.op(eng, lambda e: e.dma_start(out=out, in_=in_), reads=r, writes=w, dma=True, semkey=semkey, bg=True)
        if slow:
            return self.S.op(eng, lambda e: e.dma_start(out=out, in_=in_, allow_slow_non_contiguous=True), reads=r, writes=w, dma=True, semkey=semkey)
        return self.S.op(eng, lambda e: e.dma_start(out=out, in_=in_), reads=r, writes=w, dma=True, semkey=semkey)

    def rsqrt(self, eng, out, a, k, key_a, key_out, tag):
        ti = self.tmp_i[:, 0:k]
        y = self.tmp_y[:, 0:k]
        ta = self.tmp_a[:, 0:k]
        tb = self.tmp_b[:, 0:k]
        yf = y.bitcast(F32)
        kk = ("rsq", tag)
        self.TS(eng, ti, a.bitcast(I32), 1, None, ALU.arith_shift_right, None, [key_a], [kk + ("ti",)])
        self.TS(eng, y, ti, -1, 1597463007, ALU.mult, ALU.add, [kk + ("ti",)], [kk + ("y",)])
        for _ in range(3):
            self.TT(eng, ta, yf, yf, ALU.mult, [kk + ("y",)], [kk + ("ta",)])
            self.TT(eng, tb, ta, a, ALU.mult, [kk + ("ta",), key_a], [kk + ("tb",)])
            self.TS(eng, ta, tb, -0.5, 1.5, ALU.mult, ALU.add, [kk + ("tb",)], [kk + ("ta",)])
            self.TT(eng, yf, yf, ta, ALU.mult, [kk + ("y",), kk + ("ta",)], [kk + ("y",)])
        self.CP(eng, out, yf, [kk + ("y",)], [key_out])

    def build(self):
        S = self.S
        din = self.din
        T, NB = self.T, self.NB
        self.identb = self.alloc([128], BF16)
        self.identf = self.alloc([128], F32)
        self.gT = self.alloc([7, 8], F32)
        self.sgn = self.alloc([8], F32)
        self.lam = self.alloc([4], F32)
        self.tmp_i = self.alloc([32], I32)
        self.tmp_y = self.alloc([32], I32)
        self.tmp_a = self.alloc([32], F32)
        self.tmp_b = self.alloc([32], F32)
        self.DMA("sp", self.identb, din["c_identb"], [], ["identb"])
        self.DMA("sp", self.identf, din["c_identf"], [], ["identf"])
        self.DMA("sp", self.sgn, din["c_sgn"], [], ["sgn"])
        for l in range(2):
            for j, nm in enumerate(["norm_mix", "norm_ffn", "norm_ple"]):
                self.DMA("sp", self.gT[:, 3 * l + j, :], din[nm][l].rearrange("(kc p) -> p kc", p=128), [], ["gT"], slow=True)
        self.TS("dve", self.gT[:, 0:6, :], self.gT[:, 0:6, :], 32.0, None, ALU.mult, None, ["gT"], ["gT"])
        lamt = self.alloc([4, 64], F32)
        for j, nm in enumerate(["diff_lambda_q1", "diff_lambda_k1", "diff_lambda_q2", "diff_lambda_k2"]):
            self.DMA("sp", lamt[:, j, :], din[nm][0:1, :].to_broadcast([128, 64]), [], ["lamt"], slow=True)
        lam2 = self.alloc([2, 64], F32)
        lsum = self.alloc([2], F32)
        self.TT("dve", lam2[:, 0, :], lamt[:, 0, :], lamt[:, 1, :], ALU.mult, ["lamt"], ["lam2"])
        self.TT("dve", lam2[:, 1, :], lamt[:, 2, :], lamt[:, 3, :], ALU.mult, ["lamt"], ["lam2"])
        S.op("dve", lambda e: e.tensor_reduce(out=lsum, in_=lam2, axis=AX.X, op=ALU.add), reads=["lam2"], writes=["lsum"])
        lexp = self.alloc([2], F32)
        self.ACT(lexp, lsum, AF.Exp, ["lsum"], ["lexp"])
        lambda_init = 0.8 - 0.6 * math.exp(-0.3 * 1)
        self.TT("dve", self.lam[:, 0:1], lexp[:, 1:2], lexp[:, 0:1], ALU.subtract, ["lexp"], ["lam"])
        self.TS("dve", self.lam[:, 0:1], self.lam[:, 0:1], -lambda_init, None, ALU.add, None, ["lam"], ["lam"])
        self.DMA("sp", self.lam[:, 1:2], din["diff_subln"][0].rearrange("(e o) -> e o", o=1), [], ["lam1"], slow=True)
        self.TS("dve", self.lam[:, 1:2], self.lam[:, 1:2], (1.0 - lambda_init) * math.sqrt(128.0), None, ALU.mult, None, ["lam1"], ["lam1"])
        self.lambda_init = lambda_init
        self.base_top = self.top
        stages = self.debug if self.debug else ("l0ab", "l0c", "l1a", "l1b", "l1c")
        if "l0ab" not in stages:
            self.convert_weights(0, chain_depth=4)
        if "l1b" not in stages:
            self.convert_weights(1, chain_depth=4)
        if "l0ab" in stages:
            self.layer0_ab()
        S.barrier(include_bg=True)
        if "l0c" in stages:
            self.top = self.base_top
            self.phase_c(0)
            S.barrier()
        if "l1a" in stages:
            self.top = self.base_top
            self.layer1_a()
            S.barrier()
        if "l1b" in stages:
            self.top = self.base_top
            self.layer1_b()
        S.barrier(include_bg=True)
        if "l1c" in stages:
            self.top = self.base_top
            self.phase_c(1)
        S.emit(final_wait=self.final)
        return self.nc

    def norm_block(self, src_tiles, src_keys, gidx, hnT, hnT_key, tagp, toff=0):
        n = len(src_tiles)
        ss = self.n_ss
        self.MS("dve", ss[:, 0:n], 0.0, [(tagp, "ss")])
        for tt in range(n):
            self.ACT(self.n_junk, src_tiles[tt], AF.Square, [src_keys[tt], (tagp, "ss")], [(tagp, "ss"), "n_junk"], accum=ss[:, tt:tt + 1])
        self.TS("dve", ss[:, 0:n], ss[:, 0:n], D * EPS, None, ALU.add, None, [(tagp, "ss")], [(tagp, "ss")])
        self.rsqrt("dve", self.n_rs[:, 0:n], ss[:, 0:n], n, (tagp, "ss"), (tagp, "rs"), tagp)
        for tt in range(n):
            sl = tt % 2
            xn = self.n_xn[:, sl, :]
            self.ACT(xn, src_tiles[tt], AF.Copy, [src_keys[tt], (tagp, "rs")], [("n_xn", sl)], scale=self.n_rs[:, tt:tt + 1])
            b = self.nb()
            pbb = self.pb[b][:].bitcast(BF16)
            for kc in range(8):
                self.TR(pbb[:, kc * 128:(kc + 1) * 128], xn[:, kc * 128:(kc + 1) * 128], self.identb, [("n_xn", sl), "identb"], [("pb", b)])
            t_ = toff + tt
            self.TT("dve", hnT[:, :, t_ * 128:(t_ + 1) * 128], pbb[:, 0:1024].rearrange("p (a b) -> p a b", a=8),
                    self.gT[:, gidx, :].unsqueeze(2).to_broadcast([128, 8, 128]), ALU.mult, [("pb", b), "gT"], [hnT_key])

    def alloc_norm(self):
        self.n_ss = self.alloc([4], F32)
        self.n_rs = self.alloc([4], F32)
        self.n_junk = self.alloc([1024], BF16)
        self.n_xn = self.alloc([2, 1024], BF16)

    def phase_c(self, l):
        din = self.din
        NB = self.NB
        wo_src = self.wbf[("wo", l)].rearrange("(kc p) f -> p kc f", p=128)
        wg_src = self.wbf[("wg", l)].rearrange("(kc p) f -> p kc f", p=128)
        wu_src = self.wbf[("wu", l)].rearrange("(kc p) f -> p kc f", p=128)
        wd_src = self.wbf[("wd", l)].rearrange("(fc p) f -> p fc f", p=128)
        wpg_src = self.wbf[("wpg", l)].rearrange("(kc p) f -> p kc f", p=128)
        wpp_src = self.wbf[("wpp", l)].rearrange("(kc p) f -> p kc f", p=128)
        hsrc = din["x"] if l == 0 else self.h1
        self.alloc_norm()
        if l == 1:
            self.fing = self.alloc([1024], F32)
            self.DMA("sp", self.fing, din["final_norm"].unsqueeze(0).to_broadcast([128, 1024]), [], ["fing"], slow=True)
            self.TS("dve", self.fing, self.fing, 32.0, None, ALU.mult, None, ["fing"], ["fing"])
        Wo = self.alloc([8, 1024], BF16)
        Wg = [self.alloc([8, 512], BF16) for _ in range(2)]
        Wu = [self.alloc([8, 512], BF16) for _ in range(2)]
        Wd = [self.alloc([NFC, 256], BF16) for _ in range(2)]
        Wpg = self.alloc([8, 1024], BF16)
        Wpp = self.alloc([2, 1024], BF16)
        mixb = [self.alloc([8, 512], BF16)]
        hm2 = [self.alloc([4, 1024], F32) for _ in range(2)]
        hnT = self.alloc([8, 512], BF16)
        actT = self.alloc([NFC, 512], BF16)
        sg = [self.alloc([512], BF16) for _ in range(2)]
        th = [self.alloc([512], F32) for _ in range(2)]
        pbf = [self.alloc([256], BF16) for _ in range(2)]
        pT = [self.alloc([2, 128], BF16) for _ in range(2)]
        tg = "c%d" % l
        groups = [(0, 512), (512, 512), (1024, 512), (1536, 512), (2048, 512), (2560, 256)]
        gi = 0
        for tb in range(NB):
            t0 = tb * 512
            ms = 0
            hsl = tb % 2
            hm = hm2[hsl]
            def load_block(tbx):
                tx = tbx * 512
                sx = tbx % 2
                self.DMA("sp", mixb[ms], self.mixT.rearrange("(kc p) t -> p kc t", p=128)[:, :, tx:tx + 512], [], [("mixb", ms)])
                for tt in range(4):
                    self.DMA("sp", hm2[sx][:, tt, :], hsrc[tx + tt * 128:tx + (tt + 1) * 128, :], [], [("hm", sx, tt)])
            if tb == 0:
                load_block(0)
            self.DMA("pool", Wo, wo_src, [], ["Wo"])
            for tt in range(4):
                for half in range(2):
                    b = self.nb()
                    for kc in range(8):
                        self.MM(self.pb[b][:], mixb[ms][:, kc, tt * 128:(tt + 1) * 128], Wo[:, kc, half * 512:(half + 1) * 512],
                                kc == 0, kc == 7, [("mixb", ms), "Wo"], [("pb", b)])
                    hv = hm[:, tt, half * 512:(half + 1) * 512]
                    self.TT("dve", hv, hv, self.pb[b][:], ALU.add, [("pb", b), ("hm", hsl, tt)], [("hm", hsl, tt)])
            self.norm_block([hm[:, tt, :] for tt in range(4)], [("hm", hsl, tt) for tt in range(4)], 3 * l + 1, hnT, "hnT", tg + "n1")
            for (f0, fw) in groups:
                s = gi % 2
                gi += 1
                self.DMA("pool", Wg[s][:, :, 0:fw], wg_src[:, :, f0:f0 + fw], [], [("Wg", s)])
                self.DMA("pool", Wu[s][:, :, 0:fw], wu_src[:, :, f0:f0 + fw], [], [("Wu", s)])
                for fl in range(fw // 128):
                    fc = f0 // 128 + fl
                    bg = self.nb()
                    for kc in range(8):
                        self.MM(self.pb[bg][:], Wg[s][:, kc, fl * 128:(fl + 1) * 128], hnT[:, kc, :], kc == 0, kc == 7,
                                [("Wg", s), "hnT"], [("pb", bg)])
                    bu = self.nb()
                    for kc in range(8):
                        self.MM(self.pb[bu][:], Wu[s][:, kc, fl * 128:(fl + 1) * 128], hnT[:, kc, :], kc == 0, kc == 7,
                                [("Wu", s), "hnT"], [("pb", bu)])
                    ss_ = fc % 2
                    self.ACT(sg[ss_], self.pb[bg][:], AF.Silu, [("pb", bg)], [("sg", ss_)])
                    self.TT("dve", actT[:, fc, :], sg[ss_], self.pb[bu][:], ALU.mult, [("sg", ss_), ("pb", bu)], [("actT", fc)])
            if tb + 1 < NB:
                load_block(tb + 1)
            for q in range(4):
                self.DMA("pool", Wd[q % 2], wd_src[:, :, q * 256:(q + 1) * 256], [], [("Wd", q % 2)])
                for tt in range(4):
                    b = self.nb()
                    for fc in range(NFC):
                        self.MM(self.pb[b][:, 0:256], actT[:, fc, tt * 128:(tt + 1) * 128], Wd[q % 2][:, fc, :], fc == 0, fc == NFC - 1,
                                [("actT", fc), ("Wd", q % 2)], [("pb", b)])
                    hv = hm[:, tt, q * 256:(q + 1) * 256]
                    self.TT("dve", hv, hv, self.pb[b][:, 0:256], ALU.add, [("pb", b), ("hm", hsl, tt)], [("hm", hsl, tt)])
            self.DMA("pool", Wpg, wpg_src, [], ["Wpg"])
            self.DMA("pool", Wpp, wpp_src, [], ["Wpp"])
            self.norm_block([hm[:, tt, :] for tt in range(4)], [("hm", hsl, tt) for tt in range(4)], 3 * l + 2, hnT, "hnT", tg + "n2")
            for tt in range(4):
                s = tt % 2
                self.DMA("pool", pbf[s], din["p"][l, t0 + tt * 128:t0 + (tt + 1) * 128, :], [], [("pbf", s)])
                b = self.nb()
                pbb = self.pb[b][:].bitcast(BF16)
                for pc in range(2):
                    self.TR(pbb[:, pc * 128:(pc + 1) * 128], pbf[s][:, pc * 128:(pc + 1) * 128], self.identb, [("pbf", s), "identb"], [("pb", b)])
                self.ACT(pT[s], pbb[:, 0:256].rearrange("p (a b) -> p a b", a=2), AF.Copy, [("pb", b)], [("pT", s)])
                for half in range(2):
                    ba = self.nb()
                    for pc in range(2):
                        self.MM(self.pb[ba][:], pT[s][:, pc, :], Wpp[:, pc, half * 512:(half + 1) * 512], pc == 0, pc == 1,
                                [("pT", s), "Wpp"], [("pb", ba)])
                    bz = self.nb()
                    for kc in range(8):
                        self.MM(self.pb[bz][:], hnT[:, kc, tt * 128:(tt + 1) * 128], Wpg[:, kc, half * 512:(half + 1) * 512], kc == 0, kc == 7,
                                ["hnT", "Wpg"], [("pb", bz)])
                    self.ACT(th[half], self.pb[bz][:], AF.Tanh, [("pb", bz)], [("th", half)], scale=0.5)
                    self.STT("dve", th[half], th[half], 1.0, self.pb[ba][:], ALU.add, ALU.mult, [("th", half), ("pb", ba)], [("th", half)])
                    hv = hm[:, tt, half * 512:(half + 1) * 512]
                    self.STT("dve", hv, th[half], 0.5, hv, ALU.mult, ALU.add, [("th", half), ("hm", hsl, tt)], [("hm", hsl, tt)])
            if l == 0:
                for tt in range(4):
                    self.DMA("sp", self.h1[t0 + tt * 128:t0 + (tt + 1) * 128, :], hm[:, tt, :], [("hm", hsl, tt)], [], semkey=("hm", hsl, tt))
            else:
                ss = self.n_ss
                self.MS("dve", ss[:, 0:4], 0.0, [(tg, "fss")])
                for tt in range(4):
                    self.ACT(self.n_junk, hm[:, tt, :], AF.Square, [("hm", hsl, tt), (tg, "fss")], [(tg, "fss"), "n_junk"], accum=ss[:, tt:tt + 1])
                self.TS("dve", ss[:, 0:4], ss[:, 0:4], D * EPS, None, ALU.add, None, [(tg, "fss")], [(tg, "fss")])
                self.rsqrt("dve", self.n_rs[:, 0:4], ss[:, 0:4], 4, (tg, "fss"), (tg, "frs"), tg + "f")
                for tt in range(4):
                    self.STT("dve", hm[:, tt, :], hm[:, tt, :], self.n_rs[:, tt:tt + 1], self.fing, ALU.mult, ALU.mult,
                             [("hm", hsl, tt), (tg, "frs"), "fing"], [("hm", hsl, tt)])
                    i = self.DMA("sp", self.y[t0 + tt * 128:t0 + (tt + 1) * 128, :], hm[:, tt, :], [("hm", hsl, tt)], [], semkey=("hm", hsl, tt))
                    self.final.append(i)

    def layer1_a(self):
        din = self.din
        NB = self.NB
        self.alloc_norm()
        wsrc = din["diff_w_qkv"][0].rearrange("(kc p) f -> p kc f", p=128)
        W = self.alloc([8, 3072], BF16)
        Wsw = self.alloc([8, 2048], BF16)
        for kc in range(8):
            self.DMA("pool", W[:, kc, :], wsrc[:, kc, :], [], [("W1", kc)])
        for kc in range(8):
            wv = W[:, kc, 0:2048].rearrange("p (m h j) -> p m h j", m=32, h=2)
            sv = Wsw[:, kc, :].rearrange("p (m h j) -> p m h j", m=32, h=2)
            self.CP("dve" if kc % 2 == 0 else "pool", sv[:, :, 0, :], wv[:, :, 1, :], [("W1", kc)], [("Wsw", kc)])
            self.CP("dve" if kc % 2 == 0 else "pool", sv[:, :, 1, :], wv[:, :, 0, :], [("W1", kc)], [("Wsw", kc)])
        Wk = [("W1", kc) for kc in range(8)]
        Wswk = [("Wsw", kc) for kc in range(8)]
        hx = self.alloc([4, 1024], F32)
        hnT = self.alloc([8, 512], BF16)
        rope = [self.alloc([2, 512], F32) for _ in range(2)]
        ta = [self.alloc([512], F32) for _ in range(2)]
        tb_ = [self.alloc([512], F32) for _ in range(2)]
        qk = [self.alloc([512], BF16) for _ in range(2)]
        vb = [self.alloc([512], BF16) for _ in range(2)]
        cnt = 0
        for tb in range(NB):
            t0 = tb * 512
            rs = tb % 2
            for tt in range(4):
                self.DMA("sp", hx[:, tt, :], self.h1[t0 + tt * 128:t0 + (tt + 1) * 128, :], [], [("hx", tt)])
            self.DMA("sp", rope[rs], din["c_rope"].rearrange("c p t -> p c t")[:, :, t0:t0 + 512], [], [("rope", rs)])
            self.norm_block([hx[:, tt, :] for tt in range(4)], [("hx", tt) for tt in range(4)], 3, hnT, "hnT", "l1an")
            for ft in range(16):
                s = cnt % 2
                cnt += 1
                ba = self.nb()
                for kc in range(8):
                    self.MM(self.pb[ba][:], W[:, kc, ft * 128:(ft + 1) * 128], hnT[:, kc, :], kc == 0, kc == 7, ["hnT", Wk[kc]], [("pb", ba)])
                bs = self.nb()
                for kc in range(8):
                    self.MM(self.pb[bs][:], Wsw[:, kc, ft * 128:(ft + 1) * 128], hnT[:, kc, :], kc == 0, kc == 7, ["hnT", Wswk[kc]], [("pb", bs)])
                self.TT("dve", ta[s], self.pb[ba][:], rope[rs][:, 0, :], ALU.mult, [("pb", ba), ("rope", rs)], [("ta", s)])
                self.TT("dve", tb_[s], self.pb[bs][:], rope[rs][:, 1, :], ALU.mult, [("pb", bs), ("rope", rs)], [("tb", s)])
                self.TT("pool", qk[s], ta[s], tb_[s], ALU.add, [("ta", s), ("tb", s)], [("qk", s)])
                self.DMA("sp", self.qkT[ft * 128:(ft + 1) * 128, t0:t0 + 512], qk[s], [("qk", s)], [], semkey=("qk", s))
            for tt in range(4):
                for half in range(2):
                    s = cnt % 2
                    cnt += 1
                    b = self.nb()
                    for kc in range(8):
                        self.MM(self.pb[b][:], hnT[:, kc, tt * 128:(tt + 1) * 128], W[:, kc, 2048 + half * 512:2048 + (half + 1) * 512],
                                kc == 0, kc == 7, ["hnT", Wk[kc]], [("pb", b)])
                    self.ACT(vb[s], self.pb[b][:], AF.Copy, [("pb", b)], [("vb", s)])
                    self.DMA("sp", self.vS[t0 + tt * 128:t0 + (tt + 1) * 128, half * 512:(half + 1) * 512], vb[s], [("vb", s)], [], semkey=("vb", s))

    def layer1_b(self):
        din = self.din
        S = self.S
        T = self.T
        if os.environ.get('NOCONV') != '1':
            self.convert_weights(1, chain_depth=4)
        NQB = T // 512
        NT = T // 128
        maskb = self.alloc([128], BF16)
        self.DMA("sp", maskb, din["c_maskb"], [], ["maskb"])
        QT = [self.alloc([T], BF16) for _ in range(2)]
        KT = [self.alloc([T], BF16) for _ in range(2)]
        V = [self.alloc([NT, 130], BF16) for _ in range(2)]
        for s in range(2):
            self.MS("dve", V[s][:, :, 128:130], 1.0, [("Vone", s)])
        PT = [self.alloc([512], BF16) for _ in range(8)]
        O1 = self.alloc([4, 128], F32)
        att4 = self.alloc([4, 128], F32)
        attb = self.alloc([4, 128], BF16)
        junk = self.alloc([128], BF16)
        rsm = self.alloc([8], F32)
        ssq = self.alloc([4], F32)
        rsq = self.alloc([4], F32)
        aT = [self.alloc([512], BF16) for _ in range(2)]
        st = {"pti": 0, "blk": 0}

        def load_head(h):
            hs = h % 2
            self.DMA("sp", QT[hs], self.qkT[h * 128:(h + 1) * 128, :], [], [("QT", hs)])
            self.DMA("sp", KT[hs], self.qkT[1024 + h * 128:1024 + (h + 1) * 128, :], [], [("KT", hs)])
            self.DMA("sp", V[hs][:, :, 0:128], self.vS[:, h * 128:(h + 1) * 128].rearrange("(n p) e -> p n e", p=128),
                     [("Vone", hs)], [("V", hs)])

        pairs = [(h, qb, kt) for h in range(8) for qb in range(NQB) for kt in range(4 * qb + 4)]
        info = {}
        obs = {}
        ACC = [(0, 0), (0, 136), (0, 272), (1, 0), (1, 136), (1, 272), (2, 0), (2, 136)]

        def stage_a(pr):
            h, qb, kt = pr
            hs = h % 2
            if h == 0 and qb == 0 and kt == 0:
                load_head(0)
            q0 = qb * 512
            dk = kt - 4 * qb
            qlo = max(0, dk) * 128
            bss = [self.nb(), self.nb()]
            for rep in range(int(os.environ.get('DUP', '1'))):
              for m in range(2):
                rows = slice(m * 64, (m + 1) * 64)
                self.MM(self.pb[bss[m]][:, qlo:512], KT[hs][rows, kt * 128:(kt + 1) * 128], QT[hs][rows, q0 + qlo:q0 + 512], True, dk < 0,
                        [("KT", hs), ("QT", hs)], [("pb", bss[m])])
            pts = []
            for m in range(2):
                ps = self.pb[bss[m]]
                if dk >= 0:
                    self.MM(ps[:, qlo:qlo + 128], self.identb, maskb, False, True, ["identb", "maskb"], [("pb", bss[m])])
                pi = st["pti"] % len(PT)
                st["pti"] += 1
                pt = PT[pi]
                ptk = ("PT", pi)
                self.ACT(pt[:, qlo:512], ps[:, qlo:512], AF.Exp, [("pb", bss[m])], [ptk], scale=0.125)
                pts.append((pt, ptk))
            info[pr] = (pts, dk)

        def stage_c(pr):
            h, qb, kt = pr
            hs = h % 2
            q0 = qb * 512
            pts, dk = info.pop(pr)
            if qb == 0 and kt == 0 and h + 1 < 8:
                load_head(h + 1)
            if kt == 0:
                obs[(h, qb)] = [self.nb(reserve=True), self.nb(reserve=True), self.nb(reserve=True)]
            ob = obs[(h, qb)]

            def acc(m, qt):
                bi, col = ACC[m * 4 + qt]
                return self.pb[ob[bi]][:, col:col + 129], ("pb", ob[bi]), (kt == 0 and col == 0 and (m * 4 + qt) in (0, 3, 6))

            for m in range(2):
                pt, ptk = pts[m]
                for qt in range(max(0, dk), 4):
                    o, okey, first = acc(m, qt)
                    last = (kt == 4 * qb + qt) and (m * 4 + qt) in (2, 3, 7)
                    self.MM(o, pt[:, qt * 128:(qt + 1) * 128], V[hs][:, kt, 0:129], first, last, [ptk, ("V", hs)], [okey])
            if kt != 4 * qb + 3:
                return
            for m in range(2):
                for qt in range(4):
                    o, okey, _ = acc(m, qt)
                    rc = rsm[:, m * 4 + qt:m * 4 + qt + 1]
                    rk = ("rsm", m * 4 + qt)
                    S.op("dve", lambda e, rc=rc, o=o: e.reciprocal(out=rc, in_=o[:, 128:129]), reads=[okey], writes=[rk])
                    if m == 0:
                        self.TS("dve", O1[:, qt, :], o[:, 0:128], rc, None, ALU.mult, None, [okey, rk], [("O1", qt)])
                    else:
                        self.TT("dve", rc, rc, self.lam[:, 0:1], ALU.mult, [rk, "lam"], [rk])
                        self.STT("dve", att4[:, qt, :], o[:, 0:128], rc, O1[:, qt, :], ALU.mult, ALU.add,
                                 [okey, rk, ("O1", qt)], [("att", qt)])
                        S.op("dve", lambda e, qt=qt: e.scalar_tensor_tensor(out=junk, in0=att4[:, qt, :], scalar=1.0, in1=att4[:, qt, :],
                                                                           op0=ALU.mult, op1=ALU.mult, accum_out=ssq[:, qt:qt + 1]),
                             reads=[("att", qt)], writes=["ssq", "junkb"])
            for b_ in ob:
                self.reserved.discard(b_)
            del obs[(h, qb)]
            self.TS("dve", ssq, ssq, 128.0 * EPS, None, ALU.add, None, ["ssq"], ["ssq"])
            self.rsqrt("dve", rsq, ssq, 4, "ssq", "rsq", "l1b")
            for qt in range(4):
                self.TS("dve", attb[:, qt, :], att4[:, qt, :], rsq[:, qt:qt + 1], None, ALU.mult, None, [("att", qt), "rsq"], [("attb", qt)])

            def part2(h=h, q0=q0):
                bt = self.nb()
                pbb = self.pb[bt][:].bitcast(BF16)
                for qt in range(4):
                    self.TR(pbb[:, qt * 128:(qt + 1) * 128], attb[:, qt, :], self.identb, [("attb", qt), "identb"], [("pb", bt)])
                a = aT[st["blk"] % 2]
                ak = ("aT", st["blk"] % 2)
                st["blk"] += 1
                self.ACT(a, pbb[:, 0:512], AF.Copy, [("pb", bt), "lam1"], [ak], scale=self.lam[:, 1:2])
                self.DMA("sp", self.mixT[h * 128:(h + 1) * 128, q0:q0 + 512], a, [ak], [], semkey=ak)
            deferred.append([DEFER, part2])

        deferred = []
        DEFER = int(os.environ.get('DEFER', '2'))
        LOOK = int(os.environ.get('LOOK', '2'))
        n = len(pairs)
        for i in range(n + LOOK):
            if i < n:
                stage_a(pairs[i])
            if i - LOOK >= 0:
                stage_c(pairs[i - LOOK])
            for dfr in list(deferred):
                dfr[0] -= 1
                if dfr[0] <= 0:
                    dfr[1]()
                    deferred.remove(dfr)
        for dfr in deferred:
            dfr[1]()

    def layer0_ab(self):
        din = self.din
        S = self.S
        NB = self.NB
        self.alloc_norm()
        sgn = self.sgn
        wsrc = din["ret_s5_w_in"][0].rearrange("(kc p) f -> p kc f", p=128)
        W = self.alloc([8, 2560], BF16)
        Wsw = self.alloc([8, 1024], BF16)
        for kc in range(8):
            self.DMA("pool", W[:, kc, :], wsrc[:, kc, :], [], [("W0", kc)])
        for kc in range(8):
            wv = W[:, kc, 0:1024].rearrange("p (m h j) -> p m h j", m=16, h=2)
            sv = Wsw[:, kc, :].rearrange("p (m h j) -> p m h j", m=16, h=2)
            self.CP("dve", sv[:, :, 0, :], wv[:, :, 1, :], [("W0", kc)], [("W0sw", kc)])
            self.CP("dve", sv[:, :, 1, :], wv[:, :, 0, :], [("W0", kc)], [("W0sw", kc)])
        Wk = [("W0", kc) for kc in range(8)]
        Wswk = [("W0sw", kc) for kc in range(8)]
        Wglu = self.alloc([4, 512], BF16)
        self.DMA("pool", Wglu, din["s5_w_glu"][0].rearrange("(j p) f -> p j f", p=128), [], ["Wglu"])
        retmask = self.alloc([1024], F32)
        kdec = self.alloc([512], F32)
        g128 = self.alloc([512], F32)
        reps = self.alloc([8], F32)
        self.DMA("sp", retmask, din["c_retmask"], [], ["retmask"])
        self.DMA("sp", kdec, din["c_kdec"], [], ["kdec"])
        self.DMA("sp", g128, din["c_g128"], [], ["g128"])
        self.DMA("sp", reps, din["c_reps"], [], ["reps"])
        import os
        STOP = int(os.environ.get('L0STOP', '99'))
        RSTOP = int(os.environ.get('RSTOP', '99'))
        if STOP <= 1:
            return
        mark = self.top
        Bexp = None
        Bexp = self.alloc([32, 128], BF16)
        Bsw = self.alloc([32, 128], BF16)
        C1 = self.alloc([32, 16], BF16)
        C2 = self.alloc([32, 16], BF16)
        COS = self.alloc([32, 128], BF16)
        SINS = self.alloc([32, 128], BF16)
        rr = self.alloc([32], F32)
        Arot = self.alloc([32], F32)
        Brot = self.alloc([32], F32)
        Ddiag = self.alloc([4, 128], BF16)
        perm = self.alloc([128], F32)
        self.DMA("sp", perm, din["c_perm"], [], ["perm"])
        tmark = self.top
        LR = self.alloc([32], F32)
        LI = self.alloc([32], F32)
        DL = self.alloc([32], F32)
        for half in range(2):
            self.DMA("sp", LR[half * 64:(half + 1) * 64, :], din["s5_lambda_re"][0].rearrange("g p -> p g"), [], ["LR"], slow=True)
            self.DMA("sp", LI[half * 64:(half + 1) * 64, :], din["s5_lambda_im"][0].rearrange("g p -> p g"), [], ["LI"], slow=True)
        self.DMA("sp", DL, din["s5_log_step"][0:1, :].to_broadcast([128, 32]), [], ["DL"], slow=True)
        self.ACT(DL, DL, AF.Exp, ["DL"], ["DL"])
        th_ = self.alloc([32], F32)
        aa = self.alloc([32], F32)
        self.TT("dve", aa, LR, DL, ALU.mult, ["LR", "DL"], ["aa"])
        self.TT("dve", th_, LI, DL, ALU.mult, ["LI", "DL"], ["th"])
        self.ACT(rr, aa, AF.Exp, ["aa"], ["rr"])
        iota = self.alloc([128], F32)
        self.DMA("sp", iota, din["c_iota"], [], ["iota"])
        ang = self.alloc([32, 128], F32)
        ang2 = self.alloc([32, 128], F32)
        self.TT("dve", ang, th_.unsqueeze(2).to_broadcast([128, 32, 128]), iota.unsqueeze(1).to_broadcast([128, 32, 128]), ALU.mult,
                ["th", "iota"], ["ang"])
        angi = self.alloc([32, 128], I32)

        def sin_of(out, angle, n3, shift, scale, rk, wk):
            a2 = ang2 if n3 else ang2[:, 0, 0:32]
            ai = angi if n3 else angi[:, 0, 0:32]
            self.TS("dve", a2, angle, shift, 1.0 / (2 * PI), ALU.add, ALU.mult, rk + [wk], ["ang2"])
            self.CP("dve", ai, a2, ["ang2"], ["angi"])
            self.CP("dve", a2, ai, ["angi"], ["ang2"])
            self.STT("dve", a2, a2, -2 * PI, angle, ALU.mult, ALU.add, ["ang2"] + rk, ["ang2"])
            self.TS("dve", a2, a2, shift, None, ALU.add, None, ["ang2"], ["ang2"])
            self.TS("dve", a2, a2, -PI, PI, ALU.max, ALU.min, ["ang2"], ["ang2"])
            self.ACT(out, a2, AF.Sin, ["ang2", "sgn"], [wk], scale=scale)

        sin_of(COS, ang, True, PI / 2, 1.0, ["ang"], "COS")
        sin_of(SINS, ang, True, 0.0, sgn[:, 2:3], ["ang"], "SINS")
        a128 = self.alloc([32], F32)
        self.TS("dve", a128, th_, 128.0, None, ALU.mult, None, ["th"], ["a128"])
        sin_of(Arot, a128, False, PI / 2, 1.0, ["a128"], "Arot")
        sin_of(Brot, a128, False, 0.0, sgn[:, 0:1], ["a128"], "Brot")
        c1 = self.alloc([32], F32)
        s1 = self.alloc([32], F32)
        sin_of(c1, th_, False, PI / 2, 1.0, ["th"], "c1")
        sin_of(s1, th_, False, 0.0, 1.0, ["th"], "s1")
        if STOP <= 2:
            return
        nre = self.alloc([32], F32)
        nim = self.alloc([32], F32)
        den = self.alloc([32], F32)
        fre = self.alloc([32], F32)
        fim = self.alloc([32], F32)
        u1 = self.alloc([32], F32)
        self.TT("dve", nre, rr, c1, ALU.mult, ["rr", "c1"], ["nre"])
        self.TS("dve", nre, nre, -1.0, None, ALU.add, None, ["nre"], ["nre"])
        self.TT("dve", nim, rr, s1, ALU.mult, ["rr", "s1"], ["nim"])
        self.TT("dve", den, LR, LR, ALU.mult, ["LR"], ["den"])
        self.TT("dve", u1, LI, LI, ALU.mult, ["LI"], ["u1"])
        self.TT("dve", den, den, u1, ALU.add, ["den", "u1"], ["den"])
        S.op("dve", lambda e: e.reciprocal(out=den, in_=den), reads=["den"], writes=["den"])
        self.TT("dve", fre, nre, LR, ALU.mult, ["nre", "LR"], ["fre"])
        self.TT("dve", u1, nim, LI, ALU.mult, ["nim", "LI", "den"], ["u1"])
        self.TT("dve", fre, fre, u1, ALU.add, ["fre", "u1"], ["fre"])
        self.TT("dve", fre, fre, den, ALU.mult, ["fre", "den"], ["fre"])
        self.TT("dve", fim, nim, LR, ALU.mult, ["nim", "LR"], ["fim"])
        self.TT("dve", u1, nre, LI, ALU.mult, ["nre", "LI", "fre"], ["u1"])
        self.TT("dve", fim, fim, u1, ALU.subtract, ["fim", "u1"], ["fim"])
        self.TT("dve", fim, fim, den, ALU.mult, ["fim", "den"], ["fim"])
        Bre = self.alloc([32, 16], F32)
        Bim = self.alloc([32, 16], F32)
        self.DMA("sp", Bre[0:64], din["s5_b_re"][0].rearrange("g p c -> p g c"), [], ["Bre"], slow=True)
        self.DMA("sp", Bim[0:64], din["s5_b_im"][0].rearrange("g p c -> p g c"), [], ["Bim"], slow=True)
        Bbr = self.alloc([32, 16], F32)
        Bbi = self.alloc([32, 16], F32)
        v1 = self.alloc([32, 16], F32)
        frb = fre[0:64].unsqueeze(2).to_broadcast([64, 32, 16])
        fib = fim[0:64].unsqueeze(2).to_broadcast([64, 32, 16])
        self.TT("dve", Bbr[0:64], Bre[0:64], frb, ALU.mult, ["Bre", "fre"], ["Bbr"])
        self.TT("dve", v1[0:64], Bim[0:64], fib, ALU.mult, ["Bim", "fim"], ["v1"])
        self.TT("dve", Bbr[0:64], Bbr[0:64], v1[0:64], ALU.subtract, ["Bbr", "v1"], ["Bbr"])
        self.TT("dve", Bbi[0:64], Bim[0:64], frb, ALU.mult, ["Bim", "fre"], ["Bbi"])
        self.TT("dve", v1[0:64], Bre[0:64], fib, ALU.mult, ["Bre", "fim", "Bbr"], ["v1"])
        self.TT("dve", Bbi[0:64], Bbi[0:64], v1[0:64], ALU.add, ["Bbi", "v1"], ["Bbi"])
        if STOP <= 3:
            return
        rowmask = self.alloc([8], F32)
        self.DMA("sp", rowmask, din["c_rowmask"], [], ["rowmask"])
        Tre = self.alloc([64], F32)
        Tim = self.alloc([64], F32)
        rmb = rowmask.unsqueeze(2).to_broadcast([128, 8, 64])
        for j in range(4):
            b = self.nb()
            self.TR(self.pb[b][:, 0:64], Bbr[0:64, j * 8:(j + 1) * 8, :].rearrange("p g c -> p (g c)"), self.identf[0:64, 0:64], ["Bbr", "identf"], [("pb", b)])
            self.TR(self.pb[b][:, 64:128], Bbi[0:64, j * 8:(j + 1) * 8, :].rearrange("p g c -> p (g c)"), self.identf[0:64, 0:64], ["Bbi", "identf"], [("pb", b)])
            self.CP("dve", Tre, self.pb[b][:, 0:64], [("pb", b)], ["Tre"])
            self.CP("dve", Tim, self.pb[b][:, 64:128], [("pb", b)], ["Tim"])
            treb = Tre.unsqueeze(1).to_broadcast([128, 8, 64])
            timb = Tim.unsqueeze(1).to_broadcast([128, 8, 64])
            self.TT("dve", Bexp[:, j * 8:(j + 1) * 8, 0:64], treb, rmb, ALU.mult, ["Tre", "rowmask"], ["Bexp"])
            self.TT("dve", Bexp[:, j * 8:(j + 1) * 8, 64:128], timb, rmb, ALU.mult, ["Tim", "rowmask"], ["Bexp"])
            self.TT("dve", Bsw[:, j * 8:(j + 1) * 8, 0:64], timb, rmb, ALU.mult, ["Tim", "rowmask"], ["Bsw"])
            self.TT("dve", Bsw[:, j * 8:(j + 1) * 8, 64:128], treb, rmb, ALU.mult, ["Tre", "rowmask"], ["Bsw"])
        if STOP <= 4:
            return
        CC = self.alloc([4, 128], F32)
        CC2 = self.alloc([4, 128], F32)
        cre = din["s5_c_re"][0].rearrange("(j g) c p -> (g c) j p", j=4)
        cim = din["s5_c_im"][0].rearrange("(j g) c p -> (g c) j p", j=4)
        self.DMA("sp", CC[:, :, 0:64], cre, [], ["CC"], slow=True)
        self.DMA("sp", CC[:, :, 64:128], cim, [], ["CC"], slow=True)
        self.DMA("sp", CC2[:, :, 0:64], cim, [], ["CC2"], slow=True)
        self.DMA("sp", CC2[:, :, 64:128], cre, [], ["CC2"], slow=True)
        for j in range(4):
            b = self.nb()
            self.TR(self.pb[b][:, 0:128], CC[:, j, :], self.identf, ["CC", "identf"], [("pb", b)])
            self.TR(self.pb[b][:, 128:256], CC2[:, j, :], self.identf, ["CC2", "identf"], [("pb", b)])
            self.TS("dve", C1[:, j * 8:(j + 1) * 8, :], self.pb[b][:, 0:128].rearrange("p (g c) -> p g c", g=8), sgn[:, 4:5], None, ALU.mult, None,
                    [("pb", b), "sgn"], ["C1"])
            self.TS("dve", C2[:, j * 8:(j + 1) * 8, :], self.pb[b][:, 128:256].rearrange("p (g c) -> p g c", g=8), sgn[:, 5:6], None, ALU.mult, None,
                    [("pb", b), "sgn"], ["C2"])
        Dcol = self.alloc([4], F32)
        self.DMA("sp", Dcol, din["s5_d"][0].rearrange("(j g) c -> (g c) j", j=4), [], ["Dcol"], slow=True)
        self.TS("dve", Dcol, Dcol, 0.5, None, ALU.mult, None, ["Dcol"], ["Dcol"])
        for j in range(4):
            self.TS("dve", Ddiag[:, j, :], self.identf, Dcol[:, j:j + 1], None, ALU.mult, None, ["identf", "Dcol"], ["Ddiag"])
        if STOP <= 5:
            return
        S5K = ["Bexp", "Bsw", "C1", "C2", "COS", "SINS", "rr", "Arot", "Brot", "Ddiag", "perm"]
        self.S.barrier()
        self.top = tmark

        hx = self.alloc([2, 1024], F32)
        hnT = self.alloc([8, 512], BF16)
        rope = [self.alloc([2, 512], F32)]
        ta = [self.alloc([512], F32)]
        tbb = [self.alloc([512], F32)]
        qT = self.alloc([4, 512], BF16)
        kT = self.alloc([4, 512], BF16)
        vtm = self.alloc([4, 512], BF16)
        sgt = self.alloc([4, 512], BF16)
        uT = self.alloc([4, 512], BF16)
        mixb = self.alloc([8, 512], BF16)
        PTm = self.alloc([1024], BF16)
        Kd = self.alloc([512], BF16)
        R = self.alloc([512], F32)
        Rbf = self.alloc([512], BF16)
        self.MS("dve", R, 0.0, ["R"])
        self.MS("dve", Rbf, 0.0, ["Rbf"])
        sum1 = self.alloc([8], F32)
        var1 = self.alloc([8], F32)
        rs8 = self.alloc([8], F32)
        xc = self.alloc([8, 64], F32)
        sq = self.alloc([8, 64], F32)
        rtm = self.alloc([512], BF16)
        RV = [self.alloc([8, 128], F32)]
        TMP = [self.alloc([8, 128], BF16)]
        Wst = [self.alloc([8, 128], F32)]
        P1 = [self.alloc([8, 128], BF16)]
        P2 = [self.alloc([8, 128], BF16)]
        w127 = self.alloc([32], F32)
        init = self.alloc([32], F32)
        ctmp = self.alloc([32], F32)
        self.MS("dve", init, 0.0, ["init"])
        gs = ta[0]
        gi1 = ta[0]
        gth = tbb[0]
        Gb = self.alloc([512], BF16)
        GT = self.alloc([4, 128], BF16)
        th2 = self.alloc([4, 128], F32)
        K2 = 2.0 * math.sqrt(2.0 / PI)
        for tb in range(NB):
            t0 = tb * 512
            rs_ = 0
            self.DMA("sp", rope[rs_], din["c_rope"].rearrange("c p t -> p c t")[:, :, t0:t0 + 512], [], [("rope", rs_)])
            for hf in range(2):
                for tt in range(2):
                    tg_ = hf * 2 + tt
                    self.DMA("sp", hx[:, tt, :], din["x"][t0 + tg_ * 128:t0 + (tg_ + 1) * 128, :], [], [("hx", tt)])
                self.norm_block([hx[:, tt, :] for tt in range(2)], [("hx", tt) for tt in range(2)], 0, hnT, "hnT", "l0n", toff=hf * 2)
            for ft in range(8):
                s = 0
                ba = self.nb()
                for kc in range(8):
                    self.MM(self.pb[ba][:], W[:, kc, ft * 128:(ft + 1) * 128], hnT[:, kc, :], kc == 0, kc == 7, ["hnT", Wk[kc]], [("pb", ba)])
                bs = self.nb()
                for kc in range(8):
                    self.MM(self.pb[bs][:], Wsw[:, kc, ft * 128:(ft + 1) * 128], hnT[:, kc, :], kc == 0, kc == 7, ["hnT", Wswk[kc]], [("pb", bs)])
                self.TT("dve", ta[s], self.pb[ba][:], rope[rs_][:, 0, :], ALU.mult, [("pb", ba), ("rope", rs_)], [("ta", s)])
                self.TT("dve", tbb[s], self.pb[bs][:], rope[rs_][:, 1, :], ALU.mult, [("pb", bs), ("rope", rs_)], [("tb", s)])
                dst = qT[:, ft, :] if ft < 4 else kT[:, ft - 4, :]
                dk_ = ("qT", ft) if ft < 4 else ("kT", ft - 4)
                self.TT("pool", dst, ta[s], tbb[s], ALU.add, [("ta", s), ("tb", s)], [dk_])
            for c in range(4):
                b = self.nb()
                for kc in range(8):
                    self.MM(self.pb[b][:], hnT[:, kc, c * 128:(c + 1) * 128], W[:, kc, 1024:1536], kc == 0, kc == 7, ["hnT", Wk[kc]], [("pb", b)])
                self.ACT(vtm[:, c, :], self.pb[b][:], AF.Copy, [("pb", b)], [("vtm", c)])
                b = self.nb()
                for kc in range(8):
                    self.MM(self.pb[b][:], hnT[:, kc, c * 128:(c + 1) * 128], W[:, kc, 1536:2048], kc == 0, kc == 7, ["hnT", Wk[kc]], [("pb", b)])
                self.ACT(sgt[:, c, :], self.pb[b][:], AF.Silu, [("pb", b)], [("sgt", c)])
            for j in range(4):
                b = self.nb()
                for kc in range(8):
                    self.MM(self.pb[b][:], W[:, kc, 2048 + j * 128:2048 + (j + 1) * 128], hnT[:, kc, :], kc == 0, kc == 7, ["hnT", Wk[kc]], [("pb", b)])
                self.ACT(uT[:, j, :], self.pb[b][:], AF.Copy, [("pb", b)], [("uT", j)])
            def ret_thread():
                for c in range(4):
                    cs = slice(c * 128, (c + 1) * 128)
                    self.convert_weights(0, n=2)
                    bs0, bs1 = self.nb(True), self.nb(True)
                    for h in range(8):
                        pr, hl = h // 2, h % 2
                        rows = slice(hl * 64, (hl + 1) * 64)
                        bb = bs0 if hl == 0 else bs1
                        self.MM(self.pb[bb][:, pr * 128:(pr + 1) * 128], kT[rows, pr, cs], qT[rows, pr, cs], True, True,
                                [("kT", pr), ("qT", pr)], [("pb", bb)])
                    yield
                    pv = PTm.rearrange("p (r l t) -> p r l t", r=4, l=2)
                    mv = retmask.rearrange("p (r l t) -> p r l t", r=4, l=2)
                    self.TT("dve", pv[:, :, 0, :], self.pb[bs0][:].rearrange("p (r t) -> p r t", r=4), mv[:, :, 0, :], ALU.mult,
                            [("pb", bs0), "retmask"], [("PTm", 0)])
                    self.TT("dve", pv[:, :, 1, :], self.pb[bs1][:].rearrange("p (r t) -> p r t", r=4), mv[:, :, 1, :], ALU.mult,
                            [("pb", bs1), "retmask"], [("PTm", 1)])
                    for b_ in (bs0, bs1):
                        self.reserved.discard(b_)
                    bk = self.nb(True)
                    pkb = self.pb[bk][:].bitcast(BF16)
                    for pr in range(4):
                        self.TR(pkb[:, pr * 128:(pr + 1) * 128], kT[:, pr, cs], self.identb, [("kT", pr), "identb"], [("pb", bk)])
                    yield
                    self.TT("dve", Kd, pkb[:, 0:512], kdec, ALU.mult, [("pb", bk), "kdec"], ["Kd"])
                    self.reserved.discard(bk)
                    bo = self.nb(True)
                    for h in range(8):
                        pr, hl = h // 2, h % 2
                        rows = slice(hl * 64, (hl + 1) * 64)
                        self.MM(self.pb[bo][:, h * 64:(h + 1) * 64], PTm[:, h * 128:(h + 1) * 128], vtm[:, c, h * 64:(h + 1) * 64], True, False,
                                [("PTm", 0), ("PTm", 1), ("vtm", c)], [("pb", bo)])
                        self.MM(self.pb[bo][:, h * 64:(h + 1) * 64], qT[rows, pr, cs], Rbf[rows, pr * 128 + hl * 64:pr * 128 + (hl + 1) * 64], False, True,
                                [("qT", pr), "Rbf"], [("pb", bo)])
                    bkv = self.nb(True)
                    for pr in range(4):
                        self.MM(self.pb[bkv][:, pr * 128:(pr + 1) * 128], Kd[:, pr * 128:(pr + 1) * 128], vtm[:, c, pr * 128:(pr + 1) * 128], True, True,
                                ["Kd", ("vtm", c)], [("pb", bkv)])
                    yield
                    self.TT("dve", R, R, g128, ALU.mult, ["R", "g128"], ["R"])
                    self.TT("dve", R, R, self.pb[bkv][:], ALU.add, ["R", ("pb", bkv)], ["R"])
                    self.reserved.discard(bkv)
                    self.ACT(Rbf, R, AF.Copy, ["R"], ["Rbf"])
                    po = self.pb[bo][:].rearrange("p (h e) -> p h e", h=8)
                    S.op("dve", lambda e, po=po: e.tensor_reduce(out=sum1, in_=po, axis=AX.X, op=ALU.add), reads=[("pb", bo)], writes=["sum1"])
                    self.TS("dve", sum1, sum1, 1.0 / 64.0, None, ALU.mult, None, ["sum1"], ["sum1"])
                    self.TT("dve", xc, po, sum1.unsqueeze(2).to_broadcast([128, 8, 64]), ALU.subtract, [("pb", bo), "sum1"], ["xc"])
                    self.reserved.discard(bo)
                    self.ACT(sq, xc, AF.Square, ["xc"], ["sq"])
                    yield
                    S.op("dve", lambda e: e.tensor_reduce(out=var1, in_=sq, axis=AX.X, op=ALU.add), reads=["sq"], writes=["var1"])
                    self.TT("dve", var1, var1, reps, ALU.add, ["var1", "reps"], ["var1"])
                    self.rsqrt("dve", rs8, var1, 8, "var1", "rs8", "l0ln")
                    self.STT("dve", xc, xc, 8.0, rs8.unsqueeze(2).to_broadcast([128, 8, 64]), ALU.mult, ALU.mult, ["xc", "rs8"], ["xc"])
                    self.TT("pool", rtm, xc.rearrange("p h e -> p (h e)"), sgt[:, c, :], ALU.mult, ["xc", ("sgt", c)], ["rtm"])
                    yield
                    bt = self.nb()
                    ptb = self.pb[bt][:].bitcast(BF16)
                    for j in range(4):
                        self.TR(ptb[:, j * 128:(j + 1) * 128], rtm[:, j * 128:(j + 1) * 128], self.identb, ["rtm", "identb"], [("pb", bt)])
                    self.ACT(mixb[:, 0:4, cs], ptb[:, 0:512].rearrange("p (j t) -> p j t", j=4), AF.Copy, [("pb", bt)], [("mixb", c)])
                    yield

            def s5_thread():
                for c in range(4):
                    cs = slice(c * 128, (c + 1) * 128)
                    by = None
                    pend = {}

                    def s1(ht):
                        j, hf = ht // 2, ht % 2
                        g0 = j * 8 + hf * 4
                        bu, bw = self.nb(True), self.nb(True)
                        for gl in range(4):
                            g = g0 + gl
                            self.MM(self.pb[bu][:, gl * 128:(gl + 1) * 128], Bexp[:, g, :], uT[:, j, cs], True, True, ["Bexp", ("uT", j)], [("pb", bu)])
                            self.MM(self.pb[bw][:, gl * 128:(gl + 1) * 128], Bsw[:, g, :], uT[:, j, cs], True, True, ["Bsw", ("uT", j)], [("pb", bw)])
                        pend[ht] = (bu, bw)

                    s1(0)
                    for ht in range(8):
                        j, hf = ht // 2, ht % 2
                        g0 = j * 8 + hf * 4
                        hs_ = slice(hf * 4, hf * 4 + 4)
                        if ht + 1 < 8:
                            s1(ht + 1)
                        bu, bw = pend.pop(ht)
                        yield
                        self.TT("dve", RV[0][:, hs_, :], self.pb[bu][:].rearrange("p (g t) -> p g t", g=4), COS[:, g0:g0 + 4, :], ALU.mult,
                                [("pb", bu), "COS"], [("RV", hf)])
                        self.TT("dve", TMP[0][:, hs_, :], self.pb[bw][:].rearrange("p (g t) -> p g t", g=4), SINS[:, g0:g0 + 4, :], ALU.mult,
                                [("pb", bw), "SINS"], [("TMP", hf)])
                        self.reserved.discard(bu)
                        self.reserved.discard(bw)
                        self.TT("dve", RV[0][:, hs_, :], RV[0][:, hs_, :], TMP[0][:, hs_, :], ALU.add, [("RV", hf), ("TMP", hf)], [("RV", hf)])
                        yield
                        for gl in range(4):
                            g = g0 + gl
                            S.op("dve", lambda e, gl=gl, g=g, hf=hf: e.tensor_tensor_scan(out=Wst[0][:, hf * 4 + gl, :], data0=rr[:, g:g + 1].to_broadcast([128, 128]),
                                                                                         data1=RV[0][:, hf * 4 + gl, :], initial=init[:, g:g + 1],
                                                                                         op0=ALU.mult, op1=ALU.add),
                                 reads=[("RV", hf), "rr", "init"], writes=[("Wst", hf)])
                        self.CP("dve", w127[:, g0:g0 + 4], Wst[0][:, hs_, 127], [("Wst", hf)], [("w127", ht)])
                        self.TT(os.environ.get('P1ENG', 'dve'), P1[0][:, hs_, :], Wst[0][:, hs_, :], COS[:, g0:g0 + 4, :], ALU.mult, [("Wst", hf), "COS"], [("P1", hf)])
                        self.TT("pool", P2[0][:, hs_, :], Wst[0][:, hs_, :], SINS[:, g0:g0 + 4, :], ALU.mult, [("Wst", hf), "SINS"], [("P2", hf)])
                        yield
                        if by is None:
                            by = self.nb(True)
                        if hf == 0:
                            self.MM(self.pb[by][:, j * 128:(j + 1) * 128], uT[:, j, cs], Ddiag[:, j, :], True, False, [("uT", j), "Ddiag"], [("pb", by)])
                        for gl in range(4):
                            g = g0 + gl
                            self.MM(self.pb[by][:, g * 16:(g + 1) * 16], P1[0][:, hf * 4 + gl, :], C1[:, g, :], False, False, [("P1", hf), "C1"], [("pb", by)])
                            self.MM(self.pb[by][:, g * 16:(g + 1) * 16], P2[0][:, hf * 4 + gl, :], C2[:, g, :], False, hf == 1 and gl == 3, [("P2", hf), "C2"], [("pb", by)])
                    bc = self.nb(True)
                    wk = [("w127", ht) for ht in range(8)]
                    self.MM(self.pb[bc][:, 0:32], perm, w127, True, True, ["perm"] + wk, [("pb", bc)])
                    yield
                    self.TT("dve", ctmp, self.pb[bc][:, 0:32], Brot, ALU.mult, [("pb", bc), "Brot"], ["ctmp"])
                    self.reserved.discard(bc)
                    self.TT("dve", init, w127, Arot, ALU.mult, wk + ["Arot"], ["init"])
                    self.TT("dve", init, init, ctmp, ALU.add, ["init", "ctmp"], ["init"])
                    yh = self.pb[by][:]
                    yk = ("pb", by)
                    self.ACT(gs, yh, AF.Square, [yk], [("ta", 0)], scale=math.sqrt(0.044715 * 4.0))
                    self.STT("dve", gi1, gs, 1.0, yh, ALU.add, ALU.mult, [("ta", 0), yk], [("ta", 0)])
                    self.ACT(gth, gi1, AF.Tanh, [("ta", 0)], [("tb", 0)], scale=K2)
                    yield
                    self.STT("dve", Gb, gth, 1.0, yh, ALU.add, ALU.mult, [("tb", 0), yk], ["Gb"])
                    self.reserved.discard(by)
                    bt = self.nb(True)
                    ptb = self.pb[bt][:].bitcast(BF16)
                    for j in range(4):
                        self.TR(ptb[:, j * 128:(j + 1) * 128], Gb[:, j * 128:(j + 1) * 128], self.identb, ["Gb", "identb"], [("pb", bt)])
                    yield
                    self.ACT(GT, ptb[:, 0:512].rearrange("p (j t) -> p j t", j=4), AF.Copy, [("pb", bt)], ["GT"], scale=0.5)
                    self.reserved.discard(bt)
                    bz = self.nb(True)
                    for j2 in range(4):
                        for j in range(4):
                            self.MM(self.pb[bz][:, j2 * 128:(j2 + 1) * 128], Wglu[:, j, j2 * 128:(j2 + 1) * 128], GT[:, j, :], j == 0, j == 3,
                                    ["Wglu", "GT"], [("pb", bz)])
                    yield
                    self.ACT(th2, self.pb[bz][:].rearrange("p (j t) -> p j t", j=4), AF.Tanh, [("pb", bz)], ["th2"])
                    self.reserved.discard(bz)
                    self.STT("dve", mixb[:, 4:8, cs], th2, 1.0, GT, ALU.add, ALU.mult, ["th2", "GT"], [("mixb2", c)])
                    yield

            threads = [ret_thread(), s5_thread()]
            weights = [int(os.environ.get('WR', '1')), int(os.environ.get('WS', '4'))]
            if os.environ.get('ONLY') == 'ret':
                threads, weights = [ret_thread()], [1]
            if os.environ.get('ONLY') == 's5':
                threads, weights = [s5_thread()], [1]
            if os.environ.get('SEQ') == '1':
                threads, weights = [ret_thread(), s5_thread()], [1000, 1000]
            while threads:
                for ti in range(len(threads) - 1, -1, -1):
                    for _ in range(weights[ti]):
                        try:
                            next(threads[ti])
                        except StopIteration:
                            threads.pop(ti)
                            weights.pop(ti)
                            break
            st_i = self.DMA("sp", self.mixT.rearrange("(kc p) t -> p kc t", p=128)[:, :, t0:t0 + 512], mixb,
                     [("mixb", c) for c in range(4)] + [("mixb2", c) for c in range(4)], [], semkey="mixb_out")
            if tb == NB - 1:
                self.convert_weights(0, chain_depth=4)
            for c in range(4):
                self.S.readers.setdefault(("mixb", c), []).append(st_i)
                self.S.readers.setdefault(("mixb2", c), []).append(st_i)


_CACHE = {}


def _get_nc(T, debug=()):
    key = (T, tuple(debug))
    if key not in _CACHE:
        _CACHE[key] = Builder(T, debug).build()
    return _CACHE[key]


def kernel(**inputs):
    x = np.asarray(inputs["x"], dtype=np.float32)
    p = np.asarray(inputs["p"], dtype=np.float32)
    B, T, _ = x.shape
    nc = _get_nc(T)
    hc = host_consts(T)
    in_maps = []
    for b in range(B):
        m = {"x": np.ascontiguousarray(x[b]), "p": np.ascontiguousarray(p[:, b])}
        for n in WEIGHT_NAMES:
            m[n] = np.ascontiguousarray(np.asarray(inputs[n], dtype=np.float32))
        m.update(hc)
        in_maps.append(m)
    res = run_bass_kernel_spmd(nc, in_maps, core_ids=list(range(B)))
    return np.stack([np.asarray(r["y"], dtype=np.float32) for r in res.results], axis=0)
```
